# Optimizing a Trainium2 kernel written in Bass

```python
import math, functools
import jax, jax.numpy as jnp
from jax import lax
import numpy as np

D_MODEL = 1024
BATCH = 8
SEQ = 2048
DEPTH = 4
DEC_BATCH = 128
DEC_SEQ = 8
PAST_LEN = 16384
PAGE_SIZE = 128

F32 = jnp.float32
N_MIXERS = 4
N_META = 16
CHUNK = 128
EXPAND = 2
D_INNER = EXPAND * D_MODEL
NORM_EPS = 1e-6
L_LRU = (DEPTH + 3) // 4
L_S5 = (DEPTH + 2) // 4
L_RWKV = (DEPTH + 1) // 4
L_RET = DEPTH // 4
LRU_CONV_W = 4
LRU_BLOCKS = 16
LRU_BLOCK = D_INNER // LRU_BLOCKS
LRU_C = 8.0
S5_GROUP = 16
S5_GROUPS = D_INNER // S5_GROUP
S5_STATE = 64
RWKV_HEAD = 64
RWKV_HEADS = D_INNER // RWKV_HEAD
RWKV_DECAY_LORA = 64
RWKV_A_LORA = 64
RWKV_LN_EPS = 64e-5
RET_HEADS = 4
RET_DK = D_MODEL // RET_HEADS
RET_DV = D_INNER // RET_HEADS
ROPE_BASE = 10000.0

kernel_name = 'hybrid_lru_s5_rwkv7_retention_step'


def rms_norm(x, g):
    xf = x.astype(F32)
    y = xf * lax.rsqrt(jnp.mean(xf * xf, axis=-1, keepdims=True) + NORM_EPS)
    return (y * g.astype(F32)).astype(x.dtype)


def rope(x, pos):
    half = x.shape[-1] // 2
    inv = ROPE_BASE ** (-jnp.arange(half, dtype=F32) / half)
    ang = pos.astype(F32)[:, None] * inv[None, :]
    cos = jnp.cos(ang)[None, :, None, :]
    sin = jnp.sin(ang)[None, :, None, :]
    x1, x2 = x[..., :half], x[..., half:]
    return jnp.concatenate([x1 * cos - x2 * sin, x2 * cos + x1 * sin], axis=-1)


def _lin_comb(l, r):
    return (l[0] * r[0], r[0] * l[1] + r[1])


def _cmul(ar, ai, br, bi):
    return (ar * br - ai * bi, ar * bi + ai * br)


def _clin_comb(l, r):
    a_r, a_i = _cmul(l[0], l[1], r[0], r[1])
    t_r, t_i = _cmul(r[0], r[1], l[2], l[3])
    return (a_r, a_i, t_r + r[2], t_i + r[3])


def lru_chunk(p, carry, xn, pos0):
    conv_buf, h0 = carry
    b, l, _ = xn.shape
    uz = xn @ p['w_in']
    u, gate = uz[..., :D_INNER], uz[..., D_INNER:]
    ext = jnp.concatenate([conv_buf.astype(u.dtype), u], axis=1)
    cw = p['conv_w']
    xc = p['conv_b'] + sum(ext[:, j:j + l] * cw[j] for j in range(LRU_CONV_W))
    xb = xc.reshape(b, l, LRU_BLOCKS, LRU_BLOCK)
    gate_r = jax.nn.sigmoid((jnp.einsum('blhi,hij->blhj', xb, p['wa']) + p['ba']).astype(F32)).reshape(b, l, D_INNER)
    gate_i = jax.nn.sigmoid((jnp.einsum('blhi,hij->blhj', xb, p['wx']) + p['bx']).astype(F32)).reshape(b, l, D_INNER)
    log_a = -LRU_C * gate_r * jax.nn.softplus(-p['lam'].astype(F32))
    a = jnp.exp(log_a)
    bx = jnp.sqrt(-jnp.expm1(2.0 * log_a)) * gate_i * xc.astype(F32)
    bx = bx.at[:, 0].add(a[:, 0] * h0.astype(F32))
    _, h = lax.associative_scan(_lin_comb, (a, bx), axis=1)
    y = h.astype(xn.dtype) * jax.nn.silu(gate)
    return (ext[:, l:].astype(conv_buf.dtype), h[:, -1].astype(h0.dtype)), y @ p['w_out']


def s5_chunk(p, carry, xn, pos0):
    s_re, s_im = carry
    b, l, _ = xn.shape
    uz = xn @ p['w_in']
    u, gate = uz[..., :D_INNER], uz[..., D_INNER:]
    uf = u.astype(F32)
    dt = jnp.exp(p['log_dt'].astype(F32))[:, None]
    a_re, a_im = p['a_re'].astype(F32), p['a_im'].astype(F32)
    mag = jnp.exp(dt * a_re)
    ang = dt * a_im
    ab_re, ab_im = mag * jnp.cos(ang), mag * jnp.sin(ang)
    den = a_re * a_re + a_im * a_im
    f_re = ((ab_re - 1.0) * a_re + ab_im * a_im) / den
    f_im = (ab_im * a_re - (ab_re - 1.0) * a_im) / den
    b_re, b_im = p['b_re'].astype(F32), p['b_im'].astype(F32)
    bb_re = f_re[..., None] * b_re - f_im[..., None] * b_im
    bb_im = f_re[..., None] * b_im + f_im[..., None] * b_re
    ug = uf.reshape(b, l, S5_GROUPS, S5_GROUP)
    bu_re = jnp.einsum('blgc,gnc->blgn', ug, bb_re)
    bu_im = jnp.einsum('blgc,gnc->blgn', ug, bb_im)
    sr, si = s_re.astype(F32), s_im.astype(F32)
    bu_re = bu_re.at[:, 0].add(ab_re * sr - ab_im * si)
    bu_im = bu_im.at[:, 0].add(ab_re * si + ab_im * sr)
    shp = bu_re.shape
    _, _, x_re, x_im = lax.associative_scan(
        _clin_comb, (jnp.broadcast_to(ab_re, shp), jnp.broadcast_to(ab_im, shp), bu_re, bu_im), axis=1)
    c_re, c_im = p['c_re'].astype(F32), p['c_im'].astype(F32)
    y = jnp.einsum('blgn,gcn->blgc', x_re, c_re) - jnp.einsum('blgn,gcn->blgc', x_im, c_im)
    y = y.reshape(b, l, D_INNER) + p['d'].astype(F32) * uf
    y = jax.nn.gelu(y).astype(xn.dtype)
    y = y * jax.nn.sigmoid(y @ p['glu_w'] + p['glu_b'])
    y = y * jax.nn.silu(gate)
    return (x_re[:, -1].astype(s_re.dtype), x_im[:, -1].astype(s_im.dtype)), y @ p['w_out']


def rwkv_chunk(p, carry, xn, pos0):
    x_prev, s0 = carry
    b, l, _ = xn.shape
    shifted = jnp.concatenate([x_prev[:, None].astype(xn.dtype), xn[:, :-1]], axis=1)
    xx = shifted - xn
    mu = p['mu']
    xr, xw, xk, xv, xa, xg = (xn + xx * mu[n] for n in range(6))
    r = xr @ p['w_r']
    k = xk @ p['w_k']
    v = xv @ p['w_v']
    g = jax.nn.silu(xg @ p['w_g'])
    w_raw = (p['w0'] + jnp.tanh(xw @ p['w1']) @ p['w2']).astype(F32)
    decay = jnp.exp(-jnp.exp(-jax.nn.softplus(-w_raw) - 0.5))
    a = jax.nn.sigmoid((p['a0'] + (xa @ p['a1']) @ p['a2']).astype(F32))
    hd = lambda t: t.astype(F32).reshape(b, l, RWKV_HEADS, RWKV_HEAD)
    per_head = lambda t: t.astype(F32).reshape(RWKV_HEADS, RWKV_HEAD)
    r, k, v, a, decay = hd(r), hd(k), hd(v), hd(a), hd(decay)
    kk = k * per_head(p['k_k'])
    kk = kk * lax.rsqrt(jnp.maximum(jnp.sum(kk * kk, axis=-1, keepdims=True), 1e-24))
    k = k * (1.0 + (a - 1.0) * per_head(p['k_a']))
    tm = lambda t: jnp.swapaxes(t, 0, 1)

    def step(s, inp):
        r_t, w_t, k_t, v_t, a_t, b_t = inp
        sa = jnp.einsum('bhvk,bhk->bhv', s, a_t)
        s = s * w_t[:, :, None, :] + sa[..., None] * b_t[:, :, None, :] + v_t[..., None] * k_t[:, :, None, :]
        return s, jnp.einsum('bhvk,bhk->bhv', s, r_t)

    s_new, y = lax.scan(step, s0.astype(F32), (tm(r), tm(decay), tm(k), tm(v), tm(-kk), tm(kk * a)))
    y = tm(y)
    mean = jnp.mean(y, axis=-1, keepdims=True)
    var = jnp.mean(jnp.square(y - mean), axis=-1, keepdims=True)
    yn = (y - mean) * lax.rsqrt(var + RWKV_LN_EPS) * per_head(p['ln_w']) + per_head(p['ln_b'])
    bonus = jnp.sum(r * k * per_head(p['r_k']), axis=-1, keepdims=True) * v
    out = (yn + bonus).reshape(b, l, D_INNER).astype(xn.dtype) * g
    return (xn[:, -1].astype(x_prev.dtype), s_new.astype(s0.dtype)), out @ p['w_o']


def ret_chunk(p, carry, xn, pos0):
    (s0,) = carry
    b, l, _ = xn.shape
    pos = pos0 + jnp.arange(l, dtype=jnp.int32)
    q = rope((xn @ p['w_q']).astype(F32).reshape(b, l, RET_HEADS, RET_DK), pos)
    k = rope((xn @ p['w_k']).astype(F32).reshape(b, l, RET_HEADS, RET_DK), pos) * (RET_DK ** -0.5)
    v = (xn @ p['w_v']).astype(F32).reshape(b, l, RET_HEADS, RET_DV)
    g = jax.nn.silu(xn @ p['w_g'])
    log_g = jnp.log1p(-jnp.exp2(-5.0 - jnp.arange(RET_HEADS, dtype=F32)))
    n = jnp.arange(l, dtype=F32)
    diff = n[:, None] - n[None, :]
    mask = jnp.where(diff[None] >= 0, jnp.exp(diff[None] * log_g[:, None, None]), 0.0)
    scores = jnp.einsum('blhd,bmhd->bhlm', q, k) * mask
    s0f = s0.astype(F32)
    y = jnp.einsum('bhlm,bmhe->blhe', scores, v)
    y = y + jnp.einsum('blhd,bhde->blhe', q, s0f) * jnp.exp((n[:, None] + 1.0) * log_g)[None, :, :, None]
    kw = k * jnp.exp((l - 1.0 - n)[:, None] * log_g)[None, :, :, None]
    s_new = jnp.exp(l * log_g)[None, :, None, None] * s0f + jnp.einsum('bmhd,bmhe->bhde', kw, v)
    y = y * lax.rsqrt(jnp.mean(y * y, axis=-1, keepdims=True) + NORM_EPS)
    out = y.reshape(b, l, D_INNER).astype(xn.dtype) * g
    return (s_new.astype(s0.dtype),), out @ p['w_o']


def run_prompt(chunk_fn, carry, xn):
    b, t, d = xn.shape
    carry, y_meta = chunk_fn(carry, xn[:, :N_META], 0)
    n_chunks = (t - N_META) // CHUNK
    xc = jnp.swapaxes(xn[:, N_META:].reshape(b, n_chunks, CHUNK, d), 0, 1)
    starts = N_META + CHUNK * jnp.arange(n_chunks, dtype=jnp.int32)

    def body(c, inp):
        start, xk = inp
        return chunk_fn(c, xk, start)

    carry, yc = lax.scan(body, carry, (starts, xc))
    y = jnp.swapaxes(yc, 0, 1).reshape(b, n_chunks * CHUNK, yc.shape[-1])
    return carry, jnp.concatenate([y_meta.astype(y.dtype), y], axis=1)


def prompt_init_carry(m, b):
    z = lambda *s: jnp.zeros(s, F32)
    if m == 0:
        return (z(b, LRU_CONV_W - 1, D_INNER), z(b, D_INNER))
    if m == 1:
        return (z(b, S5_GROUPS, S5_STATE), z(b, S5_GROUPS, S5_STATE))
    if m == 2:
        return (z(b, D_MODEL), z(b, RWKV_HEADS, RWKV_HEAD, RWKV_HEAD))
    return (z(b, RET_HEADS, RET_DK, RET_DV),)


def setup_inputs(seed: int = 0) -> dict:
    key = jax.random.key(seed)
    keys = iter(jax.random.split(key, 96))

    def nrm(shape, scale):
        return scale * jax.random.normal(next(keys), shape, F32)

    def uni(shape, lo, hi):
        return jax.random.uniform(next(keys), shape, F32, lo, hi)

    D, E = D_MODEL, D_INNER
    sd, se = D ** -0.5, E ** -0.5
    u_lam = uni((L_LRU, E), 0.9, 0.999)
    s_lam = u_lam ** (1.0 / LRU_C)
    w0_base = jnp.tile(jnp.linspace(-6.0, -1.0, RWKV_HEAD, dtype=F32), RWKV_HEADS)
    return {
        'x_prompt': nrm((BATCH, SEQ, D), 1.0),
        'x_sample': nrm((DEC_BATCH, DEC_SEQ, D), 1.0),
        'state_lru_conv': nrm((L_LRU, DEC_BATCH, LRU_CONV_W - 1, E), 1.0),
        'state_lru_h': nrm((L_LRU, DEC_BATCH, E), 0.5),
        'state_s5_re': nrm((L_S5, DEC_BATCH, S5_GROUPS, S5_STATE), 0.1),
        'state_s5_im': nrm((L_S5, DEC_BATCH, S5_GROUPS, S5_STATE), 0.1),
        'state_rwkv_shift': nrm((L_RWKV, DEC_BATCH, D), 1.0),
        'state_rwkv_wkv': nrm((L_RWKV, DEC_BATCH, RWKV_HEADS, RWKV_HEAD, RWKV_HEAD), 0.1),
        'state_ret': nrm((L_RET, DEC_BATCH, RET_HEADS, RET_DK, RET_DV), 0.1),
        'meta_tokens': nrm((N_META, D), 1.0),
        'norm_pre': 1.0 + nrm((DEPTH, D), 0.02),
        'norm_post': 1.0 + nrm((DEPTH, D), 0.02),
        'lru_w_in': nrm((L_LRU, D, 2 * E), sd),
        'lru_conv_w': nrm((L_LRU, LRU_CONV_W, E), 0.5),
        'lru_conv_b': nrm((L_LRU, E), 0.01),
        'lru_wa': nrm((L_LRU, LRU_BLOCKS, LRU_BLOCK, LRU_BLOCK), LRU_BLOCK ** -0.5),
        'lru_ba': nrm((L_LRU, LRU_BLOCKS, LRU_BLOCK), 0.01),
        'lru_wx': nrm((L_LRU, LRU_BLOCKS, LRU_BLOCK, LRU_BLOCK), LRU_BLOCK ** -0.5),
        'lru_bx': nrm((L_LRU, LRU_BLOCKS, LRU_BLOCK), 0.01),
        'lru_lam': jnp.log(s_lam) - jnp.log1p(-s_lam),
        'lru_w_out': nrm((L_LRU, E, D), se),
        's5_w_in': nrm((L_S5, D, 2 * E), sd),
        's5_log_dt': uni((L_S5, S5_GROUPS), math.log(1e-3), math.log(1e-1)),
        's5_a_re': -0.5 + nrm((L_S5, S5_GROUPS, S5_STATE), 0.01),
        's5_a_im': jnp.pi * jnp.arange(S5_STATE, dtype=F32) + nrm((L_S5, S5_GROUPS, S5_STATE), 0.01),
        's5_b_re': nrm((L_S5, S5_GROUPS, S5_STATE, S5_GROUP), (2 * S5_GROUP) ** -0.5),
        's5_b_im': nrm((L_S5, S5_GROUPS, S5_STATE, S5_GROUP), (2 * S5_GROUP) ** -0.5),
        's5_c_re': nrm((L_S5, S5_GROUPS, S5_GROUP, S5_STATE), (2 * S5_STATE) ** -0.5),
        's5_c_im': nrm((L_S5, S5_GROUPS, S5_GROUP, S5_STATE), (2 * S5_STATE) ** -0.5),
        's5_d': nrm((L_S5, E), 1.0),
        's5_glu_w': nrm((L_S5, E, E), se),
        's5_glu_b': nrm((L_S5, E), 0.01),
        's5_w_out': nrm((L_S5, E, D), se),
        'rwkv_mu': uni((L_RWKV, 6, D), 0.0, 1.0),
        'rwkv_w_r': nrm((L_RWKV, D, E), sd),
        'rwkv_w_k': nrm((L_RWKV, D, E), sd),
        'rwkv_w_v': nrm((L_RWKV, D, E), sd),
        'rwkv_w_g': nrm((L_RWKV, D, E), sd),
        'rwkv_w0': w0_base + nrm((L_RWKV, E), 0.1),
        'rwkv_w1': nrm((L_RWKV, D, RWKV_DECAY_LORA), sd),
        'rwkv_w2': nrm((L_RWKV, RWKV_DECAY_LORA, E), 0.1 * RWKV_DECAY_LORA ** -0.5),
        'rwkv_a0': nrm((L_RWKV, E), 0.1),
        'rwkv_a1': nrm((L_RWKV, D, RWKV_A_LORA), sd),
        'rwkv_a2': nrm((L_RWKV, RWKV_A_LORA, E), 0.1 * RWKV_A_LORA ** -0.5),
        'rwkv_k_k': 0.85 + nrm((L_RWKV, E), 0.02),
        'rwkv_k_a': 1.0 + nrm((L_RWKV, E), 0.02),
        'rwkv_r_k': nrm((L_RWKV, E), 0.1),
        'rwkv_ln_w': 1.0 + nrm((L_RWKV, E), 0.02),
        'rwkv_ln_b': nrm((L_RWKV, E), 0.01),
        'rwkv_w_o': nrm((L_RWKV, E, D), se),
        'ret_w_q': nrm((L_RET, D, RET_HEADS * RET_DK), sd),
        'ret_w_k': nrm((L_RET, D, RET_HEADS * RET_DK), sd),
        'ret_w_v': nrm((L_RET, D, E), sd),
        'ret_w_g': nrm((L_RET, D, E), sd),
        'ret_w_o': nrm((L_RET, E, D), se),
    }


def reference(x_prompt, x_sample, state_lru_conv, state_lru_h, state_s5_re, state_s5_im,
              state_rwkv_shift, state_rwkv_wkv, state_ret, meta_tokens, norm_pre, norm_post,
              lru_w_in, lru_conv_w, lru_conv_b, lru_wa, lru_ba, lru_wx, lru_bx, lru_lam, lru_w_out,
              s5_w_in, s5_log_dt, s5_a_re, s5_a_im, s5_b_re, s5_b_im, s5_c_re, s5_c_im, s5_d,
              s5_glu_w, s5_glu_b, s5_w_out,
              rwkv_mu, rwkv_w_r, rwkv_w_k, rwkv_w_v, rwkv_w_g, rwkv_w0, rwkv_w1, rwkv_w2,
              rwkv_a0, rwkv_a1, rwkv_a2, rwkv_k_k, rwkv_k_a, rwkv_r_k, rwkv_ln_w, rwkv_ln_b, rwkv_w_o,
              ret_w_q, ret_w_k, ret_w_v, ret_w_g, ret_w_o):
    params = (
        dict(w_in=lru_w_in, conv_w=lru_conv_w, conv_b=lru_conv_b, wa=lru_wa, ba=lru_ba,
             wx=lru_wx, bx=lru_bx, lam=lru_lam, w_out=lru_w_out),
        dict(w_in=s5_w_in, log_dt=s5_log_dt, a_re=s5_a_re, a_im=s5_a_im, b_re=s5_b_re, b_im=s5_b_im,
             c_re=s5_c_re, c_im=s5_c_im, d=s5_d, glu_w=s5_glu_w, glu_b=s5_glu_b, w_out=s5_w_out),
        dict(mu=rwkv_mu, w_r=rwkv_w_r, w_k=rwkv_w_k, w_v=rwkv_w_v, w_g=rwkv_w_g, w0=rwkv_w0,
             w1=rwkv_w1, w2=rwkv_w2, a0=rwkv_a0, a1=rwkv_a1, a2=rwkv_a2, k_k=rwkv_k_k,
             k_a=rwkv_k_a, r_k=rwkv_r_k, ln_w=rwkv_ln_w, ln_b=rwkv_ln_b, w_o=rwkv_w_o),
        dict(w_q=ret_w_q, w_k=ret_w_k, w_v=ret_w_v, w_g=ret_w_g, w_o=ret_w_o),
    )
    chunk_fns = (lru_chunk, s5_chunk, rwkv_chunk, ret_chunk)
    sample_states = ((state_lru_conv, state_lru_h), (state_s5_re, state_s5_im),
                     (state_rwkv_shift, state_rwkv_wkv), (state_ret,))
    b = x_prompt.shape[0]
    meta = jnp.broadcast_to(meta_tokens[None].astype(x_prompt.dtype), (b, N_META, D_MODEL))
    hp = jnp.concatenate([meta, x_prompt], axis=1)
    hs = x_sample
    new_p = [[] for _ in range(N_MIXERS)]
    new_s = [[] for _ in range(N_MIXERS)]
    for i in range(DEPTH):
        m, j = i % N_MIXERS, i // N_MIXERS
        p = {name: w[j] for name, w in params[m].items()}
        fn = functools.partial(chunk_fns[m], p)
        carry_p, z_p = run_prompt(fn, prompt_init_carry(m, b), rms_norm(hp, norm_pre[i]))
        carry_s, z_s = fn(tuple(s[j] for s in sample_states[m]), rms_norm(hs, norm_pre[i]), PAST_LEN)
        hp = hp + rms_norm(z_p, norm_post[i])
        hs = hs + rms_norm(z_s, norm_post[i])
        new_p[m].append(carry_p)
        new_s[m].append(carry_s)

    def stk(lst, idx):
        return jnp.stack([c[idx] for c in lst])

    y_prompt = hp[:, N_META:]
    y_sample = hs
    return (y_prompt, y_sample,
            stk(new_p[0], 0), stk(new_p[0], 1), stk(new_p[1], 0), stk(new_p[1], 1),
            stk(new_p[2], 0), stk(new_p[2], 1), stk(new_p[3], 0),
            stk(new_s[0], 0), stk(new_s[0], 1), stk(new_s[1], 0), stk(new_s[1], 1),
            stk(new_s[2], 0), stk(new_s[2], 1), stk(new_s[3], 0))
```

```python
import contextlib
import numpy as np
import concourse.bass as bass
import concourse.mybir as mybir
from concourse.bass_utils import run_bass_kernel_spmd

F32 = mybir.dt.float32
BF16 = mybir.dt.bfloat16
ALU = mybir.AluOpType
AF = mybir.ActivationFunctionType
AX = mybir.AxisListType

NCORES = 8
D = 1024
E = 2048
NMETA = 16
SEQ = 2048
PT = NMETA + SEQ
NS = 16
TS = 8
WS = NS * TS
WALL = PT + WS
WMAX = 768
SEGS = [dict(p0=0, Wp=768, Ws=0), dict(p0=768, Wp=768, Ws=0), dict(p0=1536, Wp=528, Ws=WS)]
PI = float(np.pi)


def ctiles(W):
    res = []
    c = 0
    while c < W:
        n = min(512, W - c)
        res.append((c, n))
        c += n
    return res


class KB:
    EPOCH = 30000
    NEAR = 4

    def __init__(self, nc, es):
        self.nc, self.es = nc, es
        self.engs = dict(pe=nc.tensor, act=nc.scalar, dve=nc.vector, pool=nc.gpsimd, sp=nc.sync)
        self.semobj = {}
        self.cnt = {e: 0 for e in self.engs}
        self.epoch = {e: 0 for e in self.engs}
        self.nsem = 0
        for e in self.engs:
            self.semobj[(e, 0)] = self._newsem()
        self.seen = {e: {} for e in self.engs}
        self.ndma = 24
        self.dma_tgt = [0] * self.ndma
        self.dma_ep = [0] * self.ndma
        for i in range(self.ndma):
            self.semobj[('dma', i, 0)] = self._newsem()
        self.dma_slots = {'pool': (0, 4), 'sp': (4, self.ndma)}
        self.dma_rrq = {'pool': 0, 'sp': 4}
        self.lastw = {}
        self.readers = {}
        self.ninstr = 0

    def _newsem(self):
        self.nsem += 1
        return self.es.enter_context(self.nc.semaphore(f"sem{self.nsem}"))

    def _wait(self, e, sk, c, raw=False):
        if sk[0] == e:
            if not (raw and e != 'pe' and sk[1] == self.epoch[e] and self.cnt[e] - c < self.NEAR):
                return
        if self.seen[e].get(sk, 0) >= c:
            return
        self.engs[e].wait_ge(self.semobj[sk], c)
        self.seen[e][sk] = c

    @staticmethod
    def _key(r):
        sub = None
        if isinstance(r, tuple):
            r, sub = r
        if not isinstance(r, str):
            n = r.name
            r = n() if callable(n) else n
        return r, sub

    def _collect(self, reads, writes):
        toks = []
        for r in reads:
            n, s = self._key(r)
            for s2, t in self.lastw.get(n, {}).items():
                if s is None or s2 is None or s == s2:
                    toks.append((t, True))
        for w in writes:
            n, s = self._key(w)
            for s2, t in self.lastw.get(n, {}).items():
                if s is None or s2 is None or s == s2:
                    toks.append((t, False))
            for s2, d in self.readers.get(n, {}).items():
                if s is None or s2 is None or s == s2:
                    toks.extend((t, False) for t in d.values())
        return toks

    def _record(self, reads, writes, tok, who):
        for r in reads:
            n, s = self._key(r)
            self.readers.setdefault(n, {}).setdefault(s, {})[who] = tok
        for w in writes:
            n, s = self._key(w)
            lw = self.lastw.setdefault(n, {})
            rd = self.readers.setdefault(n, {})
            if s is None:
                lw.clear()
                rd.clear()
            else:
                rd.pop(s, None)
            lw[s] = tok

    def op(self, e, fn, r=(), w=()):
        for ((sk, c), raw) in self._collect(r, w):
            self._wait(e, sk, c, raw)
        ins = fn(self.engs[e])
        if self.cnt[e] >= self.EPOCH:
            self.epoch[e] += 1
            self.cnt[e] = 0
            self.semobj[(e, self.epoch[e])] = self._newsem()
        sk = (e, self.epoch[e])
        self.cnt[e] += 1
        ins.then_inc(self.semobj[sk], 1)
        tok = (sk, self.cnt[e])
        self._record(r, w, tok, e)
        self.ninstr += 1
        return tok

    def dma(self, q, out, in_, r=None, w=None, **kw):
        r = [in_] if r is None else r
        w = [out] if w is None else w
        for ((sk, c), raw) in self._collect(r, w):
            self._wait(q, sk, c)
        lo, hi = self.dma_slots[q]
        i = self.dma_rrq[q]
        self.dma_rrq[q] = lo + (i + 1 - lo) % (hi - lo)
        sk = ('dma', i, self.dma_ep[i])
        if self.dma_tgt[i] > 0:
            self._wait(q, sk, self.dma_tgt[i])
        if self.dma_tgt[i] >= self.EPOCH:
            self.dma_ep[i] += 1
            self.dma_tgt[i] = 0
            sk = ('dma', i, self.dma_ep[i])
            self.semobj[sk] = self._newsem()
        ins = self.engs[q].dma_start(out=out, in_=in_, **kw)
        self.dma_tgt[i] += 16
        ins.then_inc(self.semobj[sk], 16)
        tok = (sk, self.dma_tgt[i])
        self._record(r, w, tok, ('dma', i))
        self.ninstr += 1
        return tok

    def barrier(self):
        for e in self.engs:
            for e2 in self.engs:
                if e2 != e and self.cnt[e2] > 0:
                    self._wait(e, (e2, self.epoch[e2]), self.cnt[e2])
            for i in range(self.ndma):
                if self.dma_tgt[i] > 0:
                    self._wait(e, ('dma', i, self.dma_ep[i]), self.dma_tgt[i])

    def finish(self):
        e = 'sp'
        for i in range(self.ndma):
            if self.dma_tgt[i] > 0:
                self._wait(e, ('dma', i, self.dma_ep[i]), self.dma_tgt[i])
        for e2 in self.engs:
            if e2 != e and self.cnt[e2] > 0:
                self._wait(e, (e2, self.epoch[e2]), self.cnt[e2])

    def mm(self, ps, lhsT, rhs, start=True, stop=True, r=None, w=None):
        return self.op('pe', lambda e: e.matmul(ps, lhsT, rhs, start=start, stop=stop),
                       r=r if r is not None else [lhsT, rhs], w=w if w is not None else [ps])

    def tr(self, ps, in_, ident, r=None, w=None):
        return self.op('pe', lambda e: e.transpose(ps, in_, ident),
                       r=r if r is not None else [in_, ident], w=w if w is not None else [ps])

    def act(self, func, out, in_, scale=1.0, bias=None, accum_out=None, r=None, w=None, eng='act'):
        rr = [in_]
        kw = {}
        if bias is not None:
            kw['bias'] = bias
            if not isinstance(bias, (int, float)):
                rr.append(bias)
        if not isinstance(scale, (int, float)):
            rr.append(scale)
        ww = [out]
        if accum_out is not None:
            kw['accum_out'] = accum_out
            ww.append(accum_out)
        return self.op(eng, lambda e: e.activation(out=out, in_=in_, func=func, scale=scale, **kw),
                       r=r if r is not None else rr, w=w if w is not None else ww)

    def tt(self, out, in0, in1, op, r=None, w=None, eng='dve'):
        return self.op(eng, lambda e: e.tensor_tensor(out=out, in0=in0, in1=in1, op=op),
                       r=r if r is not None else [in0, in1], w=w if w is not None else [out])

    def ts(self, out, in0, s1, op0, s2=None, op1=None, r=None, w=None, eng='dve'):
        rr = [in0] + [s for s in (s1, s2) if s is not None and not isinstance(s, (int, float))]
        if s2 is None:
            f = lambda e: e.tensor_scalar(out=out, in0=in0, scalar1=s1, scalar2=None, op0=op0)
        else:
            f = lambda e: e.tensor_scalar(out=out, in0=in0, scalar1=s1, scalar2=s2, op0=op0, op1=op1)
        return self.op(eng, f, r=r if r is not None else rr, w=w if w is not None else [out])

    def stt(self, out, in0, scalar, in1, op0, op1, r=None, w=None, eng='dve'):
        rr = [in0, in1] + ([scalar] if not isinstance(scalar, (int, float)) else [])
        return self.op(eng, lambda e: e.scalar_tensor_tensor(out=out, in0=in0, scalar=scalar, in1=in1,
                                                             op0=op0, op1=op1),
                       r=r if r is not None else rr, w=w if w is not None else [out])

    def scan(self, out, d0, d1, init, op0=None, op1=None, r=None, w=None):
        op0 = op0 or ALU.mult
        op1 = op1 or ALU.add
        rr = [d0, d1] + ([init] if not isinstance(init, (int, float)) else [])
        return self.op('dve', lambda e: e.tensor_tensor_scan(out=out, data0=d0, data1=d1, initial=init,
                                                             op0=op0, op1=op1),
                       r=r if r is not None else rr, w=w if w is not None else [out])

    def copy(self, out, in_, eng='dve', r=None, w=None):
        if eng == 'act':
            return self.act(AF.Identity, out, in_, r=r, w=w)
        return self.op(eng, lambda e: e.tensor_copy(out=out, in_=in_),
                       r=r if r is not None else [in_], w=w if w is not None else [out])

    def memset(self, out, val, eng='dve', w=None):
        return self.op(eng, lambda e: e.memset(out, val), r=[], w=w if w is not None else [out])

    def recip(self, out, in_, r=None, w=None):
        return self.op('dve', lambda e: e.reciprocal(out=out, in_=in_),
                       r=r if r is not None else [in_], w=w if w is not None else [out])


class Prog:
    pass


def build(order=(0, 1, 2, 3)):
    nc = bass.Bass("TRN2", target_bir_lowering=False)
    es = contextlib.ExitStack()
    P = Prog()
    P.nc = nc
    with es:
        k = KB(nc, es)
        P.k = k

        def din(name, shape):
            return nc.dram_tensor(name, list(shape), F32, kind="ExternalInput").ap()

        def dout(name, shape):
            return nc.dram_tensor(name, list(shape), F32, kind="ExternalOutput").ap()

        P.uid = 0
        P.scope = es

        def sb(name, shape, dt=F32):
            P.uid += 1
            return P.scope.enter_context(nc.sbuf_tensor(f"{name}_u{P.uid}", list(shape), dt))

        @contextlib.contextmanager
        def layer_scope():
            old = P.scope
            with contextlib.ExitStack() as ls:
                P.scope = ls
                yield
                k.barrier()
            P.scope = old

        hin = din("hin", [D, WALL])
        yout = dout("yout", [D, SEQ + WS])
        normpre = din("normpre", [128, 4, 8])
        normpost = din("normpost", [128, 4, 8])
        ident_d = din("ident", [128, 128])
        lru_w_in = din("lru_w_in", [D, 2 * E])
        lru_cw = din("lru_cw", [128, 16, 4])
        lru_cb = din("lru_cb", [128, 16])
        lru_wa = din("lru_wa", [16, 128, 128])
        lru_wx = din("lru_wx", [16, 128, 128])
        lru_ba = din("lru_ba", [128, 16])
        lru_bx = din("lru_bx", [128, 16])
        lru_lam = din("lru_lam", [128, 16])
        lru_w_out = din("lru_w_out", [E, D])
        st_lru_conv = din("st_lru_conv", [128, 16, NS, 3])
        st_lru_h = din("st_lru_h", [128, 16, NS])
        o_p_lru_conv = dout("o_p_lru_conv", [128, 16, 3])
        o_p_lru_h = dout("o_p_lru_h", [128, 16])
        o_s_lru_conv = dout("o_s_lru_conv", [128, 16, NS, 3])
        o_s_lru_h = dout("o_s_lru_h", [128, 16, NS])

        s5_w_in = din("s5_w_in", [D, 2 * E])
        s5_are = din("s5_are", [128, 64])
        s5_aim = din("s5_aim", [128, 64])
        s5_ldt = din("s5_ldt", [128, 64])
        s5_bre = din("s5_bre", [16, 128, 4, 128])
        s5_bim = din("s5_bim", [16, 128, 4, 128])
        s5_cre = din("s5_cre", [16, 128, 4, 128])
        s5_cim = din("s5_cim", [16, 128, 4, 128])
        s5_d = din("s5_d", [128, 16])
        s5_glu_w = din("s5_glu_w", [E, E])
        s5_glu_b = din("s5_glu_b", [128, 16])
        s5_w_out = din("s5_w_out", [E, D])
        s5_iota = din("s5_iota", [128, 256])
        st_s5_re = din("st_s5_re", [128, 64, NS])
        st_s5_im = din("st_s5_im", [128, 64, NS])
        o_p_s5_re = dout("o_p_s5_re", [128, 64])
        o_p_s5_im = dout("o_p_s5_im", [128, 64])
        o_s_s5_re = dout("o_s_s5_re", [128, 64, NS])
        o_s_s5_im = dout("o_s_s5_im", [128, 64, NS])

        rw_mu = din("rw_mu", [128, 6, 8])
        rw_w_r = din("rw_w_r", [D, E])
        rw_w_k = din("rw_w_k", [D, E])
        rw_w_v = din("rw_w_v", [D, E])
        rw_w_g = din("rw_w_g", [D, E])
        rw_w_o = din("rw_w_o", [E, D])
        rw_w1 = din("rw_w1", [D, 64])
        rw_w2 = din("rw_w2", [64, E])
        rw_a1 = din("rw_a1", [D, 64])
        rw_a2 = din("rw_a2", [64, E])
        rw_vecs = din("rw_vecs", [128, 5, 16])
        rw_lnw = din("rw_lnw", [128, E])
        rw_lnb = din("rw_lnb", [128, E])
        rw_masks = din("rw_masks", [128, 2, 3, 128])
        rw_reset = din("rw_reset", [128, 128])
        rw_sel = din("rw_sel", [128, 2])
        rw_oh = din("rw_oh", [128, NS])
        st_rw_shift = din("st_rw_shift", [128, 8, NS])
        st_rw_wkv = din("st_rw_wkv", [16, 128, NS, 64])
        o_p_rw_shift = dout("o_p_rw_shift", [128, 8])
        o_s_rw_shift = dout("o_s_rw_shift", [128, 8, NS])
        o_p_rw_wkv = dout("o_p_rw_wkv", [128, 16, 64])
        o_s_rw_wkv = dout("o_s_rw_wkv", [16, 128, NS, 64])
        ret_w_q = din("ret_w_q", [D, D])
        ret_w_k = din("ret_w_k", [D, D])
        ret_w_v = din("ret_w_v", [D, E])
        ret_w_g = din("ret_w_g", [D, E])
        ret_w_o = din("ret_w_o", [E, D])
        ret_cos = din("ret_cos", [128, WALL])
        ret_sin = din("ret_sin", [128, WALL])
        ret_mk = din("ret_mk", [128, 2, 4, 128])
        ret_grow = din("ret_grow", [128, 2, 4, 128])
        ret_gcol = din("ret_gcol", [128, 3, 4])
        ret_sc = din("ret_sc", [128, NS, 4])
        ret_cmask = din("ret_cmask", [128, NS, 128])
        st_ret = din("st_ret", [NS, 4, 256, 512])
        o_p_ret = dout("o_p_ret", [4, 256, 512])
        o_s_ret = dout("o_s_ret", [NS, 4, 256, 512])

        H = sb("H", [128, 8, WMAX])
        XN = sb("XN", [128, 8, WMAX], BF16)
        BIGA = sb("BIGA", [128, 16, WMAX], BF16)
        HB = H[:, :, :].bitcast(BF16).rearrange("p a (b c) -> p (a b) c", c=WMAX)
        hspill = nc.dram_tensor("hspill", [128, 8, WMAX], F32, kind="Internal").ap()
        RS = sb("RS", [128, WMAX])
        ONES = sb("ONES", [128, 128], BF16)
        CONST = sb("CONST", [128, 8])
        NPRE = sb("NPRE", [128, 4, 8])
        NPOST = sb("NPOST", [128, 4, 8])
        WI = [sb(f"WI{i}", [128, 2, 8, 128], BF16) for i in range(2)]

        def mk_tf(n=9):
            return [[sb(f"TF{p}_{i}", [128, WMAX + 16]) for i in range(n)] for p in range(2)]

        def mk_tb():
            return [sb(f"TB{p}", [128, WMAX], BF16) for p in range(2)]
        PS = [es.enter_context(nc.psum_tensor(f"PS{i}", [128, 512], F32)) for i in range(8)]

        EPS_AP = CONST[:, 0:1]
        ONE_AP = CONST[:, 1:2]
        k.memset(CONST[:, 0:1], 1e-6)
        k.memset(CONST[:, 1:2], 1.0)
        k.memset(CONST[:, 2:3], -PI)
        NPI_AP = CONST[:, 2:3]
        k.memset(ONES[:, :], 1.0)
        k.dma('sp', NPRE[:, :, :], normpre[:, :, :])
        k.dma('sp', NPOST[:, :, :], normpost[:, :, :])

        hin_v = hin.rearrange("(kt p) c -> p kt c", p=128)
        yout_v = yout.rearrange("(kt p) c -> p kt c", p=128)

        def pre_norm(li, seg):
            W = seg['W']
            cts_ = ctiles(W)
            for ti, (c0, n) in enumerate(cts_):
                for kt in range(8):
                    k.act(AF.Square, XN[:, kt, c0:c0 + n], H[:, kt, c0:c0 + n])
            for ti, (c0, n) in enumerate(cts_):
                ps = PS[6 + ti % 2]
                for kt in range(8):
                    k.mm(ps[:, :n], ONES[:, :], XN[:, kt, c0:c0 + n], start=(kt == 0), stop=(kt == 7))
            for ti, (c0, n) in enumerate(cts_):
                ps = PS[6 + ti % 2]
                k.act(AF.Sqrt, RS[:, c0:c0 + n], ps[:, :n], scale=1.0 / D, bias=EPS_AP)
                k.recip(RS[:, c0:c0 + n], RS[:, c0:c0 + n])
            for ti, (c0, n) in enumerate(cts_):
                for kt in range(8):
                    k.stt(XN[:, kt, c0:c0 + n], H[:, kt, c0:c0 + n], NPRE[:, li, kt:kt + 1],
                          RS[:, c0:c0 + n], ALU.mult, ALU.mult)

        def out_proj(li, seg, Y, w_out, ZTd):
            W = seg['W']
            wv = w_out.rearrange("(kt p) n -> p kt n", p=128)
            cts = ctiles(W)
            ZSQ = [sb(f"ZSQ{i}", [128, 512], BF16) for i in range(2)]
            WO = [sb(f"WO{i}", [128, 16, 128], BF16) for i in range(2)]
            pending = None
            for d in range(8):
                wo = WO[d % 2]
                k.dma('pool', wo[:, :, :], wv[:, :, d * 128:(d + 1) * 128])
                for ti, (c0, n) in enumerate(cts):
                    ps = PS[(2 * d + ti) % 6]
                    for kt in range(16):
                        k.mm(ps[:, :n], wo[:, kt, :], Y[:, kt, c0:c0 + n], start=(kt == 0), stop=(kt == 15))
                    if pending is not None:
                        pending()
                    zs = ZSQ[(2 * d + ti) % 2]
                    k.act(AF.Identity, ZTd[d][:, c0:c0 + n], ps[:, :n])
                    k.act(AF.Square, zs[:, :n], ps[:, :n])
                    pending = (lambda ti=ti, n=n, zs=zs, d=d:
                               k.mm(PS[6 + ti][:, :n], ONES[:, :], zs[:, :n], start=(d == 0), stop=(d == 7)))
            pending()
            k.dma('sp', H[:, :, 0:W], hspill[:, :, 0:W])
            for ti, (c0, n) in enumerate(cts):
                k.act(AF.Sqrt, RS[:, c0:c0 + n], PS[6 + ti][:, :n], scale=1.0 / D, bias=EPS_AP)
                k.recip(RS[:, c0:c0 + n], RS[:, c0:c0 + n])
                for d in range(8):
                    k.stt(ZTd[d][:, c0:c0 + n], ZTd[d][:, c0:c0 + n], NPOST[:, li, d:d + 1],
                          RS[:, c0:c0 + n], ALU.mult, ALU.mult)
                    k.tt(H[:, d, c0:c0 + n], H[:, d, c0:c0 + n], ZTd[d][:, c0:c0 + n], ALU.add)

        LRUCONV = sb("LRUCONV", [128, 16, 3])
        LRUH = sb("LRUH", [128, 16])
        LCW = sb("LCW", [128, 16, 4])
        LCB = sb("LCB", [128, 16])
        LBA = sb("LBA", [128, 16])
        LBX = sb("LBX", [128, 16])
        LSP = sb("LSP", [128, 16])
        LSP8 = sb("LSP8", [128, 16])
        LSP16 = sb("LSP16", [128, 16])
        TMP16 = sb("TMP16", [128, NS])
        TMP16B = sb("TMP16B", [128, NS])

        def lru_setup():
            k.dma('sp', LCW[:, :, :], lru_cw[:, :, :])
            k.dma('sp', LCB[:, :], lru_cb[:, :])
            k.dma('sp', LBA[:, :], lru_ba[:, :])
            k.dma('sp', LBX[:, :], lru_bx[:, :])
            k.dma('sp', LSP[:, :], lru_lam[:, :])
            k.act(AF.Exp, LSP[:, :], LSP[:, :], scale=-1.0)
            k.act(AF.Ln, LSP[:, :], LSP[:, :], bias=ONE_AP)
            k.ts(LSP8[:, :], LSP[:, :], -8.0, ALU.mult)
            k.ts(LSP16[:, :], LSP[:, :], -16.0, ALU.mult)

        def lru_layer(li, seg, si):
            W, Wp, Ws = seg['W'], seg['Wp'], seg['Ws']
            cts = ctiles(W)
            wv = lru_w_in.rearrange("(kt p) n -> p kt n", p=128)
            Y = BIGA
            TF = mk_tf(9)
            TBh = mk_tb()
            UES = [sb(f"UES{i}", [128, NS, 3 + TS]) for i in range(2)]
            WAB = [sb(f"WAB{i}", [128, 2, 128], BF16) for i in range(2)]
            if Ws:
                STC = sb("STC", [128, 16, NS, 3])
                STH = sb("STH", [128, 16, NS])
                OSC = sb("OSC", [128, 16, NS, 3])
                OSH = sb("OSH", [128, 16, NS])
                k.dma('sp', STC[:, :, :, :], st_lru_conv[:, :, :, :])
                k.dma('sp', STH[:, :, :], st_lru_h[:, :, :])
            def lru_tile(j):
                par = j % 2
                UE, XC, GR, GI, A, S, BX, HS, SG = TF[par]
                XCB = TBh[par]
                ues = UES[par]
                wi = WI[par]
                wab = WAB[par]
                k.dma('pool', wi[:, 0, :, :], wv[:, :, j * 128:(j + 1) * 128])
                k.dma('pool', wi[:, 1, :, :], wv[:, :, E + j * 128:E + (j + 1) * 128])
                k.dma('pool', wab[:, 0, :], lru_wa[j, :, :])
                k.dma('pool', wab[:, 1, :], lru_wx[j, :, :])
                if si == 0:
                    k.memset(UE[:, 0:3], 0.0)
                else:
                    k.copy(UE[:, 0:3], LRUCONV[:, j, :])
                if Ws:
                    k.copy(ues[:, :, 0:3], STC[:, j, :, :])
                for ti, (c0, n) in enumerate(cts):
                    for kt in range(8):
                        k.mm(PS[ti][:, :n], wi[:, 0, kt, :], XN[:, kt, c0:c0 + n], start=(kt == 0), stop=(kt == 7))
                    for kt in range(8):
                        k.mm(PS[2 + ti][:, :n], wi[:, 1, kt, :], XN[:, kt, c0:c0 + n], start=(kt == 0), stop=(kt == 7))
                    npr = min(c0 + n, Wp) - c0
                    if npr > 0:
                        k.act(AF.Identity, UE[:, 3 + c0:3 + c0 + npr], PS[ti][:, :npr])
                    if c0 + n > Wp:
                        s0 = max(c0, Wp)
                        ns_ = c0 + n - s0
                        q0 = (s0 - Wp) // TS
                        k.act(AF.Identity, ues[:, q0:q0 + ns_ // TS, 3:3 + TS],
                              PS[ti][:, s0 - c0:s0 - c0 + ns_].rearrange("p (s t) -> p s t", t=TS))
                    k.act(AF.Silu, SG[:, c0:c0 + n], PS[2 + ti][:, :n])
                    yield
                k.ts(XC[:, 0:Wp], UE[:, 3:3 + Wp], LCW[:, j, 3:4], ALU.mult, LCB[:, j:j + 1], ALU.add)
                for c in (2, 1, 0):
                    k.stt(XC[:, 0:Wp], UE[:, c:c + Wp], LCW[:, j, c:c + 1], XC[:, 0:Wp], ALU.mult, ALU.add)
                if Ws:
                    XC3 = XC[:, Wp:W].rearrange("p (s t) -> p s t", t=TS)
                    k.ts(XC3, ues[:, :, 3:3 + TS], LCW[:, j, 3:4], ALU.mult, LCB[:, j:j + 1], ALU.add)
                    for c in (2, 1, 0):
                        k.stt(XC3, ues[:, :, c:c + TS], LCW[:, j, c:c + 1], XC3, ALU.mult, ALU.add)
                yield
                k.act(AF.Identity, XCB[:, 0:W], XC[:, 0:W])
                yield
                for ti, (c0, n) in enumerate(cts):
                    k.mm(PS[4 + ti][:, :n], wab[:, 0, :], XCB[:, c0:c0 + n])
                    k.mm(PS[6 + ti][:, :n], wab[:, 1, :], XCB[:, c0:c0 + n])
                    k.act(AF.Sigmoid, GR[:, c0:c0 + n], PS[4 + ti][:, :n], bias=LBA[:, j:j + 1])
                    k.act(AF.Sigmoid, GI[:, c0:c0 + n], PS[6 + ti][:, :n], bias=LBX[:, j:j + 1])
                yield
                k.act(AF.Exp, A[:, 0:W], GR[:, 0:W], scale=LSP8[:, j:j + 1])
                k.act(AF.Exp, S[:, 0:W], GR[:, 0:W], scale=LSP16[:, j:j + 1])
                k.act(AF.Sqrt, S[:, 0:W], S[:, 0:W], scale=-1.0, bias=ONE_AP)
                yield
                k.tt(BX[:, 0:W], S[:, 0:W], GI[:, 0:W], ALU.mult)
                k.tt(BX[:, 0:W], BX[:, 0:W], XC[:, 0:W], ALU.mult)
                k.scan(HS[:, 0:Wp], A[:, 0:Wp], BX[:, 0:Wp], 0.0 if si == 0 else LRUH[:, j:j + 1])
                if Ws:
                    A3 = A[:, Wp:W].rearrange("p (s t) -> p s t", t=TS)
                    BX3 = BX[:, Wp:W].rearrange("p (s t) -> p s t", t=TS)
                    HS3 = HS[:, Wp:W].rearrange("p (s t) -> p s t", t=TS)
                    for q in range(NS):
                        c1 = Wp + q * TS
                        k.scan(HS[:, c1:c1 + TS], A[:, c1:c1 + TS], BX[:, c1:c1 + TS], STH[:, j, q:q + 1])
                yield
                k.tt(Y[:, j, 0:W], HS[:, 0:W], SG[:, 0:W], ALU.mult)
                k.copy(LRUCONV[:, j, :], UE[:, Wp:Wp + 3])
                k.copy(LRUH[:, j:j + 1], HS[:, Wp - 1:Wp])
                if Ws:
                    k.copy(OSC[:, j, :, :], ues[:, :, TS:TS + 3])
                    k.copy(OSH[:, j, :], HS3[:, :, TS - 1])
                yield

            for jp in range(8):
                gens = [lru_tile(2 * jp), lru_tile(2 * jp + 1)]
                alive = [True, True]
                while any(alive):
                    for gi in range(2):
                        if alive[gi] and next(gens[gi], 'end') == 'end':
                            alive[gi] = False
            out_proj(li, seg, Y, lru_w_out, TF[0][0:8])
            if si == len(SEGS) - 1:
                k.dma('sp', o_p_lru_conv[:, :, :], LRUCONV[:, :, :])
                k.dma('sp', o_p_lru_h[:, :], LRUH[:, :])
                k.dma('sp', o_s_lru_conv[:, :, :, :], OSC[:, :, :, :])
                k.dma('sp', o_s_lru_h[:, :, :], OSH[:, :, :])


        def sm(name, shape=(128, 64), dt=F32):
            return sb(name, list(shape), dt)
        S5ARE, S5AIM, S5DT, S5MAG, S5TH = (sm(n) for n in ("S5ARE", "S5AIM", "S5DT", "S5MAG", "S5TH"))
        S5FRE, S5FIM, S5NFRE, S5NFIM, S5FIR, S5FII = (sm(n) for n in ("S5FRE", "S5FIM", "S5NFRE", "S5NFIM", "S5FIR", "S5FII"))
        S5T = [sm(f"S5T{i}") for i in range(6)]
        S5R, S5I, OP5R, OP5I = (sm(n) for n in ("S5R", "S5I", "OP5R", "OP5I"))
        S5TI = sm("S5TI", (128, 64), mybir.dt.int32)
        IOTA = sm("IOTA", (128, 256))
        S5D = sm("S5D", (128, 16))
        S5GB = sm("S5GB", (128, 16))
        TWO_PI = 2.0 * PI

        I32 = mybir.dt.int32

        def sincos(outs, outc, ang, tmp, tmpi):
            k.ts(tmp, ang, 1.0 / TWO_PI, ALU.mult)
            k.copy(tmpi, tmp)
            k.copy(tmp, tmpi)
            k.stt(tmp, tmp, -TWO_PI, ang, ALU.mult, ALU.add)
            k.ts(outc, tmp, 0.5 * PI, ALU.add)
            k.ts(outs, tmp, PI, ALU.is_gt, -TWO_PI, ALU.mult)
            k.tt(tmp, tmp, outs, ALU.add)
            k.act(AF.Sin, outs, tmp)
            k.ts(tmp, outc, PI, ALU.is_gt, -TWO_PI, ALU.mult)
            k.tt(tmp, tmp, outc, ALU.add)
            k.act(AF.Sin, outc, tmp)

        RUN = 256
        s5tab = nc.dram_tensor("s5tab", [64, 128, 2, RUN], F32, kind="Internal").ap()
        s5cp = nc.dram_tensor("s5cp", [16, 3, 128, 4, 128], BF16, kind="Internal").ap()
        CLAST = sm("CLAST", (128, 64, 3, 2))

        def s5_setup():
            k.dma('sp', S5ARE[:, :], s5_are[:, :])
            k.dma('sp', S5AIM[:, :], s5_aim[:, :])
            k.dma('sp', S5DT[:, :], s5_ldt[:, :])
            k.dma('sp', IOTA[:, :], s5_iota[:, :])
            k.dma('sp', S5D[:, :], s5_d[:, :])
            k.dma('sp', S5GB[:, :], s5_glu_b[:, :])
            t0, t1, t2, t3, t4, t5 = S5T
            k.act(AF.Exp, S5DT[:, :], S5DT[:, :])
            k.tt(t0[:, :], S5DT[:, :], S5ARE[:, :], ALU.mult)
            k.act(AF.Exp, S5MAG[:, :], t0[:, :])
            k.tt(S5TH[:, :], S5DT[:, :], S5AIM[:, :], ALU.mult)
            sincos(t1[:, :], t2[:, :], S5TH[:, :], t0[:, :], S5TI[:, :])
            k.tt(t3[:, :], S5MAG[:, :], t2[:, :], ALU.mult)
            k.tt(t4[:, :], S5MAG[:, :], t1[:, :], ALU.mult)
            k.ts(t3[:, :], t3[:, :], -1.0, ALU.add)
            k.tt(t0[:, :], S5ARE[:, :], S5ARE[:, :], ALU.mult)
            k.tt(t1[:, :], S5AIM[:, :], S5AIM[:, :], ALU.mult)
            k.tt(t0[:, :], t0[:, :], t1[:, :], ALU.add)
            k.recip(t0[:, :], t0[:, :])
            k.tt(t1[:, :], t3[:, :], S5ARE[:, :], ALU.mult)
            k.tt(t2[:, :], t4[:, :], S5AIM[:, :], ALU.mult)
            k.tt(t1[:, :], t1[:, :], t2[:, :], ALU.add)
            k.tt(S5FRE[:, :], t1[:, :], t0[:, :], ALU.mult)
            k.tt(t1[:, :], t4[:, :], S5ARE[:, :], ALU.mult)
            k.tt(t2[:, :], t3[:, :], S5AIM[:, :], ALU.mult)
            k.tt(t1[:, :], t1[:, :], t2[:, :], ALU.subtract)
            k.tt(S5FIM[:, :], t1[:, :], t0[:, :], ALU.mult)
            k.ts(S5NFRE[:, :], S5FRE[:, :], -1.0, ALU.mult)
            k.ts(S5NFIM[:, :], S5FIM[:, :], -1.0, ALU.mult)
            k.tt(t0[:, :], S5FRE[:, :], S5FRE[:, :], ALU.mult)
            k.tt(t1[:, :], S5FIM[:, :], S5FIM[:, :], ALU.mult)
            k.tt(t0[:, :], t0[:, :], t1[:, :], ALU.add)
            k.recip(t0[:, :], t0[:, :])
            k.tt(S5FIR[:, :], S5FRE[:, :], t0[:, :], ALU.mult)
            k.tt(S5FII[:, :], S5NFIM[:, :], t0[:, :], ALU.mult)
            with layer_scope():
                TT_ = [sm(f"TTB{i}", (128, 2, RUN)) for i in range(2)]
                TG_ = [sm(f"TTG{i}", (128, RUN)) for i in range(2)]
                TM_ = [sm(f"TTM{i}", (128, RUN)) for i in range(2)]
                TI_ = [sm(f"TTI{i}", (128, RUN), mybir.dt.int32) for i in range(2)]
                for sg in range(64):
                    p_ = sg % 2
                    k.ts(TG_[p_][:, :], IOTA[:, :], S5TH[:, sg:sg + 1], ALU.mult)
                    sincos(TT_[p_][:, 1, :], TT_[p_][:, 0, :], TG_[p_][:, :], TM_[p_][:, :], TI_[p_][:, :])
                    k.dma('sp', s5tab[sg, :, :, :], TT_[p_][:, :, :])
                    for vi, col in enumerate((RUN - 1, 15, TS - 1)):
                        k.copy(CLAST[:, sg, vi, :], TT_[p_][:, :, col])

        def s5_layer(li, seg, si):
            W, Wp, Ws = seg['W'], seg['Wp'], seg['Ws']
            cts = ctiles(W)
            wv = s5_w_in.rearrange("(kt p) n -> p kt n", p=128)
            BIGB = HB
            FB = [sb(f"FB{i}", [128, WMAX + 16]) for i in range(8)]
            WRs = [[FB[0], FB[1]], [FB[2], FB[3]]]
            U32s = [FB[4], FB[5]]
            G1, Q1 = FB[6], FB[7]
            UBs = [sm(f"UB{i}", (128, WMAX), BF16) for i in range(2)]
            PB = [sm(f"PB{i}", (128, 2, WMAX), BF16) for i in range(2)]
            VB = [sm(f"VB{i}", (128, 2, WMAX), BF16) for i in range(2)]
            WB = [sm(f"WB{i}", (128, 2, WMAX), BF16) for i in range(2)]
            TP = [[sm(f"TP{i}_{q}", (128, WMAX), BF16) for q in range(4)] for i in range(2)]
            NCPR = [sm(f"NCPR{i}", (128, 4, 128), BF16) for i in range(2)]
            TT1 = [sm(f"TT1{i}", (128, WMAX), BF16) for i in range(2)]
            TT2 = [sm(f"TT2{i}", (128, WMAX), BF16) for i in range(2)]
            TCB = [sm(f"TCB{i}", (128, 2, RUN), BF16) for i in range(2)]
            BBR = [sm(f"BBR{i}", (128, 4, 128), BF16) for i in range(2)]
            BBI = [sm(f"BBI{i}", (128, 4, 128), BF16) for i in range(2)]
            CCR = [sm(f"CCR{i}", (128, 4, 128)) for i in range(2)]
            CCI = [sm(f"CCI{i}", (128, 4, 128)) for i in range(2)]
            CPR = [sm(f"CPR{i}", (128, 4, 128), BF16) for i in range(2)]
            CPI = [sm(f"CPI{i}", (128, 4, 128), BF16) for i in range(2)]
            CT1 = sm("CT1", (128, 128))
            XLS = [sm(f"XLS{i}", (128, 8, 4)) for i in range(2)]
            if Ws:
                X0R = sm("X0R", (128, 64, NS))
                X0I = sm("X0I", (128, 64, NS))
                OS5R = sm("OS5R", (128, 64, NS))
                OS5I = sm("OS5I", (128, 64, NS))
                X7 = [sm(f"X7{i}", (128, 2, NS)) for i in range(2)]
                k.dma('sp', X0R[:, :, :], st_s5_re[:, :, :])
                k.dma('sp', X0I[:, :, :], st_s5_im[:, :, :])
                fir = S5FIR[:, :].unsqueeze(2).to_broadcast([128, 64, NS])
                fii = S5FII[:, :].unsqueeze(2).to_broadcast([128, 64, NS])
                k.tt(OS5I[:, :, :], X0R[:, :, :], fii, ALU.mult)
                k.tt(OS5R[:, :, :], X0I[:, :, :], fii, ALU.mult)
                k.tt(X0R[:, :, :], X0R[:, :, :], fir, ALU.mult)
                k.tt(X0R[:, :, :], X0R[:, :, :], OS5R[:, :, :], ALU.subtract)
                k.tt(X0I[:, :, :], X0I[:, :, :], fir, ALU.mult)
                k.tt(X0I[:, :, :], X0I[:, :, :], OS5I[:, :, :], ALU.add)

            groups = []
            nfull = Wp // RUN
            if nfull:
                groups.append((0, nfull, RUN))
            if Wp - nfull * RUN > 0:
                groups.append((nfull * RUN, 1, Wp - nfull * RUN))
            if Ws:
                groups.append((Wp, Ws // TS, TS))
            runs = []
            c = 0
            while c < Wp:
                L = min(RUN, Wp - c)
                runs.append((c, L))
                c += L

            def tile_of(c):
                for ti, (c0, n) in enumerate(cts):
                    if c0 <= c < c0 + n:
                        return ti, c0
                raise ValueError

            def rot(dst, src, tcb, sign):
                for (g0, nr, rl) in groups:
                    sh = [128, nr, rl]
                    cb = tcb[:, 0, 0:rl].unsqueeze(1).to_broadcast(sh)
                    sb_ = tcb[:, 1, 0:rl].unsqueeze(1).to_broadcast(sh)
                    v3 = lambda t: t[:, g0:g0 + nr * rl].rearrange("p (r l) -> p r l", l=rl)
                    sr, si_ = v3(src[:, 0, :]), v3(src[:, 1, :])
                    t1, t2 = v3(rot.t1), v3(rot.t2)
                    k.tt(t1, sr, cb, ALU.mult)
                    k.tt(t2, si_, sb_, ALU.mult)
                    k.tt(v3(dst[:, 0, :]), t1, t2, ALU.add if sign < 0 else ALU.subtract)
                    k.tt(t1, si_, cb, ALU.mult)
                    k.tt(t2, sr, sb_, ALU.mult)
                    k.tt(v3(dst[:, 1, :]), t1, t2, ALU.subtract if sign < 0 else ALU.add)

            for j in range(16):
                par = j % 2
                U32 = U32s[par]
                UB = UBs[par]
                wi = WI[par]
                k.dma('pool', wi[:, 0, :, :], wv[:, :, j * 128:(j + 1) * 128])
                k.dma('pool', BBR[par][:, :, :], s5_bre[j, :, :, :])
                k.dma('pool', BBI[par][:, :, :], s5_bim[j, :, :, :])
                if si == 0:
                    k.dma('sp', CCR[par][:, :, :], s5_cre[j, :, :, :])
                    k.dma('sp', CCI[par][:, :, :], s5_cim[j, :, :, :])
                    for s_ in range(4):
                        sg = 4 * j + s_
                        k.ts(CT1[:, :], CCR[par][:, s_, :], S5FRE[:, sg:sg + 1], ALU.mult)
                        k.stt(CPR[par][:, s_, :], CCI[par][:, s_, :], S5NFIM[:, sg:sg + 1], CT1[:, :], ALU.mult, ALU.add)
                        k.ts(NCPR[par][:, s_, :], CPR[par][:, s_, :], -1.0, ALU.mult)
                        k.ts(CT1[:, :], CCR[par][:, s_, :], S5NFIM[:, sg:sg + 1], ALU.mult)
                        k.stt(CPI[par][:, s_, :], CCI[par][:, s_, :], S5NFRE[:, sg:sg + 1], CT1[:, :], ALU.mult, ALU.add)
                    k.dma('sp', s5cp[j, 0, :, :, :], CPR[par][:, :, :])
                    k.dma('sp', s5cp[j, 1, :, :, :], NCPR[par][:, :, :])
                    k.dma('sp', s5cp[j, 2, :, :, :], CPI[par][:, :, :])
                else:
                    k.dma('sp', CPR[par][:, :, :], s5cp[j, 0, :, :, :])
                    k.dma('sp', NCPR[par][:, :, :], s5cp[j, 1, :, :, :])
                    k.dma('sp', CPI[par][:, :, :], s5cp[j, 2, :, :, :])
                for ti, (c0, n) in enumerate(cts):
                    for kt in range(8):
                        k.mm(PS[ti][:, :n], wi[:, 0, kt, :], XN[:, kt, c0:c0 + n], start=(kt == 0), stop=(kt == 7))
                    k.act(AF.Identity, U32[:, c0:c0 + n], PS[ti][:, :n])
                    k.act(AF.Identity, UB[:, c0:c0 + n], PS[ti][:, :n])
                for pair in range(2):
                    tiles = [(2 * pair + q, q) for q in range(2)]
                    for s_, sl_ in tiles:
                        sg = 4 * j + s_
                        k.dma('pool', TCB[sl_][:, :, :], s5tab[sg, :, :, :])
                        for ti, (c0, n) in enumerate(cts):
                            k.mm(PS[2 + ti][:, :n], BBR[par][:, s_, :], UB[:, c0:c0 + n])
                            k.mm(PS[4 + ti][:, :n], BBI[par][:, s_, :], UB[:, c0:c0 + n])
                            k.act(AF.Identity, PB[sl_][:, 0, c0:c0 + n], PS[2 + ti][:, :n])
                            k.act(AF.Identity, PB[sl_][:, 1, c0:c0 + n], PS[4 + ti][:, :n])
                    for s_, sl_ in tiles:
                        rot.t1, rot.t2 = TT1[sl_], TT2[sl_]
                        rot(VB[sl_], PB[sl_], TCB[sl_], -1)
                    for ri, (c, L) in enumerate(runs):
                        for s_, sl_ in tiles:
                            sg = 4 * j + s_
                            WR, WIm = WRs[sl_]
                            xls = XLS[sl_]
                            magb = S5MAG[:, sg:sg + 1]
                            if ri == 0:
                                ir = 0.0 if si == 0 else S5R[:, sg:sg + 1]
                                ii = 0.0 if si == 0 else S5I[:, sg:sg + 1]
                            else:
                                ir, ii = xls[:, ri - 1, 0:1], xls[:, ri - 1, 1:2]
                            k.scan(WR[:, c:c + L], magb.to_broadcast([128, L]), VB[sl_][:, 0, c:c + L], ir)
                            k.scan(WIm[:, c:c + L], magb.to_broadcast([128, L]), VB[sl_][:, 1, c:c + L], ii)
                        for s_, sl_ in tiles:
                            sg = 4 * j + s_
                            WR, WIm = WRs[sl_]
                            xls = XLS[sl_]
                            last = ri == len(runs) - 1
                            orr = S5R[:, sg:sg + 1] if last else xls[:, ri, 0:1]
                            oii = S5I[:, sg:sg + 1] if last else xls[:, ri, 1:2]
                            wl, wil = WR[:, c + L - 1:c + L], WIm[:, c + L - 1:c + L]
                            vi_ = 0 if L == RUN else 1
                            assert L in (RUN, 16)
                            cl, sl2 = CLAST[:, sg, vi_, 0:1], CLAST[:, sg, vi_, 1:2]
                            k.tt(xls[:, ri, 2:3], wil, sl2, ALU.mult)
                            k.tt(xls[:, ri, 3:4], wl, sl2, ALU.mult)
                            k.stt(orr, wl, cl, xls[:, ri, 2:3], ALU.mult, ALU.subtract)
                            k.stt(oii, wil, cl, xls[:, ri, 3:4], ALU.mult, ALU.add)
                    if Ws:
                        for s_, sl_ in tiles:
                            sg = 4 * j + s_
                            WR, WIm = WRs[sl_]
                            magb = S5MAG[:, sg:sg + 1]
                            for q in range(NS):
                                c = Wp + q * TS
                                k.scan(WR[:, c:c + TS], magb.to_broadcast([128, TS]), VB[sl_][:, 0, c:c + TS], X0R[:, sg, q:q + 1])
                                k.scan(WIm[:, c:c + TS], magb.to_broadcast([128, TS]), VB[sl_][:, 1, c:c + TS], X0I[:, sg, q:q + 1])
                            wr7 = WR[:, Wp:W].rearrange("p (s t) -> p s t", t=TS)[:, :, TS - 1]
                            wi7 = WIm[:, Wp:W].rearrange("p (s t) -> p s t", t=TS)[:, :, TS - 1]
                            c7, s7 = CLAST[:, sg, 2, 0:1], CLAST[:, sg, 2, 1:2]
                            x7 = X7[sl_]
                            k.ts(TMP16[:, :], wi7, s7, ALU.mult)
                            k.stt(x7[:, 0, :], wr7, c7, TMP16[:, :], ALU.mult, ALU.subtract)
                            k.ts(TMP16B[:, :], wr7, s7, ALU.mult)
                            k.stt(x7[:, 1, :], wi7, c7, TMP16B[:, :], ALU.mult, ALU.add)
                            k.ts(TMP16[:, :], x7[:, 0, :], S5FRE[:, sg:sg + 1], ALU.mult)
                            k.stt(OS5R[:, sg, :], x7[:, 1, :], S5NFIM[:, sg:sg + 1], TMP16[:, :], ALU.mult, ALU.add)
                            k.ts(TMP16B[:, :], x7[:, 1, :], S5FRE[:, sg:sg + 1], ALU.mult)
                            k.stt(OS5I[:, sg, :], x7[:, 0, :], S5FIM[:, sg:sg + 1], TMP16B[:, :], ALU.mult, ALU.add)
                    for s_, sl_ in tiles:
                        WR, WIm = WRs[sl_]
                        k.act(AF.Identity, WB[sl_][:, 0, 0:W], WR[:, 0:W])
                        k.act(AF.Identity, WB[sl_][:, 1, 0:W], WIm[:, 0:W])
                    for s_, sl_ in tiles:
                        t1, t2, t3, t4 = TP[sl_]
                        for (g0, nr, rl) in groups:
                            sh = [128, nr, rl]
                            cb = TCB[sl_][:, 0, 0:rl].unsqueeze(1).to_broadcast(sh)
                            sb_ = TCB[sl_][:, 1, 0:rl].unsqueeze(1).to_broadcast(sh)
                            v3 = lambda t: t[:, g0:g0 + nr * rl].rearrange("p (r l) -> p r l", l=rl)
                            wr_, wi_ = v3(WB[sl_][:, 0, :]), v3(WB[sl_][:, 1, :])
                            k.tt(v3(t1), wr_, cb, ALU.mult)
                            k.tt(v3(t2), wi_, sb_, ALU.mult)
                            k.tt(v3(t3), wi_, cb, ALU.mult)
                            k.tt(v3(t4), wr_, sb_, ALU.mult)
                    for s_, sl_ in tiles:
                        t1, t2, t3, t4 = TP[sl_]
                        for ti, (c0, n) in enumerate(cts):
                            k.mm(PS[6 + ti][:, :n], CPR[par][:, s_, :], t1[:, c0:c0 + n], start=(s_ == 0), stop=False)
                            k.mm(PS[6 + ti][:, :n], NCPR[par][:, s_, :], t2[:, c0:c0 + n], start=False, stop=False)
                            k.mm(PS[6 + ti][:, :n], CPI[par][:, s_, :], t3[:, c0:c0 + n], start=False, stop=False)
                            k.mm(PS[6 + ti][:, :n], CPI[par][:, s_, :], t4[:, c0:c0 + n], start=False, stop=(s_ == 3))
                for ti, (c0, n) in enumerate(cts):
                    k.stt(G1[:, c0:c0 + n], U32[:, c0:c0 + n], S5D[:, j:j + 1], PS[6 + ti][:, :n], ALU.mult, ALU.add)
                k.act(AF.Square, Q1[:, 0:W], G1[:, 0:W])
                k.ts(Q1[:, 0:W], Q1[:, 0:W], 0.044715, ALU.mult, 1.0, ALU.add)
                k.tt(Q1[:, 0:W], Q1[:, 0:W], G1[:, 0:W], ALU.mult)
                k.act(AF.Sigmoid, Q1[:, 0:W], Q1[:, 0:W], scale=1.5957691216057308)
                k.tt(BIGA[:, j, 0:W], G1[:, 0:W], Q1[:, 0:W], ALU.mult)
            gw = s5_glu_w.rearrange("(kt p) n -> p kt n", p=128)
            WO = [sb(f"WG{i}", [128, 16, 128], BF16) for i in range(2)]
            for i in range(16):
                par = i % 2
                wo, wi = WO[par], WI[par]
                SGM, SG = FB[2 * par], FB[2 * par + 1]
                k.dma('pool', wo[:, :, :], gw[:, :, i * 128:(i + 1) * 128])
                k.dma('pool', wi[:, 1, :, :], wv[:, :, E + i * 128:E + (i + 1) * 128])
                for ti, (c0, n) in enumerate(cts):
                    pa, pb = PS[4 * par + ti], PS[4 * par + 2 + ti]
                    for kt in range(16):
                        k.mm(pa[:, :n], wo[:, kt, :], BIGA[:, kt, c0:c0 + n], start=(kt == 0), stop=(kt == 15))
                    for kt in range(8):
                        k.mm(pb[:, :n], wi[:, 1, kt, :], XN[:, kt, c0:c0 + n], start=(kt == 0), stop=(kt == 7))
                    k.act(AF.Sigmoid, SGM[:, c0:c0 + n], pa[:, :n], bias=S5GB[:, i:i + 1])
                    k.act(AF.Silu, SG[:, c0:c0 + n], pb[:, :n])
                k.tt(SGM[:, 0:W], SGM[:, 0:W], SG[:, 0:W], ALU.mult)
                k.tt(BIGB[:, i, 0:W], BIGA[:, i, 0:W], SGM[:, 0:W], ALU.mult)
            out_proj(li, seg, BIGB, s5_w_out, FB)
            if si == len(SEGS) - 1:
                t0, t1 = S5T[0], S5T[1]
                k.tt(t0[:, :], S5FRE[:, :], S5R[:, :], ALU.mult)
                k.tt(t1[:, :], S5FIM[:, :], S5I[:, :], ALU.mult)
                k.tt(OP5R[:, :], t0[:, :], t1[:, :], ALU.subtract)
                k.tt(t0[:, :], S5FRE[:, :], S5I[:, :], ALU.mult)
                k.tt(t1[:, :], S5FIM[:, :], S5R[:, :], ALU.mult)
                k.tt(OP5I[:, :], t0[:, :], t1[:, :], ALU.add)
                k.dma('sp', o_p_s5_re[:, :], OP5R[:, :])
                k.dma('sp', o_p_s5_im[:, :], OP5I[:, :])
                k.dma('sp', o_s_s5_re[:, :, :], OS5R[:, :, :])
                k.dma('sp', o_s_s5_im[:, :, :], OS5I[:, :, :])


        rets_d = nc.dram_tensor("rets_d", [128, 4, 2, 512], F32, kind="Internal").ap()
        IDB = sb("IDB", [128, 128], BF16)
        RET_G = [1.0 - 2.0 ** (-5.0 - h) for h in range(4)]

        k.dma('pool', IDB[:, :], ident_d[:, :])

        def ret_setup():
            pass

        def ret_layer(li, seg, si):
            W, Wp, Ws = seg['W'], seg['Wp'], seg['Ws']
            cts = ctiles(W)
            g0 = seg['p0']
            QF = sb("QF", [128, 8, WMAX], BF16)
            KF = sb("KF", [128, 8, WMAX], BF16)
            SBF = sb("SBF", [128, 4, 2, 512], BF16)
            RETS = sb("RETS", [128, 4, 2, 512])
            if si == 0:
                k.memset(RETS[:, :, :, :], 0.0)
            else:
                k.dma('sp', RETS[:, :, :, :], rets_d[:, :, :, :])
            for h_ in range(4):
                k.act(AF.Identity, SBF[:, h_, :, :], RETS[:, h_, :, :])
            VTB = sb("VTB", [128, 6, E], BF16)
            VTF = VTB[:, :, :].bitcast(F32).rearrange("p a b -> p (a b)")
            ZTd = [VTF[:, d * WMAX:(d + 1) * WMAX] for d in range(8)]
            GF = HB
            Y = BIGA
            PSB = [PS[i][:, :].bitcast(BF16) for i in range(8)]
            wq = ret_w_q.rearrange("(kt p) n -> p kt n", p=128)
            wk = ret_w_k.rearrange("(kt p) n -> p kt n", p=128)
            wv = ret_w_v.rearrange("(kt p) n -> p kt n", p=128)
            wg = ret_w_g.rearrange("(kt p) n -> p kt n", p=128)
            def phase1():
              RC = sb("RC", [128, WMAX])
              RSN = sb("RSN", [128, WMAX])
              T4 = [sb(f"RT{i}", [128, WMAX]) for i in range(4)]
              k.dma('sp', RC[:, 0:W], ret_cos[:, g0:g0 + W])
              k.dma('sp', RSN[:, 0:W], ret_sin[:, g0:g0 + W])
              nload = 0
              if True:
                for (wsrc, dst, scale) in ((wq, QF, 1.0), (wk, KF, 1.0 / 16.0)):
                    for h in range(4):
                        for dt in range(2):
                            t = 2 * h + dt
                            wi = WI[nload % 2]
                            nload += 1
                            k.dma('pool', wi[:, 0, :, :], wsrc[:, :, t * 128:(t + 1) * 128])
                            for ti, (c0, n) in enumerate(cts):
                                ps = PS[2 * dt + ti]
                                for kt in range(8):
                                    k.mm(ps[:, :n], wi[:, 0, kt, :], XN[:, kt, c0:c0 + n], start=(kt == 0), stop=(kt == 7))
                                k.act(AF.Identity, T4[dt][:, c0:c0 + n], ps[:, :n], scale=scale)
                        k.tt(T4[2][:, 0:W], T4[0][:, 0:W], RC[:, 0:W], ALU.mult)
                        k.tt(T4[3][:, 0:W], T4[1][:, 0:W], RSN[:, 0:W], ALU.mult)
                        k.tt(dst[:, 2 * h, 0:W], T4[2][:, 0:W], T4[3][:, 0:W], ALU.subtract)
                        k.tt(T4[2][:, 0:W], T4[1][:, 0:W], RC[:, 0:W], ALU.mult)
                        k.tt(T4[3][:, 0:W], T4[0][:, 0:W], RSN[:, 0:W], ALU.mult)
                        k.tt(dst[:, 2 * h + 1, 0:W], T4[2][:, 0:W], T4[3][:, 0:W], ALU.add)
                for t in range(16):
                    wi = WI[nload % 2]
                    nload += 1
                    k.dma('pool', wi[:, 0, :, :], wg[:, :, t * 128:(t + 1) * 128])
                    for ti, (c0, n) in enumerate(cts):
                        ps = PS[4 + 2 * (t % 2) + ti]
                        for kt in range(8):
                            k.mm(ps[:, :n], wi[:, 0, kt, :], XN[:, kt, c0:c0 + n], start=(kt == 0), stop=(kt == 7))
                        k.act(AF.Silu, GF[:, t, c0:c0 + n], ps[:, :n])
            chunks = [(c, min(128, Wp - c)) for c in range(0, Wp, 128)]
            if Ws:
                chunks.append((Wp, 128))

            def phase2():
              WV = sb("WV", [128, 8, 512], BF16)
              if True:
                for eg in range(4):
                    k.dma('pool', WV[:, :, :], wv[:, :, eg * 512:(eg + 1) * 512])
                    for ci, (c0, L) in enumerate(chunks):
                        ps = PS[ci % 4]
                        for kt in range(8):
                            k.mm(ps[0:L, :], XN[:, kt, c0:c0 + L], WV[:, kt, :], start=(kt == 0), stop=(kt == 7))
                        k.act(AF.Identity, VTB[0:L, ci, eg * 512:(eg + 1) * 512], ps[0:L, :])
            def phase3():
              MK = sb("MK", [128, 2, 4, 128])
              GROW = sb("GROW", [128, 2, 4, 128])
              GCOL = sb("GCOL", [128, 3, 4])
              ST = [sb(f"ST{i}", [128, 128], BF16) for i in range(2)]
              QG = [sb(f"QG{i}", [128, 2, 128], BF16) for i in range(2)]
              YN = [sb(f"YN{i}", [128, 512], BF16) for i in range(2)]
              JUNK = sb("JUNK", [128, 512], BF16)
              SS = [sb(f"SS{i}", [128, 1]) for i in range(2)]
              KW = [sb(f"KW{i}", [128, 2, 128], BF16) for i in range(2)]
              k.dma('sp', MK[:, :, :, :], ret_mk[:, :, :, :])
              k.dma('sp', GROW[:, :, :, :], ret_grow[:, :, :, :])
              k.dma('sp', GCOL[:, :, :], ret_gcol[:, :, :])
              if Ws:
                SC = sb("SC", [128, NS, 4])
                CMASK = sb("CMASK", [128, NS, 128], BF16)
                KTS = sb("KTS", [128, 2, 128], BF16)
                QGZ = [sb(f"QGZ{i}", [128, 2, 128], BF16) for i in range(2)]
                KWZ = [sb(f"KWZ{i}", [128, 2, 128], BF16) for i in range(2)]
                S0F = [sb(f"S0F{i}", [128, 2, 512]) for i in range(2)]
                S0B = [sb(f"S0B{i}", [128, 2, 512], BF16) for i in range(2)]
                k.dma('sp', SC[:, :, :], ret_sc[:, :, :])
                k.dma('pool', CMASK[:, :, :], ret_cmask[:, :, :])
              if True:
                for ci, (c0, L) in enumerate(chunks):
                    samp = c0 >= Wp
                    mv = 1 if samp else 0
                    gv = 2 if samp else (0 if L == 128 else 1)

                    def head_gen(h):
                        hp = h % 2
                        gL = RET_G[h] ** (TS if samp else L)
                        for dt in range(2):
                            k.tt(QG[hp][:, dt, 0:L], QF[:, 2 * h + dt, c0:c0 + L], GROW[:, mv, h, 0:L], ALU.mult)
                        pss = PS[hp]
                        for dt in range(2):
                            k.mm(pss[0:L, 0:L], KF[:, 2 * h + dt, c0:c0 + L], QF[:, 2 * h + dt, c0:c0 + L],
                                 start=(dt == 0), stop=(dt == 1))
                        yield
                        k.tt(ST[hp][0:L, 0:L], pss[0:L, 0:L], MK[0:L, mv, h, 0:L], ALU.mult)
                        psy = PS[2 + hp]
                        vt = VTB[0:L, ci, h * 512:(h + 1) * 512]
                        k.mm(psy[0:L, :], ST[hp][0:L, 0:L], vt, start=True, stop=False)
                        if not samp:
                            for dt in range(2):
                                k.mm(psy[0:L, :], QG[hp][:, dt, 0:L], SBF[:, h, dt, :], start=False, stop=(dt == 1))
                        else:
                            for dt in range(2):
                                pst = PSB[4 + hp][0:L, 512 + dt * 128:512 + (dt + 1) * 128]
                                k.tr(pst, KF[:, 2 * h + dt, c0:c0 + L], IDB[:, :])
                                k.act(AF.Identity, KTS[0:L, dt, :], pst)
                            for i in range(NS):
                                ip = i % 2
                                k.dma('sp', S0F[ip][:, :, :], st_ret[i, h].rearrange("(dt p) e -> p dt e", p=128))
                                k.act(AF.Identity, S0B[ip][:, :, :], S0F[ip][:, :, :])
                                k.tt(QGZ[ip][:, :, :], QG[hp][:, :, :],
                                     CMASK[:, i, :].unsqueeze(1).to_broadcast([128, 2, 128]), ALU.mult)
                                for dt in range(2):
                                    k.mm(psy[0:L, :], QGZ[ip][:, dt, :], S0B[ip][:, dt, :], start=False,
                                         stop=(i == NS - 1 and dt == 1))
                                k.act(AF.Identity, KWZ[ip][:, :, :], KTS[:, :, :], scale=SC[:, i, h:h + 1])
                                for dt in range(2):
                                    k.mm(PS[6 + dt][:, :], KWZ[ip][:, dt, :], vt, start=True, stop=True)
                                    k.stt(S0F[ip][:, dt, :], S0F[ip][:, dt, :], gL, PS[6 + dt][:, :], ALU.mult, ALU.add)
                                k.dma('sp', o_s_ret[i, h].rearrange("(dt p) e -> p dt e", p=128), S0F[ip][:, :, :])
                        yield
                        k.act(AF.Square, JUNK[0:L, :], psy[0:L, :], accum_out=SS[hp][0:L, 0:1])
                        k.act(AF.Sqrt, SS[hp][0:L, 0:1], SS[hp][0:L, 0:1], scale=1.0 / 512.0, bias=CONST[0:L, 0:1])
                        yield
                        k.recip(SS[hp][0:L, 0:1], SS[hp][0:L, 0:1])
                        k.act(AF.Identity, YN[hp][0:L, :], psy[0:L, :], scale=SS[hp][0:L, 0:1])
                        yield
                        for et in range(4):
                            pst = PSB[4 + hp][:, et * 128:et * 128 + L]
                            k.tr(pst, YN[hp][0:L, et * 128:(et + 1) * 128], IDB[0:L, 0:L])
                        yield
                        for et in range(4):
                            pst = PSB[4 + hp][:, et * 128:et * 128 + L]
                            k.tt(Y[:, 4 * h + et, c0:c0 + L], pst, GF[:, 4 * h + et, c0:c0 + L], ALU.mult)
                        if not samp:
                            for dt in range(2):
                                pst = PSB[4 + hp][0:L, 512 + dt * 128:512 + (dt + 1) * 128]
                                k.tr(pst, KF[:, 2 * h + dt, c0:c0 + L], IDB[:, :])
                            yield
                            for dt in range(2):
                                pst = PSB[4 + hp][0:L, 512 + dt * 128:512 + (dt + 1) * 128]
                                k.act(AF.Identity, KW[hp][0:L, dt, :], pst, scale=GCOL[0:L, gv, h:h + 1])
                            yield
                            for dt in range(2):
                                k.mm(PS[6 + dt][:, :], KW[hp][0:L, dt, :], vt, start=True, stop=True)
                                k.stt(RETS[:, h, dt, :], RETS[:, h, dt, :], gL, PS[6 + dt][:, :], ALU.mult, ALU.add)
                                k.act(AF.Identity, SBF[:, h, dt, :], RETS[:, h, dt, :])
                        yield

                    if samp:
                        for h in range(4):
                            for _ in head_gen(h):
                                pass
                    else:
                        for hpair in range(2):
                            gens = [head_gen(2 * hpair), head_gen(2 * hpair + 1)]
                            alive = [True, True]
                            while any(alive):
                                for gi in range(2):
                                    if alive[gi] and next(gens[gi], 'end') == 'end':
                                        alive[gi] = False

            with layer_scope():
                phase1()
            with layer_scope():
                phase2()
            with layer_scope():
                phase3()
            out_proj(li, seg, Y, ret_w_o, ZTd)
            if si == len(SEGS) - 1:
                k.dma('sp', o_p_ret.rearrange("h (dt p) e -> p h dt e", p=128), RETS[:, :, :, :])
            else:
                k.dma('sp', rets_d[:, :, :, :], RETS[:, :, :, :])


        RWM = sb("RWM", [128, 16, 64])
        RWMB = sb("RWMB", [128, 16, 64], BF16)
        SHC = sb("SHC", [128, 8])
        RMU = sb("RMU", [128, 6, 8])
        RVEC = sb("RVEC", [128, 5, 16])
        DEC_C = 0.6065306597126334

        def rwkv_setup():
            k.memset(RWM[:, :, :], 0.0)
            k.memset(RWMB[:, :, :], 0.0)
            k.memset(SHC[:, :], 0.0)
            k.memset(CONST[:, 3:4], 64e-5)
            k.dma('sp', RMU[:, :, :], rw_mu[:, :, :])
            k.dma('sp', RVEC[:, :, :], rw_vecs[:, :, :])

        def rwkv_layer(li, seg, si):
            W, Wp, Ws = seg['W'], seg['Wp'], seg['Ws']
            cts = ctiles(W)
            chunks = [(c, min(128, Wp - c)) for c in range(0, Wp, 128)]
            if Ws:
                chunks.append((Wp, 128))
            nch = len(chunks)
            PSB = [PS[i][:, :].bitcast(BF16) for i in range(8)]
            RW0, RA0, RKK, RKA, RRK = (RVEC[:, i, :] for i in range(5))
            Y = BIGA
            GF = HB

            def phaseA():
                XX = sb("XX", [128, 8, W], BF16)
                XMb = sb("XM", [128, 8, W], BF16)
                SHN = sb("SHN", [128, 8])
                W1 = sb("W1", [128, 8, 64], BF16)
                A1W = sb("A1W", [128, 8, 64], BF16)
                TW = sb("TW", [64, W], BF16)
                TA = sb("TA", [64, W], BF16)
                BONES = sb("BONES", [128, 128], BF16)
                MSK = sb("MSK", [128, 2, 3, 128])
                SEL = sb("SEL", [128, 2], BF16)
                RT = sb("RT", [128, 4, W], BF16)
                AT = sb("AT", [128, 4, W], BF16)
                BT_ = sb("BT_", [128, 4, W], BF16)
                KT_ = sb("KT_", [128, 4, W], BF16)
                GL = sb("GL", [128, 4, 8])
                VTg = sb("VTg", [128, 6, 512], BF16)
                LNW = sb("LNW", [128, 512])
                LNB = sb("LNB", [128, 512])
                if Ws:
                    OSS = sb("OSS", [128, 8, NS])
                    STS = sb("STS", [128, 8, NS])
                    RESET = sb("RESET", [128, 128])
                    CMASK = sb("CMASK", [128, NS, 128], BF16)
                    OH = sb("OH", [128, NS], BF16)
                    GLS = sb("GLS", [128, 4, NS])
                    k.dma('sp', RESET[:, :], rw_reset[:, :])
                    k.dma('pool', CMASK[:, :, :], ret_cmask[:, :, :])
                    k.dma('pool', OH[:, :], rw_oh[:, :])
                    k.dma('sp', STS[:, :, :], st_rw_shift[:, :, :])
                k.dma('sp', MSK[:, :, :, :], rw_masks[:, :, :, :])
                k.dma('pool', SEL[:, :], rw_sel[:, :])
                k.dma('pool', W1[:, :, :], rw_w1.rearrange("(kt p) n -> p kt n", p=128))
                k.dma('pool', A1W[:, :, :], rw_a1.rearrange("(kt p) n -> p kt n", p=128))
                k.memset(BONES[:, :], 0.0)
                k.memset(BONES[0:64, 0:64], 1.0)
                k.memset(BONES[64:128, 64:128], 1.0)
                for kt in range(8):
                    k.stt(SHN[:, kt:kt + 1], H[:, kt, Wp - 1:Wp], NPRE[:, li, kt:kt + 1], RS[:, Wp - 1:Wp],
                          ALU.mult, ALU.mult)
                if Ws:
                    for kt in range(8):
                        hv = H[:, kt, Wp:W].rearrange("p (s t) -> p s t", t=TS)[:, :, TS - 1]
                        rv = RS[:, Wp:W].rearrange("p (s t) -> p s t", t=TS)[:, :, TS - 1]
                        k.stt(OSS[:, kt, :], hv, NPRE[:, li, kt:kt + 1], rv, ALU.mult, ALU.mult)
                    k.dma('sp', o_s_rw_shift[:, :, :], OSS[:, :, :])
                for kt in range(8):
                    k.tt(XX[:, kt, 1:W], XN[:, kt, 0:W - 1], XN[:, kt, 1:W], ALU.subtract)
                    k.tt(XX[:, kt, 0:1], SHC[:, kt:kt + 1], XN[:, kt, 0:1], ALU.subtract)
                    if Ws:
                        xs = XX[:, kt, Wp:W].rearrange("p (s t) -> p s t", t=TS)[:, :, 0]
                        x0 = XN[:, kt, Wp:W].rearrange("p (s t) -> p s t", t=TS)[:, :, 0]
                        k.tt(xs, STS[:, kt, :], x0, ALU.subtract)
                k.copy(SHC[:, :], SHN[:, :])
                if si == len(SEGS) - 1:
                    k.dma('sp', o_p_rw_shift[:, :], SHC[:, :])

                def mix(n_, dst):
                    for kt in range(8):
                        k.stt(dst[:, kt, 0:W], XX[:, kt, 0:W], RMU[:, n_, kt:kt + 1], XN[:, kt, 0:W], ALU.mult, ALU.add)

                mix(1, XMb)
                for ti, (c0, n) in enumerate(cts):
                    for kt in range(8):
                        k.mm(PS[ti][0:64, :n], W1[:, kt, :], XMb[:, kt, c0:c0 + n], start=(kt == 0), stop=(kt == 7))
                    k.act(AF.Tanh, TW[:, c0:c0 + n], PS[ti][0:64, :n])
                mix(4, XMb)
                for ti, (c0, n) in enumerate(cts):
                    for kt in range(8):
                        k.mm(PS[2 + ti][0:64, :n], A1W[:, kt, :], XMb[:, kt, c0:c0 + n], start=(kt == 0), stop=(kt == 7))
                    k.act(AF.Identity, TA[:, c0:c0 + n], PS[2 + ti][0:64, :n])
                mix(5, XMb)
                wg = rw_w_g.rearrange("(kt p) n -> p kt n", p=128)
                for t in range(16):
                    wi = WI[t % 2]
                    k.dma('pool', wi[:, 0, :, :], wg[:, :, t * 128:(t + 1) * 128])
                    for ti, (c0, n) in enumerate(cts):
                        ps = PS[4 + 2 * (t % 2) + ti]
                        for kt in range(8):
                            k.mm(ps[:, :n], wi[:, 0, kt, :], XMb[:, kt, c0:c0 + n], start=(kt == 0), stop=(kt == 7))
                        k.act(AF.Silu, GF[:, t, c0:c0 + n], ps[:, :n])
                wr = rw_w_r.rearrange("(kt p) n -> p kt n", p=128)
                wk = rw_w_k.rearrange("(kt p) n -> p kt n", p=128)
                wv = rw_w_v.rearrange("(kt p) n -> p kt n", p=128)
                def passes(hg):
                    W2T = [sb(f"W2T{i}", [64, 2, 128], BF16) for i in range(2)]
                    TB7s = [[sb(f"TB7_{p}_{i}", [128, W]) for i in range(7)] for p in range(2)]
                    SQBs = [sb(f"SQB{p}", [128, W], BF16) for p in range(2)]
                    k.dma('sp', LNW[:, :], rw_lnw[:, hg * 512:(hg + 1) * 512])
                    k.dma('sp', LNB[:, :], rw_lnb[:, hg * 512:(hg + 1) * 512])
                    mix(2, XMb)
                    def p1_tile(tl):
                        t = 4 * hg + tl
                        wi = WI[tl % 2]
                        w2t = W2T[tl % 2]
                        KFt, AS, SG_, KK, T5, CS, EG = TB7s[tl % 2]
                        SQB = SQBs[tl % 2]
                        k.dma('pool', wi[:, 1, :, :], wk[:, :, t * 128:(t + 1) * 128])
                        k.dma('pool', w2t[:, 0, :], rw_w2[:, t * 128:(t + 1) * 128])
                        k.dma('pool', w2t[:, 1, :], rw_a2[:, t * 128:(t + 1) * 128])
                        for ti, (c0, n) in enumerate(cts):
                            for kt in range(8):
                                k.mm(PS[2 + ti][:, :n], wi[:, 1, kt, :], XMb[:, kt, c0:c0 + n], start=(kt == 0), stop=(kt == 7))
                            k.mm(PS[4 + ti][:, :n], w2t[:, 1, :], TA[:, c0:c0 + n])
                            k.mm(PS[6 + ti][:, :n], w2t[:, 0, :], TW[:, c0:c0 + n])
                            k.act(AF.Identity, KFt[:, c0:c0 + n], PS[2 + ti][:, :n])
                            k.act(AF.Sigmoid, AS[:, c0:c0 + n], PS[4 + ti][:, :n], bias=RA0[:, t:t + 1])
                            k.act(AF.Sigmoid, SG_[:, c0:c0 + n], PS[6 + ti][:, :n], bias=RW0[:, t:t + 1])
                            yield
                        w_ = slice(0, W)
                        k.ts(KK[:, w_], KFt[:, w_], RKK[:, t:t + 1], ALU.mult)
                        k.act(AF.Square, SQB[:, w_], KK[:, w_])
                        for ti, (c0, n) in enumerate(cts):
                            k.mm(PS[ti][:, :n], BONES[:, :], SQB[:, c0:c0 + n])
                            k.ts(T5[:, c0:c0 + n], PS[ti][:, :n], 1e-24, ALU.max)
                        yield
                        k.act(AF.Sqrt, T5[:, w_], T5[:, w_])
                        k.recip(T5[:, w_], T5[:, w_])
                        k.tt(KK[:, w_], KK[:, w_], T5[:, w_], ALU.mult)
                        yield
                        k.ts(T5[:, w_], AS[:, w_], 1.0, ALU.subtract, RKA[:, t:t + 1], ALU.mult)
                        k.stt(T5[:, w_], T5[:, w_], 1.0, KFt[:, w_], ALU.add, ALU.mult)
                        k.tt(KFt[:, w_], KK[:, w_], AS[:, w_], ALU.mult)
                        for (c0, L) in chunks:
                            if c0 >= Wp:
                                k.scan(CS[:, c0:c0 + L], RESET[:, 0:L], SG_[:, c0:c0 + L], 0.0)
                            else:
                                k.scan(CS[:, c0:c0 + L], ONE_AP.to_broadcast([128, L]), SG_[:, c0:c0 + L], 0.0)
                        yield
                        k.act(AF.Exp, EG[:, w_], CS[:, w_], scale=-DEC_C)
                        k.act(AF.Identity, RT[:, tl, w_], EG[:, w_])
                        for ci, (c0, L) in enumerate(chunks):
                            if c0 >= Wp:
                                k.copy(GLS[:, tl, :], EG[:, c0:c0 + L].rearrange("p (s t) -> p s t", t=TS)[:, :, TS - 1])
                            else:
                                k.copy(GL[:, tl, ci:ci + 1], EG[:, c0 + L - 1:c0 + L])
                        yield
                        k.act(AF.Exp, AS[:, w_], CS[:, w_], scale=DEC_C)
                        k.tt(BT_[:, tl, w_], KFt[:, w_], AS[:, w_], ALU.mult)
                        k.tt(KT_[:, tl, w_], T5[:, w_], AS[:, w_], ALU.mult)
                        k.tt(CS[:, w_], CS[:, w_], SG_[:, w_], ALU.subtract)
                        k.act(AF.Exp, CS[:, w_], CS[:, w_], scale=-DEC_C)
                        k.stt(AT[:, tl, w_], KK[:, w_], -1.0, CS[:, w_], ALU.mult, ALU.mult)
                        yield

                    for tp in range(2):
                        gens = [p1_tile(2 * tp), p1_tile(2 * tp + 1)]
                        alive = [True, True]
                        while any(alive):
                            for gi in range(2):
                                if alive[gi] and next(gens[gi], 'end') == 'end':
                                    alive[gi] = False
                    mix(0, XMb)
                    for tl in range(4):
                        t = 4 * hg + tl
                        wi = WI[tl % 2]
                        k.dma('pool', wi[:, 0, :, :], wr[:, :, t * 128:(t + 1) * 128])
                        for ti, (c0, n) in enumerate(cts):
                            for kt in range(8):
                                k.mm(PS[ti][:, :n], wi[:, 0, kt, :], XMb[:, kt, c0:c0 + n], start=(kt == 0), stop=(kt == 7))
                            k.tt(RT[:, tl, c0:c0 + n], PS[ti][:, :n], RT[:, tl, c0:c0 + n], ALU.mult)

                def pass3(hg):
                    mix(3, XMb)
                    cnt_ = 0
                    for q in range(4):
                        wi = WI[q % 2]
                        k.dma('pool', wi[:, 0, :, :], wv[:, :, hg * 512 + q * 128:hg * 512 + (q + 1) * 128])
                        for ci, (c0, L) in enumerate(chunks):
                            ps = PS[5 + ci % 3]
                            for kt in range(8):
                                k.mm(ps[0:L, 0:128], XMb[:, kt, c0:c0 + L], wi[:, 0, kt, :], start=(kt == 0), stop=(kt == 7))
                            k.act(AF.Identity, VTg[0:L, ci, q * 128:(q + 1) * 128], ps[0:L, 0:128])
                            cnt_ += 1
                            if cnt_ % 4 == 0:
                                yield

                def chunkloop(hg):
                    MATS = [sb(f"MATS{i}", [128, 4, 8, 128], BF16) for i in range(2)]
                    LJ = [[sb(f"LJ{g}_{i}", [128, 4, 128], BF16) for i in range(2)] for g in range(2)]
                    NJ = [[sb(f"NJ{g}_{i}", [128, 4, 128], BF16) for i in range(2)] for g in range(2)]
                    PJ = [[sb(f"PJ{g}_{i}", [128, 4, 128], BF16) for i in range(2)] for g in range(2)]
                    BKTs = [sb(f"BKT{i}", [128, 1024], BF16) for i in range(2)]
                    RKc = sb("RKc", [128, 4, 128], BF16)
                    XB = sb("XB", [128, 512], BF16)
                    UB = sb("UB", [128, 512], BF16)
                    YS = sb("YS", [128, 512])
                    SQ = sb("SQ", [128, 512])
                    OUTB = sb("OUTB", [128, 512], BF16)
                    ST8 = sb("ST8", [128, 8])
                    ST8b = sb("ST8b", [128, 8])
                    if Ws:
                        S0B = sb("S0B", [128, NS, 64], BF16)
                        S0T = sb("S0T", [128, NS, 64])
                        AZt = sb("AZt", [128, NS, 128], BF16)
                        UZt = AZt
                        VZt = sb("VZt", [128, NS, 128], BF16)

                    def cinfo(ci):
                        c0, L = chunks[ci]
                        samp = c0 >= Wp
                        mv = 1 if samp else 0
                        blk_ = TS if samp else L
                        nlev = 1
                        while (1 << (nlev + 1)) < blk_:
                            nlev += 1
                        return c0, L, samp, mv, nlev

                    def precompute(ci, gen=None):
                        c0, L, samp, mv, nlev = cinfo(ci)

                        def step():
                            if gen is not None:
                                next(gen, None)
                        MX = MATS[ci % 2]
                        MSU = MSK[0:L, mv, 0, 0:L]
                        MSL = MSK[0:L, mv, 1, 0:L]
                        MIU = MSK[0:L, mv, 2, 0:L]
                        v4 = lambda ps: ps[0:L, :].rearrange("p (a b) -> p a b", b=128)[:, :, 0:L]
                        m4 = lambda M: M.unsqueeze(1).to_broadcast([L, 4, L])
                        v2 = lambda ps: ps[0:L, 0:256].rearrange("p (a b) -> p a b", b=128)[:, :, 0:L]
                        m2 = lambda M: M.unsqueeze(1).to_broadcast([L, 2, L])
                        for g4 in range(2):
                            lj, nj, pj = LJ[g4], NJ[g4], PJ[g4]

                            def st1(types):
                                for ti_, ty in enumerate(types):
                                    for i in range(4):
                                        hl = 4 * g4 + i
                                        tl, hh = hl // 2, hl % 2
                                        hr = slice(hh * 64, hh * 64 + 64)
                                        At, Bt = AT[hr, tl, c0:c0 + L], BT_[hr, tl, c0:c0 + L]
                                        Kt, Rt = KT_[hr, tl, c0:c0 + L], RT[hr, tl, c0:c0 + L]
                                        lhs, rhs = {'nab': (Bt, At), 'lab': (At, Bt), 'nak': (Kt, At),
                                                    'nrb': (Bt, Rt), 'nrk': (Kt, Rt)}[ty]
                                        pr = i // 2
                                        k.mm(PS[2 * ti_ + hh][0:L, pr * 128:pr * 128 + L], lhs, rhs)
                            st1(['nab', 'lab'])
                            for hh in range(2):
                                k.tt(nj[0][0:L, 2 * hh:2 * hh + 2, 0:L], v2(PS[hh]), m2(MSU), ALU.mult)
                                k.tt(lj[0][0:L, 2 * hh:2 * hh + 2, 0:L], v2(PS[2 + hh]), m2(MSL), ALU.mult)
                            k.tt(pj[0][0:L, :, 0:L], nj[0][0:L, :, 0:L], m4(IDB[0:L, 0:L]), ALU.add)
                            st1(['nak', 'nrb'])
                            for hh in range(2):
                                ms = slice(4 * g4 + 2 * hh, 4 * g4 + 2 * hh + 2)
                                k.tt(MX[0:L, 1, ms, 0:L], v2(PS[hh]), m2(MSU), ALU.mult)
                                k.tt(MX[0:L, 2, ms, 0:L], v2(PS[2 + hh]), m2(MIU), ALU.mult)
                            st1(['nrk'])
                            for hh in range(2):
                                ms = slice(4 * g4 + 2 * hh, 4 * g4 + 2 * hh + 2)
                                k.tt(MX[0:L, 3, ms, 0:L], v2(PS[hh]), m2(MIU), ALU.mult)
                        for j in range(1, nlev + 1):
                            last = j == nlev
                            step()
                            for g4 in range(2):
                                lj, nj, pj = LJ[g4], NJ[g4], PJ[g4]
                                bx, by = PS[2 * g4], PS[2 * g4 + 1]
                                lp, np_ = lj[(j - 1) % 2], nj[(j - 1) % 2]
                                ln, nn = lj[j % 2], nj[j % 2]
                                for i in range(4):
                                    k.mm(bx[0:L, i * 128:i * 128 + L], np_[0:L, i, 0:L], lp[0:L, i, 0:L])
                                if not last:
                                    for i in range(4):
                                        k.mm(by[0:L, i * 128:i * 128 + L], lp[0:L, i, 0:L], np_[0:L, i, 0:L])
                                k.act(AF.Identity, ln[0:L, :, 0:L], v4(bx))
                                if not last:
                                    k.copy(nn[0:L, :, 0:L], v4(by))
                            for g4 in range(2):
                                lj, pj = LJ[g4], PJ[g4]
                                bx = PS[2 * g4]
                                hs = slice(4 * g4, 4 * g4 + 4)
                                ln, pp, pn = lj[j % 2], pj[(j - 1) % 2], pj[j % 2]
                                for i in range(4):
                                    k.mm(bx[0:L, i * 128:i * 128 + L], IDB[0:L, 0:L], pp[0:L, i, 0:L], start=True, stop=False)
                                    k.mm(bx[0:L, i * 128:i * 128 + L], ln[0:L, i, 0:L], pp[0:L, i, 0:L], start=False, stop=True)
                                Pn = MX[0:L, 0, hs, 0:L] if last else pn[0:L, :, 0:L]
                                k.act(AF.Identity, Pn, v4(bx))
                        p5 = PSB[4]
                        bkt = BKTs[ci % 2]
                        for tl in range(4):
                            k.tr(p5[0:L, tl * 128:(tl + 1) * 128], BT_[:, tl, c0:c0 + L], IDB[:, :])
                            k.tr(p5[0:L, 512 + tl * 128:512 + (tl + 1) * 128], KT_[:, tl, c0:c0 + L], IDB[:, :])
                        k.act(AF.Identity, bkt[0:L, :], p5[0:L, :])

                    def hidx(hl):
                        return 4 * (hl // 4) + ((hl % 4) % 2) * 2 + (hl % 4) // 2

                    def sequential(ci):
                        c0, L, samp, mv, nlev = cinfo(ci)
                        MX = MATS[ci % 2]
                        BKT = BKTs[ci % 2]
                        p5 = PSB[5]
                        vt = lambda hl: VTg[0:L, ci, hl * 64:(hl + 1) * 64]
                        for hl in range(8):
                            tl, hh = hl // 2, hl % 2
                            hr = slice(hh * 64, hh * 64 + 64)
                            po = PS[5][0:L, hl * 64:(hl + 1) * 64]
                            if not samp:
                                k.mm(po, AT[hr, tl, c0:c0 + L], RWMB[hr, 4 * hg + tl, :], start=True, stop=False)
                            else:
                                if hh == 0:
                                    k.dma('pool', S0B[:, :, :], st_rw_wkv[4 * hg + tl, :, :, :])
                                    k.tt(AZt[:, :, :], AT[:, tl, c0:c0 + L].unsqueeze(1).to_broadcast([128, NS, 128]),
                                         CMASK[:, :, :], ALU.mult)
                                for i in range(NS):
                                    k.mm(po, AZt[hr, i, :], S0B[hr, i, :], start=(i == 0), stop=False)
                            k.mm(po, MX[0:L, 1, hidx(hl), 0:L], vt(hl), start=False, stop=True)
                        k.act(AF.Identity, XB[0:L, :], PS[5][0:L, :])
                        yield
                        for hl in range(8):
                            k.mm(PS[5][0:L, hl * 64:(hl + 1) * 64], MX[0:L, 0, hidx(hl), 0:L], XB[0:L, hl * 64:(hl + 1) * 64])
                        k.act(AF.Identity, UB[0:L, :], PS[5][0:L, :])
                        yield
                        if not samp:
                            for hl in range(8):
                                tl, hh = hl // 2, hl % 2
                                hr = slice(hh * 64, hh * 64 + 64)
                                po = PS[7][hr, tl * 64:(tl + 1) * 64]
                                k.mm(po, BKT[0:L, tl * 128 + hh * 64:tl * 128 + hh * 64 + 64], UB[0:L, hl * 64:(hl + 1) * 64],
                                     start=True, stop=False)
                                k.mm(po, BKT[0:L, 512 + tl * 128 + hh * 64:512 + tl * 128 + hh * 64 + 64], vt(hl),
                                     start=False, stop=True)
                        yield
                        for hl in range(8):
                            tl, hh = hl // 2, hl % 2
                            hr = slice(hh * 64, hh * 64 + 64)
                            po = PS[6][0:L, hl * 64:(hl + 1) * 64]
                            if not samp:
                                k.mm(po, RT[hr, tl, c0:c0 + L], RWMB[hr, 4 * hg + tl, :], start=True, stop=False)
                            else:
                                if hh == 0:
                                    k.dma('pool', S0B[:, :, :], st_rw_wkv[4 * hg + tl, :, :, :])
                                    k.tt(AZt[:, :, :], RT[:, tl, c0:c0 + L].unsqueeze(1).to_broadcast([128, NS, 128]),
                                         CMASK[:, :, :], ALU.mult)
                                for i in range(NS):
                                    k.mm(po, AZt[hr, i, :], S0B[hr, i, :], start=(i == 0), stop=False)
                            k.mm(po, MX[0:L, 2, hidx(hl), 0:L], UB[0:L, hl * 64:(hl + 1) * 64], start=False, stop=False)
                            k.mm(po, MX[0:L, 3, hidx(hl), 0:L], vt(hl), start=False, stop=True)
                        if not samp:
                            mg = RWM[:, 4 * hg:4 * hg + 4, :]
                            k.tt(mg, mg, PS[7][:, 0:256].rearrange("p (a b) -> p a b", b=64), ALU.add)
                            k.tt(mg, mg, GL[:, :, ci:ci + 1].to_broadcast([128, 4, 64]), ALU.mult)
                            k.act(AF.Identity, RWMB[:, 4 * hg:4 * hg + 4, :], mg)
                        k.act(AF.Identity, YS[0:L, :], PS[6][0:L, :])
                        yield
                        if samp:
                            for tl in range(4):
                                t = 4 * hg + tl
                                k.dma('sp', S0T[:, :, :], st_rw_wkv[t, :, :, :])
                                ohb = OH[:, :].unsqueeze(2).to_broadcast([128, NS, 128])
                                k.tt(UZt[:, :, :], UB[:, tl * 128:(tl + 1) * 128].unsqueeze(1).to_broadcast([128, NS, 128]),
                                     ohb, ALU.mult)
                                k.tt(VZt[:, :, :], VTg[:, ci, tl * 128:(tl + 1) * 128].unsqueeze(1).to_broadcast([128, NS, 128]),
                                     ohb, ALU.mult)
                                for hh in range(2):
                                    hr = slice(hh * 64, hh * 64 + 64)
                                    for i in range(NS):
                                        po = PS[6 + i // 8][hr, (i % 8) * 64:(i % 8 + 1) * 64]
                                        k.mm(po, BKT[:, tl * 128 + hh * 64:tl * 128 + hh * 64 + 64],
                                             UZt[:, i, hh * 64:(hh + 1) * 64], start=True, stop=False)
                                        k.mm(po, BKT[:, 512 + tl * 128 + hh * 64:512 + tl * 128 + hh * 64 + 64],
                                             VZt[:, i, hh * 64:(hh + 1) * 64], start=False, stop=True)
                                for half in range(2):
                                    sv = S0T[:, half * 8:(half + 1) * 8, :]
                                    k.tt(sv, sv, PS[6 + half][:, :].rearrange("p (a b) -> p a b", b=64), ALU.add)
                                k.tt(S0T[:, :, :], S0T[:, :, :], GLS[:, tl, :].unsqueeze(2).to_broadcast([128, NS, 64]), ALU.mult)
                                k.dma('sp', o_s_rw_wkv[t, :, :, :], S0T[:, :, :])
                        yield
                        YS3 = YS[0:L, :].rearrange("p (a b) -> p a b", b=64)
                        SQ3 = SQ[0:L, :].rearrange("p (a b) -> p a b", b=64)
                        k.op('dve', lambda e: e.reduce_sum(out=ST8[0:L, :], in_=YS3, axis=AX.X), r=[YS], w=[ST8])
                        k.ts(ST8[0:L, :], ST8[0:L, :], 1.0 / 64.0, ALU.mult)
                        k.tt(YS3, YS3, ST8[0:L, :].unsqueeze(2).to_broadcast([L, 8, 64]), ALU.subtract)
                        k.tt(SQ3, YS3, YS3, ALU.mult)
                        k.op('dve', lambda e: e.reduce_sum(out=ST8b[0:L, :], in_=SQ3, axis=AX.X), r=[SQ], w=[ST8b])
                        k.act(AF.Sqrt, ST8b[0:L, :], ST8b[0:L, :], scale=1.0 / 64.0, bias=CONST[0:L, 3:4])
                        k.recip(ST8b[0:L, :], ST8b[0:L, :])
                        k.tt(YS3, YS3, ST8b[0:L, :].unsqueeze(2).to_broadcast([L, 8, 64]), ALU.mult)
                        k.tt(YS[0:L, :], YS[0:L, :], LNW[0:L, :], ALU.mult)
                        k.tt(YS[0:L, :], YS[0:L, :], LNB[0:L, :], ALU.add)
                        yield
                        for tl in range(4):
                            k.stt(RKc[:, tl, 0:L], RT[:, tl, c0:c0 + L], RRK[:, 4 * hg + tl:4 * hg + tl + 1],
                                  KT_[:, tl, c0:c0 + L], ALU.mult, ALU.mult)
                            k.mm(PS[7][0:L, 256 + 2 * tl:256 + 2 * tl + 2], RKc[:, tl, 0:L], SEL[:, :])
                        k.tt(SQ3, VTg[0:L, ci, :].rearrange("p (a b) -> p a b", b=64),
                             PS[7][0:L, 256:264].unsqueeze(2).to_broadcast([L, 8, 64]), ALU.mult)
                        k.tt(OUTB[0:L, :], YS[0:L, :], SQ[0:L, :], ALU.add)
                        for tl in range(4):
                            k.tr(p5[:, tl * 128:tl * 128 + L], OUTB[0:L, tl * 128:(tl + 1) * 128], IDB[0:L, 0:L])
                            k.tt(Y[:, 4 * hg + tl, c0:c0 + L], p5[:, tl * 128:tl * 128 + L],
                                 GF[:, 4 * hg + tl, c0:c0 + L], ALU.mult)

                    g3 = pass3(hg)
                    precompute(0, g3)
                    for _ in g3:
                        pass
                    for ci in range(nch):
                        gen = sequential(ci)
                        if ci + 1 < nch:
                            precompute(ci + 1, gen)
                        for _ in gen:
                            pass

                for hg in range(4):
                    with layer_scope():
                        passes(hg)
                    with layer_scope():
                        chunkloop(hg)
                if si == len(SEGS) - 1:
                    k.dma('sp', o_p_rw_wkv[:, :, :], RWM[:, :, :])

            with layer_scope():
                phaseA()
            ZTd = [sb(f"ZTD{i}", [128, WMAX]) for i in range(8)]
            out_proj(li, seg, Y, rw_w_o, ZTd)

        LAYERS = {0: (lru_setup, lru_layer), 1: (s5_setup, s5_layer), 2: (rwkv_setup, rwkv_layer), 3: (ret_setup, ret_layer)}
        for li in order:
            LAYERS[li][0]()

        for si, seg in enumerate(SEGS):
            seg = dict(seg)
            seg['W'] = seg['Wp'] + seg['Ws']
            W = seg['W']
            g0 = seg['p0']
            k.dma('sp', H[:, :, 0:W], hin_v[:, :, g0:g0 + W])
            for li in order:
                pre_norm(li, seg)
                k.dma('sp', hspill[:, :, 0:W], H[:, :, 0:W])
                with layer_scope():
                    LAYERS[li][1](li, seg, si)
            lo = NMETA if si == 0 else 0
            k.dma('sp', yout_v[:, :, g0 + lo - NMETA:g0 + W - NMETA], H[:, :, lo:W])
        k.finish()
    P.ninstr = k.ninstr
    return P


def _ct(v, ntile):
    v = np.asarray(v)
    lead = v.shape[:-1]
    v = v.reshape(lead + (ntile, 128))
    nd = v.ndim
    perm = (nd - 1, nd - 2) + tuple(range(nd - 2))
    return np.ascontiguousarray(v.transpose(perm))


def prep_core(inp, c):
    f = lambda a: np.ascontiguousarray(a, dtype=np.float32)
    m = {}
    xp = inp['x_prompt'][c]
    xs = inp['x_sample'][c * NS:(c + 1) * NS].reshape(WS, D)
    m['hin'] = f(np.concatenate([inp['meta_tokens'], xp, xs], axis=0).T)
    m['normpre'] = f(_ct(inp['norm_pre'], 8))
    m['normpre'] = f(m['normpre'].transpose(0, 2, 1))
    m['normpost'] = f(_ct(inp['norm_post'], 8).transpose(0, 2, 1))
    m['ident'] = np.eye(128, dtype=np.float32)
    m['lru_w_in'] = f(inp['lru_w_in'][0])
    m['lru_cw'] = f(_ct(inp['lru_conv_w'][0], 16))
    m['lru_cb'] = f(_ct(inp['lru_conv_b'][0], 16))
    m['lru_wa'] = f(inp['lru_wa'][0])
    m['lru_wx'] = f(inp['lru_wx'][0])
    m['lru_ba'] = f(inp['lru_ba'][0].T)
    m['lru_bx'] = f(inp['lru_bx'][0].T)
    m['lru_lam'] = f(_ct(inp['lru_lam'][0], 16))
    m['lru_w_out'] = f(inp['lru_w_out'][0])
    sl = slice(c * NS, (c + 1) * NS)
    m['st_lru_conv'] = f(_ct(inp['state_lru_conv'][0, sl], 16))
    m['st_lru_h'] = f(_ct(inp['state_lru_h'][0, sl], 16))
    m['s5_w_in'] = f(inp['s5_w_in'][0])
    chan = lambda a: f(np.asarray(a).reshape(64, 128).T)
    m['s5_are'] = chan(inp['s5_a_re'][0])
    m['s5_aim'] = chan(inp['s5_a_im'][0])
    m['s5_ldt'] = chan(np.repeat(inp['s5_log_dt'][0][:, None], 64, axis=1))
    bre = inp['s5_b_re'][0].reshape(16, 8, 64, 16)
    bim = inp['s5_b_im'][0].reshape(16, 8, 64, 16)
    cre = inp['s5_c_re'][0].reshape(16, 8, 16, 64)
    cim = inp['s5_c_im'][0].reshape(16, 8, 16, 64)
    BR = np.zeros((16, 128, 4, 128), np.float32); BI = np.zeros_like(BR)
    CR = np.zeros((16, 128, 4, 128), np.float32); CI = np.zeros_like(CR)
    for gl in range(8):
        s_, g2 = gl // 2, gl % 2
        BR[:, 16 * gl:16 * gl + 16, s_, g2 * 64:(g2 + 1) * 64] = bre[:, gl].transpose(0, 2, 1)
        BI[:, 16 * gl:16 * gl + 16, s_, g2 * 64:(g2 + 1) * 64] = bim[:, gl].transpose(0, 2, 1)
        CR[:, g2 * 64:(g2 + 1) * 64, s_, 16 * gl:16 * gl + 16] = cre[:, gl].transpose(0, 2, 1)
        CI[:, g2 * 64:(g2 + 1) * 64, s_, 16 * gl:16 * gl + 16] = cim[:, gl].transpose(0, 2, 1)
    m['s5_bre'], m['s5_bim'], m['s5_cre'], m['s5_cim'] = BR, BI, CR, CI
    m['s5_d'] = f(_ct(inp['s5_d'][0], 16))
    m['s5_glu_w'] = f(inp['s5_glu_w'][0])
    m['s5_glu_b'] = f(_ct(inp['s5_glu_b'][0], 16))
    m['s5_w_out'] = f(inp['s5_w_out'][0])
    m['s5_iota'] = f(np.tile(np.arange(1, 257, dtype=np.float32)[None, :], (128, 1)))
    m['st_s5_re'] = f(inp['state_s5_re'][0, sl].reshape(NS, 64, 128).transpose(2, 1, 0))
    m['st_s5_im'] = f(inp['state_s5_im'][0, sl].reshape(NS, 64, 128).transpose(2, 1, 0))
    m['rw_mu'] = f(_ct(inp['rwkv_mu'][0], 8))
    m['rw_mu'] = f(m['rw_mu'].transpose(0, 2, 1))
    for n_ in ('w_r', 'w_k', 'w_v', 'w_g', 'w_o'):
        m['rw_' + n_] = f(inp['rwkv_' + n_][0])
    m['rw_w1'] = f(inp['rwkv_w1'][0]); m['rw_w2'] = f(inp['rwkv_w2'][0])
    m['rw_a1'] = f(inp['rwkv_a1'][0]); m['rw_a2'] = f(inp['rwkv_a2'][0])
    vecs = np.stack([inp['rwkv_w0'][0], inp['rwkv_a0'][0], inp['rwkv_k_k'][0], inp['rwkv_k_a'][0], inp['rwkv_r_k'][0]])
    m['rw_vecs'] = f(_ct(vecs, 16).transpose(0, 2, 1))
    m['rw_lnw'] = f(np.tile(inp['rwkv_ln_w'][0][None, :], (128, 1)))
    m['rw_lnb'] = f(np.tile(inp['rwkv_ln_b'][0][None, :], (128, 1)))
    ii = np.arange(128)[:, None]; jj = np.arange(128)[None, :]
    same_ = (ii // TS) == (jj // TS)
    msk = np.zeros((128, 2, 3, 128), np.float32)
    msk[:, 0, 0] = ii < jj; msk[:, 0, 1] = jj < ii; msk[:, 0, 2] = ii <= jj
    msk[:, 1, 0] = (ii < jj) & same_; msk[:, 1, 1] = (jj < ii) & same_; msk[:, 1, 2] = (ii <= jj) & same_
    m['rw_masks'] = msk
    rs_ = np.ones((128, 128), np.float32); rs_[:, ::TS] = 0.0
    m['rw_reset'] = rs_
    sel = np.zeros((128, 2), np.float32); sel[:64, 0] = 1.0; sel[64:, 1] = 1.0
    m['rw_sel'] = sel
    m['rw_oh'] = f(np.arange(128)[:, None] // TS == np.arange(NS)[None, :])
    m['st_rw_shift'] = f(_ct(inp['state_rwkv_shift'][0, sl], 8))
    wkv = inp['state_rwkv_wkv'][0, sl].reshape(NS, 16, 2, 64, 64)
    m['st_rw_wkv'] = f(wkv.transpose(1, 2, 4, 0, 3).reshape(16, 128, NS, 64))
    for n_ in ('w_q', 'w_k', 'w_v', 'w_g', 'w_o'):
        m['ret_' + n_] = f(inp['ret_' + n_][0])
    half = 128
    inv = (np.float32(10000.0) ** (-(np.arange(half, dtype=np.float32) / np.float32(half)))).astype(np.float32)
    pos = np.concatenate([np.arange(PT), np.tile(16384 + np.arange(TS), NS)]).astype(np.float32)
    ang = (pos[None, :] * inv[:, None]).astype(np.float32).astype(np.float64)
    m['ret_cos'] = f(np.cos(ang))
    m['ret_sin'] = f(np.sin(ang))
    logg = np.log1p(-np.exp2(-5.0 - np.arange(4, dtype=np.float64)))
    mm_ = np.arange(128)[:, None]; ll = np.arange(128)[None, :]
    mk = np.zeros((128, 2, 4, 128)); grow = np.zeros((128, 2, 4, 128)); gcol = np.zeros((128, 3, 4))
    for h in range(4):
        dlt = ll - mm_
        mk[:, 0, h, :] = np.where(dlt >= 0, np.exp(dlt * logg[h]), 0.0)
        same = (mm_ // TS) == (ll // TS)
        mk[:, 1, h, :] = np.where((dlt >= 0) & same, np.exp(dlt * logg[h]), 0.0)
        grow[:, 0, h, :] = np.exp((np.arange(128) + 1.0) * logg[h])[None, :]
        grow[:, 1, h, :] = np.exp(((np.arange(128) % TS) + 1.0) * logg[h])[None, :]
        gcol[:, 0, h] = np.exp((127.0 - np.arange(128)) * logg[h])
        gcol[:, 1, h] = np.exp((15.0 - np.arange(128)) * logg[h])
        gcol[:, 2, h] = np.exp((TS - 1.0 - (np.arange(128) % TS)) * logg[h])
    m['ret_mk'], m['ret_grow'], m['ret_gcol'] = f(mk), f(grow), f(gcol)
    onehot = (np.arange(128)[:, None] // TS == np.arange(NS)[None, :]).astype(np.float64)
    m['ret_sc'] = f(onehot[:, :, None] * gcol[:, None, 2, :])
    m['ret_cmask'] = f(np.tile(onehot.T[None, :, :], (128, 1, 1)))
    m['st_ret'] = f(inp['state_ret'][0, sl])
    return m


def _unct(a):
    a = np.asarray(a)
    nd = a.ndim
    perm = tuple(range(2, nd)) + (1, 0)
    a = a.transpose(perm)
    return a.reshape(a.shape[:-2] + (a.shape[-2] * 128,))


def gather(results):
    n = len(results)
    y_prompt = np.zeros((n, SEQ, D), np.float32)
    y_sample = np.zeros((n * NS, TS, D), np.float32)
    p_lru_conv = np.zeros((1, n, 3, E), np.float32)
    p_lru_h = np.zeros((1, n, E), np.float32)
    s_lru_conv = np.zeros((1, n * NS, 3, E), np.float32)
    s_lru_h = np.zeros((1, n * NS, E), np.float32)
    p_s5_re = np.zeros((1, n, 128, 64), np.float32)
    p_s5_im = np.zeros((1, n, 128, 64), np.float32)
    s_s5_re = np.zeros((1, n * NS, 128, 64), np.float32)
    s_s5_im = np.zeros((1, n * NS, 128, 64), np.float32)
    p_rwkv_shift = np.zeros((1, n, D), np.float32)
    s_rwkv_shift = np.zeros((1, n * NS, D), np.float32)
    p_rwkv_wkv = np.zeros((1, n, 32, 64, 64), np.float32)
    s_rwkv_wkv = np.zeros((1, n * NS, 32, 64, 64), np.float32)
    p_ret = np.zeros((1, n, 4, 256, 512), np.float32)
    s_ret = np.zeros((1, n * NS, 4, 256, 512), np.float32)
    for c, r in enumerate(results):
        yo = r['yout']
        y_prompt[c] = yo[:, :SEQ].T
        y_sample[c * NS:(c + 1) * NS] = yo[:, SEQ:].T.reshape(NS, TS, D)
        sl = slice(c * NS, (c + 1) * NS)
        p_lru_conv[0, c] = _unct(r['o_p_lru_conv'])
        p_lru_h[0, c] = _unct(r['o_p_lru_h'])
        s_lru_conv[0, sl] = _unct(r['o_s_lru_conv'])
        s_lru_h[0, sl] = _unct(r['o_s_lru_h'])
        if 'o_p_rw_shift' in r:
            p_rwkv_shift[0, c] = _unct(r['o_p_rw_shift'])
            s_rwkv_shift[0, sl] = _unct(r['o_s_rw_shift'])
            p_rwkv_wkv[0, c] = r['o_p_rw_wkv'].reshape(2, 64, 16, 64).transpose(2, 0, 3, 1).reshape(32, 64, 64)
            s_rwkv_wkv[0, sl] = r['o_s_rw_wkv'].reshape(16, 2, 64, NS, 64).transpose(3, 0, 1, 4, 2).reshape(NS, 32, 64, 64)
        if 'o_p_ret' in r:
            p_ret[0, c] = r['o_p_ret']
            s_ret[0, sl] = r['o_s_ret']
        if 'o_p_s5_re' in r:
            p_s5_re[0, c] = r['o_p_s5_re'].T.reshape(128, 64)
            p_s5_im[0, c] = r['o_p_s5_im'].T.reshape(128, 64)
            s_s5_re[0, sl] = r['o_s_s5_re'].transpose(2, 1, 0).reshape(NS, 128, 64)
            s_s5_im[0, sl] = r['o_s_s5_im'].transpose(2, 1, 0).reshape(NS, 128, 64)
    return dict(y_prompt=y_prompt, y_sample=y_sample, p_lru_conv=p_lru_conv, p_lru_h=p_lru_h,
                s_lru_conv=s_lru_conv, s_lru_h=s_lru_h,
                p_s5_re=p_s5_re, p_s5_im=p_s5_im, s_s5_re=s_s5_re, s_s5_im=s_s5_im,
                p_rwkv_shift=p_rwkv_shift, p_rwkv_wkv=p_rwkv_wkv,
                s_rwkv_shift=s_rwkv_shift, s_rwkv_wkv=s_rwkv_wkv, p_ret=p_ret, s_ret=s_ret)


OUT_ORDER = ("y_prompt", "y_sample", "p_lru_conv", "p_lru_h", "p_s5_re", "p_s5_im", "p_rwkv_shift",
             "p_rwkv_wkv", "p_ret", "s_lru_conv", "s_lru_h", "s_s5_re", "s_s5_im", "s_rwkv_shift",
             "s_rwkv_wkv", "s_ret")


def kernel(**inputs):
    inputs = {k_: np.asarray(v) for k_, v in inputs.items()}
    P = build((0, 1, 2, 3))
    in_maps = []
    shared = None
    for c in range(NCORES):
        m = prep_core(inputs, c)
        if shared is None:
            shared = m
        else:
            for k_ in list(m.keys()):
                if not (k_.startswith("st_") or k_ == "hin"):
                    m[k_] = shared[k_]
        in_maps.append(m)
    res = run_bass_kernel_spmd(P.nc, in_maps, core_ids=list(range(NCORES)))
    g = gather(res.results)
    return tuple(g[n] for n in OUT_ORDER)
```

```python
import contextlib
import numpy as np
import concourse.bass as bass
import concourse.mybir as mybir
from concourse.bass_utils import run_bass_kernel_spmd

F32 = mybir.dt.float32
BF16 = mybir.dt.bfloat16
ALU = mybir.AluOpType
AF = mybir.ActivationFunctionType
AX = mybir.AxisListType

NCORES = 8
D = 1024
E = 2048
NMETA = 16
SEQ = 2048
PT = NMETA + SEQ
NS = 16
TS = 8
WS = NS * TS
WALL = PT + WS
WMAX = 768
SEGS = [dict(p0=0, Wp=768, Ws=0), dict(p0=768, Wp=768, Ws=0), dict(p0=1536, Wp=528, Ws=WS)]
PI = float(np.pi)


def ctiles(W):
    res = []
    c = 0
    while c < W:
        n = min(512, W - c)
        res.append((c, n))
        c += n
    return res


class KB:
    EPOCH = 30000
    NEAR = 4

    def __init__(self, nc, es):
        self.nc, self.es = nc, es
        self.engs = dict(pe=nc.tensor, act=nc.scalar, dve=nc.vector, pool=nc.gpsimd, sp=nc.sync)
        self.semobj = {}
        self.cnt = {e: 0 for e in self.engs}
        self.epoch = {e: 0 for e in self.engs}
        self.nsem = 0
        for e in self.engs:
            self.semobj[(e, 0)] = self._newsem()
        self.seen = {e: {} for e in self.engs}
        self.ndma = 24
        self.dma_tgt = [0] * self.ndma
        self.dma_ep = [0] * self.ndma
        for i in range(self.ndma):
            self.semobj[('dma', i, 0)] = self._newsem()
        self.dma_slots = {'pool': (0, 4), 'sp': (4, self.ndma)}
        self.dma_rrq = {'pool': 0, 'sp': 4}
        self.lastw = {}
        self.readers = {}
        self.ninstr = 0

    def _newsem(self):
        self.nsem += 1
        return self.es.enter_context(self.nc.semaphore(f"sem{self.nsem}"))

    def _wait(self, e, sk, c, raw=False):
        if sk[0] == e:
            if not (raw and e != 'pe' and sk[1] == self.epoch[e] and self.cnt[e] - c < self.NEAR):
                return
        if self.seen[e].get(sk, 0) >= c:
            return
        self.engs[e].wait_ge(self.semobj[sk], c)
        self.seen[e][sk] = c

    @staticmethod
    def _key(r):
        sub = None
        if isinstance(r, tuple):
            r, sub = r
        if not isinstance(r, str):
            n = r.name
            r = n() if callable(n) else n
        return r, sub

    def _collect(self, reads, writes):
        toks = []
        for r in reads:
            n, s = self._key(r)
            for s2, t in self.lastw.get(n, {}).items():
                if s is None or s2 is None or s == s2:
                    toks.append((t, True))
        for w in writes:
            n, s = self._key(w)
            for s2, t in self.lastw.get(n, {}).items():
                if s is None or s2 is None or s == s2:
                    toks.append((t, False))
            for s2, d in self.readers.get(n, {}).items():
                if s is None or s2 is None or s == s2:
                    toks.extend((t, False) for t in d.values())
        return toks

    def _record(self, reads, writes, tok, who):
        for r in reads:
            n, s = self._key(r)
            self.readers.setdefault(n, {}).setdefault(s, {})[who] = tok
        for w in writes:
            n, s = self._key(w)
            lw = self.lastw.setdefault(n, {})
            rd = self.readers.setdefault(n, {})
            if s is None:
                lw.clear()
                rd.clear()
            else:
                rd.pop(s, None)
            lw[s] = tok

    def op(self, e, fn, r=(), w=()):
        for ((sk, c), raw) in self._collect(r, w):
            self._wait(e, sk, c, raw)
        ins = fn(self.engs[e])
        if self.cnt[e] >= self.EPOCH:
            self.epoch[e] += 1
            self.cnt[e] = 0
            self.semobj[(e, self.epoch[e])] = self._newsem()
        sk = (e, self.epoch[e])
        self.cnt[e] += 1
        ins.then_inc(self.semobj[sk], 1)
        tok = (sk, self.cnt[e])
        self._record(r, w, tok, e)
        self.ninstr += 1
        return tok

    def dma(self, q, out, in_, r=None, w=None, **kw):
        r = [in_] if r is None else r
        w = [out] if w is None else w
        for ((sk, c), raw) in self._collect(r, w):
            self._wait(q, sk, c)
        lo, hi = self.dma_slots[q]
        i = self.dma_rrq[q]
        self.dma_rrq[q] = lo + (i + 1 - lo) % (hi - lo)
        sk = ('dma', i, self.dma_ep[i])
        if self.dma_tgt[i] > 0:
            self._wait(q, sk, self.dma_tgt[i])
        if self.dma_tgt[i] >= self.EPOCH:
            self.dma_ep[i] += 1
            self.dma_tgt[i] = 0
            sk = ('dma', i, self.dma_ep[i])
            self.semobj[sk] = self._newsem()
        ins = self.engs[q].dma_start(out=out, in_=in_, **kw)
        self.dma_tgt[i] += 16
        ins.then_inc(self.semobj[sk], 16)
        tok = (sk, self.dma_tgt[i])
        self._record(r, w, tok, ('dma', i))
        self.ninstr += 1
        return tok

    def barrier(self):
        for e in self.engs:
            for e2 in self.engs:
                if e2 != e and self.cnt[e2] > 0:
                    self._wait(e, (e2, self.epoch[e2]), self.cnt[e2])
            for i in range(self.ndma):
                if self.dma_tgt[i] > 0:
                    self._wait(e, ('dma', i, self.dma_ep[i]), self.dma_tgt[i])

    def finish(self):
        e = 'sp'
        for i in range(self.ndma):
            if self.dma_tgt[i] > 0:
                self._wait(e, ('dma', i, self.dma_ep[i]), self.dma_tgt[i])
        for e2 in self.engs:
            if e2 != e and self.cnt[e2] > 0:
                self._wait(e, (e2, self.epoch[e2]), self.cnt[e2])

    def mm(self, ps, lhsT, rhs, start=True, stop=True, r=None, w=None):
        return self.op('pe', lambda e: e.matmul(ps, lhsT, rhs, start=start, stop=stop),
                       r=r if r is not None else [lhsT, rhs], w=w if w is not None else [ps])

    def tr(self, ps, in_, ident, r=None, w=None):
        return self.op('pe', lambda e: e.transpose(ps, in_, ident),
                       r=r if r is not None else [in_, ident], w=w if w is not None else [ps])

    def act(self, func, out, in_, scale=1.0, bias=None, accum_out=None, r=None, w=None, eng='act'):
        rr = [in_]
        kw = {}
        if bias is not None:
            kw['bias'] = bias
            if not isinstance(bias, (int, float)):
                rr.append(bias)
        if not isinstance(scale, (int, float)):
            rr.append(scale)
        ww = [out]
        if accum_out is not None:
            kw['accum_out'] = accum_out
            ww.append(accum_out)
        return self.op(eng, lambda e: e.activation(out=out, in_=in_, func=func, scale=scale, **kw),
                       r=r if r is not None else rr, w=w if w is not None else ww)

    def tt(self, out, in0, in1, op, r=None, w=None, eng='dve'):
        return self.op(eng, lambda e: e.tensor_tensor(out=out, in0=in0, in1=in1, op=op),
                       r=r if r is not None else [in0, in1], w=w if w is not None else [out])

    def ts(self, out, in0, s1, op0, s2=None, op1=None, r=None, w=None, eng='dve'):
        rr = [in0] + [s for s in (s1, s2) if s is not None and not isinstance(s, (int, float))]
        if s2 is None:
            f = lambda e: e.tensor_scalar(out=out, in0=in0, scalar1=s1, scalar2=None, op0=op0)
        else:
            f = lambda e: e.tensor_scalar(out=out, in0=in0, scalar1=s1, scalar2=s2, op0=op0, op1=op1)
        return self.op(eng, f, r=r if r is not None else rr, w=w if w is not None else [out])

    def stt(self, out, in0, scalar, in1, op0, op1, r=None, w=None, eng='dve'):
        rr = [in0, in1] + ([scalar] if not isinstance(scalar, (int, float)) else [])
        return self.op(eng, lambda e: e.scalar_tensor_tensor(out=out, in0=in0, scalar=scalar, in1=in1,
                                                             op0=op0, op1=op1),
                       r=r if r is not None else rr, w=w if w is not None else [out])

    def scan(self, out, d0, d1, init, op0=None, op1=None, r=None, w=None):
        op0 = op0 or ALU.mult
        op1 = op1 or ALU.add
        rr = [d0, d1] + ([init] if not isinstance(init, (int, float)) else [])
        return self.op('dve', lambda e: e.tensor_tensor_scan(out=out, data0=d0, data1=d1, initial=init,
                                                             op0=op0, op1=op1),
                       r=r if r is not None else rr, w=w if w is not None else [out])

    def copy(self, out, in_, eng='dve', r=None, w=None):
        if eng == 'act':
            return self.act(AF.Identity, out, in_, r=r, w=w)
        return self.op(eng, lambda e: e.tensor_copy(out=out, in_=in_),
                       r=r if r is not None else [in_], w=w if w is not None else [out])

    def memset(self, out, val, eng='dve', w=None):
        return self.op(eng, lambda e: e.memset(out, val), r=[], w=w if w is not None else [out])

    def recip(self, out, in_, r=None, w=None):
        return self.op('dve', lambda e: e.reciprocal(out=out, in_=in_),
                       r=r if r is not None else [in_], w=w if w is not None else [out])


class Prog:
    pass


def build(order=(0, 1, 2, 3)):
    nc = bass.Bass("TRN2", target_bir_lowering=False)
    es = contextlib.ExitStack()
    P = Prog()
    P.nc = nc
    with es:
        k = KB(nc, es)
        P.k = k

        def din(name, shape):
            return nc.dram_tensor(name, list(shape), F32, kind="ExternalInput").ap()

        def dout(name, shape):
            return nc.dram_tensor(name, list(shape), F32, kind="ExternalOutput").ap()

        P.uid = 0
        P.scope = es

        def sb(name, shape, dt=F32):
            P.uid += 1
            return P.scope.enter_context(nc.sbuf_tensor(f"{name}_u{P.uid}", list(shape), dt))

        @contextlib.contextmanager
        def layer_scope():
            old = P.scope
            with contextlib.ExitStack() as ls:
                P.scope = ls
                yield
                k.barrier()
            P.scope = old

        hin = din("hin", [D, WALL])
        yout = dout("yout", [D, SEQ + WS])
        normpre = din("normpre", [128, 4, 8])
        normpost = din("normpost", [128, 4, 8])
        ident_d = din("ident", [128, 128])
        lru_w_in = din("lru_w_in", [D, 2 * E])
        lru_cw = din("lru_cw", [128, 16, 4])
        lru_cb = din("lru_cb", [128, 16])
        lru_wa = din("lru_wa", [16, 128, 128])
        lru_wx = din("lru_wx", [16, 128, 128])
        lru_ba = din("lru_ba", [128, 16])
        lru_bx = din("lru_bx", [128, 16])
        lru_lam = din("lru_lam", [128, 16])
        lru_w_out = din("lru_w_out", [E, D])
        st_lru_conv = din("st_lru_conv", [128, 16, NS, 3])
        st_lru_h = din("st_lru_h", [128, 16, NS])
        o_p_lru_conv = dout("o_p_lru_conv", [128, 16, 3])
        o_p_lru_h = dout("o_p_lru_h", [128, 16])
        o_s_lru_conv = dout("o_s_lru_conv", [128, 16, NS, 3])
        o_s_lru_h = dout("o_s_lru_h", [128, 16, NS])

        s5_w_in = din("s5_w_in", [D, 2 * E])
        s5_are = din("s5_are", [128, 64])
        s5_aim = din("s5_aim", [128, 64])
        s5_ldt = din("s5_ldt", [128, 64])
        s5_bre = din("s5_bre", [16, 128, 4, 128])
        s5_bim = din("s5_bim", [16, 128, 4, 128])
        s5_cre = din("s5_cre", [16, 128, 4, 128])
        s5_cim = din("s5_cim", [16, 128, 4, 128])
        s5_d = din("s5_d", [128, 16])
        s5_glu_w = din("s5_glu_w", [E, E])
        s5_glu_b = din("s5_glu_b", [128, 16])
        s5_w_out = din("s5_w_out", [E, D])
        s5_iota = din("s5_iota", [128, 256])
        st_s5_re = din("st_s5_re", [128, 64, NS])
        st_s5_im = din("st_s5_im", [128, 64, NS])
        o_p_s5_re = dout("o_p_s5_re", [128, 64])
        o_p_s5_im = dout("o_p_s5_im", [128, 64])
        o_s_s5_re = dout("o_s_s5_re", [128, 64, NS])
        o_s_s5_im = dout("o_s_s5_im", [128, 64, NS])

        rw_mu = din("rw_mu", [128, 6, 8])
        rw_w_r = din("rw_w_r", [D, E])
        rw_w_k = din("rw_w_k", [D, E])
        rw_w_v = din("rw_w_v", [D, E])
        rw_w_g = din("rw_w_g", [D, E])
        rw_w_o = din("rw_w_o", [E, D])
        rw_w1 = din("rw_w1", [D, 64])
        rw_w2 = din("rw_w2", [64, E])
        rw_a1 = din("rw_a1", [D, 64])
        rw_a2 = din("rw_a2", [64, E])
        rw_vecs = din("rw_vecs", [128, 5, 16])
        rw_lnw = din("rw_lnw", [128, E])
        rw_lnb = din("rw_lnb", [128, E])
        rw_masks = din("rw_masks", [128, 2, 3, 128])
        rw_reset = din("rw_reset", [128, 128])
        rw_sel = din("rw_sel", [128, 2])
        rw_oh = din("rw_oh", [128, NS])
        st_rw_shift = din("st_rw_shift", [128, 8, NS])
        st_rw_wkv = din("st_rw_wkv", [16, 128, NS, 64])
        o_p_rw_shift = dout("o_p_rw_shift", [128, 8])
        o_s_rw_shift = dout("o_s_rw_shift", [128, 8, NS])
        o_p_rw_wkv = dout("o_p_rw_wkv", [128, 16, 64])
        o_s_rw_wkv = dout("o_s_rw_wkv", [16, 128, NS, 64])
        ret_w_q = din("ret_w_q", [D, D])
        ret_w_k = din("ret_w_k", [D, D])
        ret_w_v = din("ret_w_v", [D, E])
        ret_w_g = din("ret_w_g", [D, E])
        ret_w_o = din("ret_w_o", [E, D])
        ret_cos = din("ret_cos", [128, WALL])
        ret_sin = din("ret_sin", [128, WALL])
        ret_mk = din("ret_mk", [128, 2, 4, 128])
        ret_grow = din("ret_grow", [128, 2, 4, 128])
        ret_gcol = din("ret_gcol", [128, 3, 4])
        ret_sc = din("ret_sc", [128, NS, 4])
        ret_cmask = din("ret_cmask", [128, NS, 128])
        st_ret = din("st_ret", [NS, 4, 256, 512])
        o_p_ret = dout("o_p_ret", [4, 256, 512])
        o_s_ret = dout("o_s_ret", [NS, 4, 256, 512])

        H = sb("H", [128, 8, WMAX])
        XN = sb("XN", [128, 8, WMAX], BF16)
        BIGA = sb("BIGA", [128, 16, WMAX], BF16)
        HB = H[:, :, :].bitcast(BF16).rearrange("p a (b c) -> p (a b) c", c=WMAX)
        hspill = nc.dram_tensor("hspill", [128, 8, WMAX], F32, kind="Internal").ap()
        RS = sb("RS", [128, WMAX])
        ONES = sb("ONES", [128, 128], BF16)
        CONST = sb("CONST", [128, 8])
        NPRE = sb("NPRE", [128, 4, 8])
        NPOST = sb("NPOST", [128, 4, 8])
        WI = [sb(f"WI{i}", [128, 2, 8, 128], BF16) for i in range(2)]

        def mk_tf(n=9):
            return [[sb(f"TF{p}_{i}", [128, WMAX + 16]) for i in range(n)] for p in range(2)]

        def mk_tb():
            return [sb(f"TB{p}", [128, WMAX], BF16) for p in range(2)]
        PS = [es.enter_context(nc.psum_tensor(f"PS{i}", [128, 512], F32)) for i in range(8)]

        EPS_AP = CONST[:, 0:1]
        ONE_AP = CONST[:, 1:2]
        k.memset(CONST[:, 0:1], 1e-6)
        k.memset(CONST[:, 1:2], 1.0)
        k.memset(CONST[:, 2:3], -PI)
        NPI_AP = CONST[:, 2:3]
        k.memset(ONES[:, :], 1.0)
        k.dma('sp', NPRE[:, :, :], normpre[:, :, :])
        k.dma('sp', NPOST[:, :, :], normpost[:, :, :])

        hin_v = hin.rearrange("(kt p) c -> p kt c", p=128)
        yout_v = yout.rearrange("(kt p) c -> p kt c", p=128)

        def pre_norm(li, seg):
            W = seg['W']
            cts_ = ctiles(W)
            for ti, (c0, n) in enumerate(cts_):
                for kt in range(8):
                    k.act(AF.Square, XN[:, kt, c0:c0 + n], H[:, kt, c0:c0 + n])
            for ti, (c0, n) in enumerate(cts_):
                ps = PS[6 + ti % 2]
                for kt in range(8):
                    k.mm(ps[:, :n], ONES[:, :], XN[:, kt, c0:c0 + n], start=(kt == 0), stop=(kt == 7))
            for ti, (c0, n) in enumerate(cts_):
                ps = PS[6 + ti % 2]
                k.act(AF.Sqrt, RS[:, c0:c0 + n], ps[:, :n], scale=1.0 / D, bias=EPS_AP)
                k.recip(RS[:, c0:c0 + n], RS[:, c0:c0 + n])
            for ti, (c0, n) in enumerate(cts_):
                for kt in range(8):
                    k.stt(XN[:, kt, c0:c0 + n], H[:, kt, c0:c0 + n], NPRE[:, li, kt:kt + 1],
                          RS[:, c0:c0 + n], ALU.mult, ALU.mult)

        def out_proj(li, seg, Y, w_out, ZTd):
            W = seg['W']
            wv = w_out.rearrange("(kt p) n -> p kt n", p=128)
            cts = ctiles(W)
            ZSQ = [sb(f"ZSQ{i}", [128, 512], BF16) for i in range(2)]
            WO = [sb(f"WO{i}", [128, 16, 128], BF16) for i in range(2)]
            pending = None
            for d in range(8):
                wo = WO[d % 2]
                k.dma('pool', wo[:, :, :], wv[:, :, d * 128:(d + 1) * 128])
                for ti, (c0, n) in enumerate(cts):
                    ps = PS[(2 * d + ti) % 6]
                    for kt in range(16):
                        k.mm(ps[:, :n], wo[:, kt, :], Y[:, kt, c0:c0 + n], start=(kt == 0), stop=(kt == 15))
                    if pending is not None:
                        pending()
                    zs = ZSQ[(2 * d + ti) % 2]
                    k.act(AF.Identity, ZTd[d][:, c0:c0 + n], ps[:, :n])
                    k.act(AF.Square, zs[:, :n], ps[:, :n])
                    pending = (lambda ti=ti, n=n, zs=zs, d=d:
                               k.mm(PS[6 + ti][:, :n], ONES[:, :], zs[:, :n], start=(d == 0), stop=(d == 7)))
            pending()
            k.dma('sp', H[:, :, 0:W], hspill[:, :, 0:W])
            for ti, (c0, n) in enumerate(cts):
                k.act(AF.Sqrt, RS[:, c0:c0 + n], PS[6 + ti][:, :n], scale=1.0 / D, bias=EPS_AP)
                k.recip(RS[:, c0:c0 + n], RS[:, c0:c0 + n])
                for d in range(8):
                    k.stt(ZTd[d][:, c0:c0 + n], ZTd[d][:, c0:c0 + n], NPOST[:, li, d:d + 1],
                          RS[:, c0:c0 + n], ALU.mult, ALU.mult)
                    k.tt(H[:, d, c0:c0 + n], H[:, d, c0:c0 + n], ZTd[d][:, c0:c0 + n], ALU.add)

        LRUCONV = sb("LRUCONV", [128, 16, 3])
        LRUH = sb("LRUH", [128, 16])
        LCW = sb("LCW", [128, 16, 4])
        LCB = sb("LCB", [128, 16])
        LBA = sb("LBA", [128, 16])
        LBX = sb("LBX", [128, 16])
        LSP = sb("LSP", [128, 16])
        LSP8 = sb("LSP8", [128, 16])
        LSP16 = sb("LSP16", [128, 16])
        TMP16 = sb("TMP16", [128, NS])
        TMP16B = sb("TMP16B", [128, NS])

        def lru_setup():
            k.dma('sp', LCW[:, :, :], lru_cw[:, :, :])
            k.dma('sp', LCB[:, :], lru_cb[:, :])
            k.dma('sp', LBA[:, :], lru_ba[:, :])
            k.dma('sp', LBX[:, :], lru_bx[:, :])
            k.dma('sp', LSP[:, :], lru_lam[:, :])
            k.act(AF.Exp, LSP[:, :], LSP[:, :], scale=-1.0)
            k.act(AF.Ln, LSP[:, :], LSP[:, :], bias=ONE_AP)
            k.ts(LSP8[:, :], LSP[:, :], -8.0, ALU.mult)
            k.ts(LSP16[:, :], LSP[:, :], -16.0, ALU.mult)

        def lru_layer(li, seg, si):
            W, Wp, Ws = seg['W'], seg['Wp'], seg['Ws']
            cts = ctiles(W)
            wv = lru_w_in.rearrange("(kt p) n -> p kt n", p=128)
            Y = BIGA
            TF = mk_tf(9)
            TBh = mk_tb()
            UES = [sb(f"UES{i}", [128, NS, 3 + TS]) for i in range(2)]
            WAB = [sb(f"WAB{i}", [128, 2, 128], BF16) for i in range(2)]
            if Ws:
                STC = sb("STC", [128, 16, NS, 3])
                STH = sb("STH", [128, 16, NS])
                OSC = sb("OSC", [128, 16, NS, 3])
                OSH = sb("OSH", [128, 16, NS])
                k.dma('sp', STC[:, :, :, :], st_lru_conv[:, :, :, :])
                k.dma('sp', STH[:, :, :], st_lru_h[:, :, :])
            def lru_tile(j):
                par = j % 2
                UE, XC, GR, GI, A, S, BX, HS, SG = TF[par]
                XCB = TBh[par]
                ues = UES[par]
                wi = WI[par]
                wab = WAB[par]
                k.dma('pool', wi[:, 0, :, :], wv[:, :, j * 128:(j + 1) * 128])
                k.dma('pool', wi[:, 1, :, :], wv[:, :, E + j * 128:E + (j + 1) * 128])
                k.dma('pool', wab[:, 0, :], lru_wa[j, :, :])
                k.dma('pool', wab[:, 1, :], lru_wx[j, :, :])
                if si == 0:
                    k.memset(UE[:, 0:3], 0.0)
                else:
                    k.copy(UE[:, 0:3], LRUCONV[:, j, :])
                if Ws:
                    k.copy(ues[:, :, 0:3], STC[:, j, :, :])
                for ti, (c0, n) in enumerate(cts):
                    for kt in range(8):
                        k.mm(PS[ti][:, :n], wi[:, 0, kt, :], XN[:, kt, c0:c0 + n], start=(kt == 0), stop=(kt == 7))
                    for kt in range(8):
                        k.mm(PS[2 + ti][:, :n], wi[:, 1, kt, :], XN[:, kt, c0:c0 + n], start=(kt == 0), stop=(kt == 7))
                    npr = min(c0 + n, Wp) - c0
                    if npr > 0:
                        k.act(AF.Identity, UE[:, 3 + c0:3 + c0 + npr], PS[ti][:, :npr])
                    if c0 + n > Wp:
                        s0 = max(c0, Wp)
                        ns_ = c0 + n - s0
                        q0 = (s0 - Wp) // TS
                        k.act(AF.Identity, ues[:, q0:q0 + ns_ // TS, 3:3 + TS],
                              PS[ti][:, s0 - c0:s0 - c0 + ns_].rearrange("p (s t) -> p s t", t=TS))
                    k.act(AF.Silu, SG[:, c0:c0 + n], PS[2 + ti][:, :n])
                    yield
                k.ts(XC[:, 0:Wp], UE[:, 3:3 + Wp], LCW[:, j, 3:4], ALU.mult, LCB[:, j:j + 1], ALU.add)
                for c in (2, 1, 0):
                    k.stt(XC[:, 0:Wp], UE[:, c:c + Wp], LCW[:, j, c:c + 1], XC[:, 0:Wp], ALU.mult, ALU.add)
                if Ws:
                    XC3 = XC[:, Wp:W].rearrange("p (s t) -> p s t", t=TS)
                    k.ts(XC3, ues[:, :, 3:3 + TS], LCW[:, j, 3:4], ALU.mult, LCB[:, j:j + 1], ALU.add)
                    for c in (2, 1, 0):
                        k.stt(XC3, ues[:, :, c:c + TS], LCW[:, j, c:c + 1], XC3, ALU.mult, ALU.add)
                yield
                k.act(AF.Identity, XCB[:, 0:W], XC[:, 0:W])
                yield
                for ti, (c0, n) in enumerate(cts):
                    k.mm(PS[4 + ti][:, :n], wab[:, 0, :], XCB[:, c0:c0 + n])
                    k.mm(PS[6 + ti][:, :n], wab[:, 1, :], XCB[:, c0:c0 + n])
                    k.act(AF.Sigmoid, GR[:, c0:c0 + n], PS[4 + ti][:, :n], bias=LBA[:, j:j + 1])
                    k.act(AF.Sigmoid, GI[:, c0:c0 + n], PS[6 + ti][:, :n], bias=LBX[:, j:j + 1])
                yield
                k.act(AF.Exp, A[:, 0:W], GR[:, 0:W], scale=LSP8[:, j:j + 1])
                k.act(AF.Exp, S[:, 0:W], GR[:, 0:W], scale=LSP16[:, j:j + 1])
                k.act(AF.Sqrt, S[:, 0:W], S[:, 0:W], scale=-1.0, bias=ONE_AP)
                yield
                k.tt(BX[:, 0:W], S[:, 0:W], GI[:, 0:W], ALU.mult)
                k.tt(BX[:, 0:W], BX[:, 0:W], XC[:, 0:W], ALU.mult)
                k.scan(HS[:, 0:Wp], A[:, 0:Wp], BX[:, 0:Wp], 0.0 if si == 0 else LRUH[:, j:j + 1])
                if Ws:
                    A3 = A[:, Wp:W].rearrange("p (s t) -> p s t", t=TS)
                    BX3 = BX[:, Wp:W].rearrange("p (s t) -> p s t", t=TS)
                    HS3 = HS[:, Wp:W].rearrange("p (s t) -> p s t", t=TS)
                    for q in range(NS):
                        c1 = Wp + q * TS
                        k.scan(HS[:, c1:c1 + TS], A[:, c1:c1 + TS], BX[:, c1:c1 + TS], STH[:, j, q:q + 1])
                yield
                k.tt(Y[:, j, 0:W], HS[:, 0:W], SG[:, 0:W], ALU.mult)
                k.copy(LRUCONV[:, j, :], UE[:, Wp:Wp + 3])
                k.copy(LRUH[:, j:j + 1], HS[:, Wp - 1:Wp])
                if Ws:
                    k.copy(OSC[:, j, :, :], ues[:, :, TS:TS + 3])
                    k.copy(OSH[:, j, :], HS3[:, :, TS - 1])
                yield

            for jp in range(8):
                gens = [lru_tile(2 * jp), lru_tile(2 * jp + 1)]
                alive = [True, True]
                while any(alive):
                    for gi in range(2):
                        if alive[gi] and next(gens[gi], 'end') == 'end':
                            alive[gi] = False
            out_proj(li, seg, Y, lru_w_out, TF[0][0:8])
            if si == len(SEGS) - 1:
                k.dma('sp', o_p_lru_conv[:, :, :], LRUCONV[:, :, :])
                k.dma('sp', o_p_lru_h[:, :], LRUH[:, :])
                k.dma('sp', o_s_lru_conv[:, :, :, :], OSC[:, :, :, :])
                k.dma('sp', o_s_lru_h[:, :, :], OSH[:, :, :])


        def sm(name, shape=(128, 64), dt=F32):
            return sb(name, list(shape), dt)
        S5ARE, S5AIM, S5DT, S5MAG, S5TH = (sm(n) for n in ("S5ARE", "S5AIM", "S5DT", "S5MAG", "S5TH"))
        S5FRE, S5FIM, S5NFRE, S5NFIM, S5FIR, S5FII = (sm(n) for n in ("S5FRE", "S5FIM", "S5NFRE", "S5NFIM", "S5FIR", "S5FII"))
        S5T = [sm(f"S5T{i}") for i in range(6)]
        S5R, S5I, OP5R, OP5I = (sm(n) for n in ("S5R", "S5I", "OP5R", "OP5I"))
        S5TI = sm("S5TI", (128, 64), mybir.dt.int32)
        IOTA = sm("IOTA", (128, 256))
        S5D = sm("S5D", (128, 16))
        S5GB = sm("S5GB", (128, 16))
        TWO_PI = 2.0 * PI

        I32 = mybir.dt.int32

        def sincos(outs, outc, ang, tmp, tmpi):
            k.ts(tmp, ang, 1.0 / TWO_PI, ALU.mult)
            k.copy(tmpi, tmp)
            k.copy(tmp, tmpi)
            k.stt(tmp, tmp, -TWO_PI, ang, ALU.mult, ALU.add)
            k.ts(outc, tmp, 0.5 * PI, ALU.add)
            k.ts(outs, tmp, PI, ALU.is_gt, -TWO_PI, ALU.mult)
            k.tt(tmp, tmp, outs, ALU.add)
            k.act(AF.Sin, outs, tmp)
            k.ts(tmp, outc, PI, ALU.is_gt, -TWO_PI, ALU.mult)
            k.tt(tmp, tmp, outc, ALU.add)
            k.act(AF.Sin, outc, tmp)

        RUN = 256
        s5tab = nc.dram_tensor("s5tab", [64, 128, 2, RUN], F32, kind="Internal").ap()
        s5cp = nc.dram_tensor("s5cp", [16, 3, 128, 4, 128], BF16, kind="Internal").ap()
        CLAST = sm("CLAST", (128, 64, 3, 3))

        def s5_setup():
            k.dma('sp', S5ARE[:, :], s5_are[:, :])
            k.dma('sp', S5AIM[:, :], s5_aim[:, :])
            k.dma('sp', S5DT[:, :], s5_ldt[:, :])
            k.dma('sp', IOTA[:, :], s5_iota[:, :])
            k.dma('sp', S5D[:, :], s5_d[:, :])
            k.dma('sp', S5GB[:, :], s5_glu_b[:, :])
            t0, t1, t2, t3, t4, t5 = S5T
            k.act(AF.Exp, S5DT[:, :], S5DT[:, :])
            k.tt(t0[:, :], S5DT[:, :], S5ARE[:, :], ALU.mult)
            k.act(AF.Exp, S5MAG[:, :], t0[:, :])
            k.tt(S5TH[:, :], S5DT[:, :], S5AIM[:, :], ALU.mult)
            sincos(t1[:, :], t2[:, :], S5TH[:, :], t0[:, :], S5TI[:, :])
            k.tt(t3[:, :], S5MAG[:, :], t2[:, :], ALU.mult)
            k.tt(t4[:, :], S5MAG[:, :], t1[:, :], ALU.mult)
            k.ts(t3[:, :], t3[:, :], -1.0, ALU.add)
            k.tt(t0[:, :], S5ARE[:, :], S5ARE[:, :], ALU.mult)
            k.tt(t1[:, :], S5AIM[:, :], S5AIM[:, :], ALU.mult)
            k.tt(t0[:, :], t0[:, :], t1[:, :], ALU.add)
            k.recip(t0[:, :], t0[:, :])
            k.tt(t1[:, :], t3[:, :], S5ARE[:, :], ALU.mult)
            k.tt(t2[:, :], t4[:, :], S5AIM[:, :], ALU.mult)
            k.tt(t1[:, :], t1[:, :], t2[:, :], ALU.add)
            k.tt(S5FRE[:, :], t1[:, :], t0[:, :], ALU.mult)
            k.tt(t1[:, :], t4[:, :], S5ARE[:, :], ALU.mult)
            k.tt(t2[:, :], t3[:, :], S5AIM[:, :], ALU.mult)
            k.tt(t1[:, :], t1[:, :], t2[:, :], ALU.subtract)
            k.tt(S5FIM[:, :], t1[:, :], t0[:, :], ALU.mult)
            k.ts(S5NFRE[:, :], S5FRE[:, :], -1.0, ALU.mult)
            k.ts(S5NFIM[:, :], S5FIM[:, :], -1.0, ALU.mult)
            k.tt(t0[:, :], S5FRE[:, :], S5FRE[:, :], ALU.mult)
            k.tt(t1[:, :], S5FIM[:, :], S5FIM[:, :], ALU.mult)
            k.tt(t0[:, :], t0[:, :], t1[:, :], ALU.add)
            k.recip(t0[:, :], t0[:, :])
            k.tt(S5FIR[:, :], S5FRE[:, :], t0[:, :], ALU.mult)
            k.tt(S5FII[:, :], S5NFIM[:, :], t0[:, :], ALU.mult)
            with layer_scope():
                TT_ = [sm(f"TTB{i}", (128, 2, RUN)) for i in range(2)]
                TG_ = [sm(f"TTG{i}", (128, RUN)) for i in range(2)]
                TM_ = [sm(f"TTM{i}", (128, RUN)) for i in range(2)]
                TI_ = [sm(f"TTI{i}", (128, RUN), mybir.dt.int32) for i in range(2)]
                for sg in range(64):
                    p_ = sg % 2
                    k.ts(TG_[p_][:, :], IOTA[:, :], S5TH[:, sg:sg + 1], ALU.mult)
                    sincos(TT_[p_][:, 1, :], TT_[p_][:, 0, :], TG_[p_][:, :], TM_[p_][:, :], TI_[p_][:, :])
                    k.dma('sp', s5tab[sg, :, :, :], TT_[p_][:, :, :])
                    for vi, col in enumerate((RUN - 1, 15, TS - 1)):
                        k.copy(CLAST[:, sg, vi, 0:2], TT_[p_][:, :, col])
                        k.ts(CLAST[:, sg, vi, 2:3], TT_[p_][:, 1, col:col + 1], -1.0, ALU.mult)

        def s5_layer(li, seg, si):
            W, Wp, Ws = seg['W'], seg['Wp'], seg['Ws']
            cts = ctiles(W)
            wv = s5_w_in.rearrange("(kt p) n -> p kt n", p=128)
            BIGB = HB
            FB = [sb(f"FB{i}", [128, WMAX + 16]) for i in range(8)]
            WRs = [[FB[0], FB[1]], [FB[2], FB[3]]]
            U32s = [FB[4], FB[5]]
            G1, Q1 = FB[6], FB[7]
            UBs = [sm(f"UB{i}", (128, WMAX), BF16) for i in range(2)]
            PB = [sm(f"PB{i}", (128, 2, WMAX), BF16) for i in range(2)]
            VB = [sm(f"VB{i}", (128, 2, WMAX), BF16) for i in range(2)]
            WB = [sm(f"WB{i}", (128, 2, WMAX), BF16) for i in range(2)]
            TP = [[sm(f"TP{i}_{q}", (128, WMAX), BF16) for q in range(4)] for i in range(2)]
            NCPR = [sm(f"NCPR{i}", (128, 4, 128), BF16) for i in range(2)]
            TT1 = [sm(f"TT1{i}", (128, WMAX), BF16) for i in range(2)]
            TT2 = [sm(f"TT2{i}", (128, WMAX), BF16) for i in range(2)]
            TCB = [sm(f"TCB{i}", (128, 2, RUN), BF16) for i in range(2)]
            BBR = [sm(f"BBR{i}", (128, 4, 128), BF16) for i in range(2)]
            BBI = [sm(f"BBI{i}", (128, 4, 128), BF16) for i in range(2)]
            CCR = [sm(f"CCR{i}", (128, 4, 128)) for i in range(2)]
            CCI = [sm(f"CCI{i}", (128, 4, 128)) for i in range(2)]
            CPR = [sm(f"CPR{i}", (128, 4, 128), BF16) for i in range(2)]
            CPI = [sm(f"CPI{i}", (128, 4, 128), BF16) for i in range(2)]
            CT1 = sm("CT1", (128, 128))
            XLS = [sm(f"XLS{i}", (128, 8, 4)) for i in range(2)]
            if Ws:
                X0R = sm("X0R", (128, 64, NS))
                X0I = sm("X0I", (128, 64, NS))
                OS5R = sm("OS5R", (128, 64, NS))
                OS5I = sm("OS5I", (128, 64, NS))
                X7 = [sm(f"X7{i}", (128, 2, NS)) for i in range(2)]
                k.dma('sp', X0R[:, :, :], st_s5_re[:, :, :])
                k.dma('sp', X0I[:, :, :], st_s5_im[:, :, :])
                fir = S5FIR[:, :].unsqueeze(2).to_broadcast([128, 64, NS])
                fii = S5FII[:, :].unsqueeze(2).to_broadcast([128, 64, NS])
                k.tt(OS5I[:, :, :], X0R[:, :, :], fii, ALU.mult)
                k.tt(OS5R[:, :, :], X0I[:, :, :], fii, ALU.mult)
                k.tt(X0R[:, :, :], X0R[:, :, :], fir, ALU.mult)
                k.tt(X0R[:, :, :], X0R[:, :, :], OS5R[:, :, :], ALU.subtract)
                k.tt(X0I[:, :, :], X0I[:, :, :], fir, ALU.mult)
                k.tt(X0I[:, :, :], X0I[:, :, :], OS5I[:, :, :], ALU.add)

            groups = []
            nfull = Wp // RUN
            if nfull:
                groups.append((0, nfull, RUN))
            if Wp - nfull * RUN > 0:
                groups.append((nfull * RUN, 1, Wp - nfull * RUN))
            if Ws:
                groups.append((Wp, Ws // TS, TS))
            runs = []
            c = 0
            while c < Wp:
                L = min(RUN, Wp - c)
                runs.append((c, L))
                c += L

            def tile_of(c):
                for ti, (c0, n) in enumerate(cts):
                    if c0 <= c < c0 + n:
                        return ti, c0
                raise ValueError

            def rot(dst, src, tcb, sign):
                for (g0, nr, rl) in groups:
                    sh = [128, nr, rl]
                    cb = tcb[:, 0, 0:rl].unsqueeze(1).to_broadcast(sh)
                    sb_ = tcb[:, 1, 0:rl].unsqueeze(1).to_broadcast(sh)
                    v3 = lambda t: t[:, g0:g0 + nr * rl].rearrange("p (r l) -> p r l", l=rl)
                    sr, si_ = v3(src[:, 0, :]), v3(src[:, 1, :])
                    t1, t2 = v3(rot.t1), v3(rot.t2)
                    k.tt(t1, sr, cb, ALU.mult)
                    k.tt(t2, si_, sb_, ALU.mult)
                    k.tt(v3(dst[:, 0, :]), t1, t2, ALU.add if sign < 0 else ALU.subtract)
                    k.tt(t1, si_, cb, ALU.mult)
                    k.tt(t2, sr, sb_, ALU.mult)
                    k.tt(v3(dst[:, 1, :]), t1, t2, ALU.subtract if sign < 0 else ALU.add)

            for j in range(16):
                par = j % 2
                U32 = U32s[par]
                UB = UBs[par]
                wi = WI[par]
                k.dma('pool', wi[:, 0, :, :], wv[:, :, j * 128:(j + 1) * 128])
                k.dma('pool', BBR[par][:, :, :], s5_bre[j, :, :, :])
                k.dma('pool', BBI[par][:, :, :], s5_bim[j, :, :, :])
                if si == 0:
                    k.dma('sp', CCR[par][:, :, :], s5_cre[j, :, :, :])
                    k.dma('sp', CCI[par][:, :, :], s5_cim[j, :, :, :])
                    for s_ in range(4):
                        sg = 4 * j + s_
                        k.ts(CT1[:, :], CCR[par][:, s_, :], S5FRE[:, sg:sg + 1], ALU.mult)
                        k.stt(CPR[par][:, s_, :], CCI[par][:, s_, :], S5NFIM[:, sg:sg + 1], CT1[:, :], ALU.mult, ALU.add)
                        k.ts(NCPR[par][:, s_, :], CPR[par][:, s_, :], -1.0, ALU.mult)
                        k.ts(CT1[:, :], CCR[par][:, s_, :], S5NFIM[:, sg:sg + 1], ALU.mult)
                        k.stt(CPI[par][:, s_, :], CCI[par][:, s_, :], S5NFRE[:, sg:sg + 1], CT1[:, :], ALU.mult, ALU.add)
                    k.dma('sp', s5cp[j, 0, :, :, :], CPR[par][:, :, :])
                    k.dma('sp', s5cp[j, 1, :, :, :], NCPR[par][:, :, :])
                    k.dma('sp', s5cp[j, 2, :, :, :], CPI[par][:, :, :])
                else:
                    k.dma('sp', CPR[par][:, :, :], s5cp[j, 0, :, :, :])
                    k.dma('sp', NCPR[par][:, :, :], s5cp[j, 1, :, :, :])
                    k.dma('sp', CPI[par][:, :, :], s5cp[j, 2, :, :, :])
                for ti, (c0, n) in enumerate(cts):
                    for kt in range(8):
                        k.mm(PS[ti][:, :n], wi[:, 0, kt, :], XN[:, kt, c0:c0 + n], start=(kt == 0), stop=(kt == 7))
                    k.act(AF.Identity, U32[:, c0:c0 + n], PS[ti][:, :n])
                    k.act(AF.Identity, UB[:, c0:c0 + n], PS[ti][:, :n])
                for pair in range(2):
                    tiles = [(2 * pair + q, q) for q in range(2)]
                    for s_, sl_ in tiles:
                        sg = 4 * j + s_
                        k.dma('pool', TCB[sl_][:, :, :], s5tab[sg, :, :, :])
                        for ti, (c0, n) in enumerate(cts):
                            k.mm(PS[2 + ti][:, :n], BBR[par][:, s_, :], UB[:, c0:c0 + n])
                            k.mm(PS[4 + ti][:, :n], BBI[par][:, s_, :], UB[:, c0:c0 + n])
                            k.act(AF.Identity, PB[sl_][:, 0, c0:c0 + n], PS[2 + ti][:, :n])
                            k.act(AF.Identity, PB[sl_][:, 1, c0:c0 + n], PS[4 + ti][:, :n])
                    for s_, sl_ in tiles:
                        rot.t1, rot.t2 = TT1[sl_], TT2[sl_]
                        rot(VB[sl_], PB[sl_], TCB[sl_], -1)
                    for ri, (c, L) in enumerate(runs):
                        for s_, sl_ in tiles:
                            sg = 4 * j + s_
                            WR, WIm = WRs[sl_]
                            xls = XLS[sl_]
                            magb = S5MAG[:, sg:sg + 1]
                            if ri == 0:
                                ir = 0.0 if si == 0 else S5R[:, sg:sg + 1]
                                ii = 0.0 if si == 0 else S5I[:, sg:sg + 1]
                            else:
                                ir, ii = xls[:, ri - 1, 0:1], xls[:, ri - 1, 1:2]
                            k.scan(WR[:, c:c + L], magb.to_broadcast([128, L]), VB[sl_][:, 0, c:c + L], ir)
                            k.scan(WIm[:, c:c + L], magb.to_broadcast([128, L]), VB[sl_][:, 1, c:c + L], ii)
                        for s_, sl_ in tiles:
                            sg = 4 * j + s_
                            WR, WIm = WRs[sl_]
                            xls = XLS[sl_]
                            last = ri == len(runs) - 1
                            orr = S5R[:, sg:sg + 1] if last else xls[:, ri, 0:1]
                            oii = S5I[:, sg:sg + 1] if last else xls[:, ri, 1:2]
                            wl, wil = WR[:, c + L - 1:c + L], WIm[:, c + L - 1:c + L]
                            vi_ = 0 if L == RUN else 1
                            assert L in (RUN, 16)
                            cl, sl2, nsl = CLAST[:, sg, vi_, 0:1], CLAST[:, sg, vi_, 1:2], CLAST[:, sg, vi_, 2:3]
                            k.act(AF.Identity, xls[:, ri, 2:3], wil, scale=nsl)
                            k.act(AF.Identity, xls[:, ri, 3:4], wl, scale=sl2)
                            k.act(AF.Identity, orr, wl, scale=cl, bias=xls[:, ri, 2:3])
                            k.act(AF.Identity, oii, wil, scale=cl, bias=xls[:, ri, 3:4])
                    if Ws:
                        for s_, sl_ in tiles:
                            sg = 4 * j + s_
                            WR, WIm = WRs[sl_]
                            magb = S5MAG[:, sg:sg + 1]
                            for q in range(NS):
                                c = Wp + q * TS
                                k.scan(WR[:, c:c + TS], magb.to_broadcast([128, TS]), VB[sl_][:, 0, c:c + TS], X0R[:, sg, q:q + 1])
                                k.scan(WIm[:, c:c + TS], magb.to_broadcast([128, TS]), VB[sl_][:, 1, c:c + TS], X0I[:, sg, q:q + 1])
                            wr7 = WR[:, Wp:W].rearrange("p (s t) -> p s t", t=TS)[:, :, TS - 1]
                            wi7 = WIm[:, Wp:W].rearrange("p (s t) -> p s t", t=TS)[:, :, TS - 1]
                            c7, s7 = CLAST[:, sg, 2, 0:1], CLAST[:, sg, 2, 1:2]
                            x7 = X7[sl_]
                            k.ts(TMP16[:, :], wi7, s7, ALU.mult)
                            k.stt(x7[:, 0, :], wr7, c7, TMP16[:, :], ALU.mult, ALU.subtract)
                            k.ts(TMP16B[:, :], wr7, s7, ALU.mult)
                            k.stt(x7[:, 1, :], wi7, c7, TMP16B[:, :], ALU.mult, ALU.add)
                            k.ts(TMP16[:, :], x7[:, 0, :], S5FRE[:, sg:sg + 1], ALU.mult)
                            k.stt(OS5R[:, sg, :], x7[:, 1, :], S5NFIM[:, sg:sg + 1], TMP16[:, :], ALU.mult, ALU.add)
                            k.ts(TMP16B[:, :], x7[:, 1, :], S5FRE[:, sg:sg + 1], ALU.mult)
                            k.stt(OS5I[:, sg, :], x7[:, 0, :], S5FIM[:, sg:sg + 1], TMP16B[:, :], ALU.mult, ALU.add)
                    for s_, sl_ in tiles:
                        WR, WIm = WRs[sl_]
                        k.act(AF.Identity, WB[sl_][:, 0, 0:W], WR[:, 0:W])
                        k.act(AF.Identity, WB[sl_][:, 1, 0:W], WIm[:, 0:W])
                    for s_, sl_ in tiles:
                        t1, t2, t3, t4 = TP[sl_]
                        for (g0, nr, rl) in groups:
                            sh = [128, nr, rl]
                            cb = TCB[sl_][:, 0, 0:rl].unsqueeze(1).to_broadcast(sh)
                            sb_ = TCB[sl_][:, 1, 0:rl].unsqueeze(1).to_broadcast(sh)
                            v3 = lambda t: t[:, g0:g0 + nr * rl].rearrange("p (r l) -> p r l", l=rl)
                            wr_, wi_ = v3(WB[sl_][:, 0, :]), v3(WB[sl_][:, 1, :])
                            k.tt(v3(t1), wr_, cb, ALU.mult)
                            k.tt(v3(t2), wi_, sb_, ALU.mult)
                            k.tt(v3(t3), wi_, cb, ALU.mult)
                            k.tt(v3(t4), wr_, sb_, ALU.mult)
                    for s_, sl_ in tiles:
                        t1, t2, t3, t4 = TP[sl_]
                        for ti, (c0, n) in enumerate(cts):
                            k.mm(PS[6 + ti][:, :n], CPR[par][:, s_, :], t1[:, c0:c0 + n], start=(s_ == 0), stop=False)
                            k.mm(PS[6 + ti][:, :n], NCPR[par][:, s_, :], t2[:, c0:c0 + n], start=False, stop=False)
                            k.mm(PS[6 + ti][:, :n], CPI[par][:, s_, :], t3[:, c0:c0 + n], start=False, stop=False)
                            k.mm(PS[6 + ti][:, :n], CPI[par][:, s_, :], t4[:, c0:c0 + n], start=False, stop=(s_ == 3))
                for ti, (c0, n) in enumerate(cts):
                    k.stt(G1[:, c0:c0 + n], U32[:, c0:c0 + n], S5D[:, j:j + 1], PS[6 + ti][:, :n], ALU.mult, ALU.add)
                k.act(AF.Square, Q1[:, 0:W], G1[:, 0:W], scale=0.044715 ** 0.5)
                k.stt(Q1[:, 0:W], Q1[:, 0:W], 1.0, G1[:, 0:W], ALU.add, ALU.mult)
                k.act(AF.Sigmoid, Q1[:, 0:W], Q1[:, 0:W], scale=1.5957691216057308)
                k.tt(BIGA[:, j, 0:W], G1[:, 0:W], Q1[:, 0:W], ALU.mult)
            gw = s5_glu_w.rearrange("(kt p) n -> p kt n", p=128)
            WO = [sb(f"WG{i}", [128, 16, 128], BF16) for i in range(2)]
            for i in range(16):
                par = i % 2
                wo, wi = WO[par], WI[par]
                SGM, SG = FB[2 * par], FB[2 * par + 1]
                k.dma('pool', wo[:, :, :], gw[:, :, i * 128:(i + 1) * 128])
                k.dma('pool', wi[:, 1, :, :], wv[:, :, E + i * 128:E + (i + 1) * 128])
                for ti, (c0, n) in enumerate(cts):
                    pa, pb = PS[4 * par + ti], PS[4 * par + 2 + ti]
                    for kt in range(16):
                        k.mm(pa[:, :n], wo[:, kt, :], BIGA[:, kt, c0:c0 + n], start=(kt == 0), stop=(kt == 15))
                    for kt in range(8):
                        k.mm(pb[:, :n], wi[:, 1, kt, :], XN[:, kt, c0:c0 + n], start=(kt == 0), stop=(kt == 7))
                    k.act(AF.Sigmoid, SGM[:, c0:c0 + n], pa[:, :n], bias=S5GB[:, i:i + 1])
                    k.act(AF.Silu, SG[:, c0:c0 + n], pb[:, :n])
                k.tt(SGM[:, 0:W], SGM[:, 0:W], SG[:, 0:W], ALU.mult)
                k.tt(BIGB[:, i, 0:W], BIGA[:, i, 0:W], SGM[:, 0:W], ALU.mult)
            out_proj(li, seg, BIGB, s5_w_out, FB)
            if si == len(SEGS) - 1:
                t0, t1 = S5T[0], S5T[1]
                k.tt(t0[:, :], S5FRE[:, :], S5R[:, :], ALU.mult)
                k.tt(t1[:, :], S5FIM[:, :], S5I[:, :], ALU.mult)
                k.tt(OP5R[:, :], t0[:, :], t1[:, :], ALU.subtract)
                k.tt(t0[:, :], S5FRE[:, :], S5I[:, :], ALU.mult)
                k.tt(t1[:, :], S5FIM[:, :], S5R[:, :], ALU.mult)
                k.tt(OP5I[:, :], t0[:, :], t1[:, :], ALU.add)
                k.dma('sp', o_p_s5_re[:, :], OP5R[:, :])
                k.dma('sp', o_p_s5_im[:, :], OP5I[:, :])
                k.dma('sp', o_s_s5_re[:, :, :], OS5R[:, :, :])
                k.dma('sp', o_s_s5_im[:, :, :], OS5I[:, :, :])


        rets_d = nc.dram_tensor("rets_d", [128, 4, 2, 512], F32, kind="Internal").ap()
        IDB = sb("IDB", [128, 128], BF16)
        RET_G = [1.0 - 2.0 ** (-5.0 - h) for h in range(4)]

        k.dma('pool', IDB[:, :], ident_d[:, :])

        def ret_setup():
            pass

        def ret_layer(li, seg, si):
            W, Wp, Ws = seg['W'], seg['Wp'], seg['Ws']
            cts = ctiles(W)
            g0 = seg['p0']
            QF = sb("QF", [128, 8, WMAX], BF16)
            KF = sb("KF", [128, 8, WMAX], BF16)
            SBF = sb("SBF", [128, 4, 2, 512], BF16)
            RETS = sb("RETS", [128, 4, 2, 512])
            if si == 0:
                k.memset(RETS[:, :, :, :], 0.0)
            else:
                k.dma('sp', RETS[:, :, :, :], rets_d[:, :, :, :])
            for h_ in range(4):
                k.act(AF.Identity, SBF[:, h_, :, :], RETS[:, h_, :, :])
            VTB = sb("VTB", [128, 6, E], BF16)
            VTF = VTB[:, :, :].bitcast(F32).rearrange("p a b -> p (a b)")
            ZTd = [VTF[:, d * WMAX:(d + 1) * WMAX] for d in range(8)]
            GF = HB
            Y = BIGA
            PSB = [PS[i][:, :].bitcast(BF16) for i in range(8)]
            wq = ret_w_q.rearrange("(kt p) n -> p kt n", p=128)
            wk = ret_w_k.rearrange("(kt p) n -> p kt n", p=128)
            wv = ret_w_v.rearrange("(kt p) n -> p kt n", p=128)
            wg = ret_w_g.rearrange("(kt p) n -> p kt n", p=128)
            def phase1():
              RC = sb("RC", [128, WMAX])
              RSN = sb("RSN", [128, WMAX])
              T4 = [sb(f"RT{i}", [128, WMAX]) for i in range(4)]
              k.dma('sp', RC[:, 0:W], ret_cos[:, g0:g0 + W])
              k.dma('sp', RSN[:, 0:W], ret_sin[:, g0:g0 + W])
              nload = 0
              if True:
                for (wsrc, dst, scale) in ((wq, QF, 1.0), (wk, KF, 1.0 / 16.0)):
                    for h in range(4):
                        for dt in range(2):
                            t = 2 * h + dt
                            wi = WI[nload % 2]
                            nload += 1
                            k.dma('pool', wi[:, 0, :, :], wsrc[:, :, t * 128:(t + 1) * 128])
                            for ti, (c0, n) in enumerate(cts):
                                ps = PS[2 * dt + ti]
                                for kt in range(8):
                                    k.mm(ps[:, :n], wi[:, 0, kt, :], XN[:, kt, c0:c0 + n], start=(kt == 0), stop=(kt == 7))
                                k.act(AF.Identity, T4[dt][:, c0:c0 + n], ps[:, :n], scale=scale)
                        k.tt(T4[2][:, 0:W], T4[0][:, 0:W], RC[:, 0:W], ALU.mult)
                        k.tt(T4[3][:, 0:W], T4[1][:, 0:W], RSN[:, 0:W], ALU.mult)
                        k.tt(dst[:, 2 * h, 0:W], T4[2][:, 0:W], T4[3][:, 0:W], ALU.subtract)
                        k.tt(T4[2][:, 0:W], T4[1][:, 0:W], RC[:, 0:W], ALU.mult)
                        k.tt(T4[3][:, 0:W], T4[0][:, 0:W], RSN[:, 0:W], ALU.mult)
                        k.tt(dst[:, 2 * h + 1, 0:W], T4[2][:, 0:W], T4[3][:, 0:W], ALU.add)
                for t in range(16):
                    wi = WI[nload % 2]
                    nload += 1
                    k.dma('pool', wi[:, 0, :, :], wg[:, :, t * 128:(t + 1) * 128])
                    for ti, (c0, n) in enumerate(cts):
                        ps = PS[4 + 2 * (t % 2) + ti]
                        for kt in range(8):
                            k.mm(ps[:, :n], wi[:, 0, kt, :], XN[:, kt, c0:c0 + n], start=(kt == 0), stop=(kt == 7))
                        k.act(AF.Silu, GF[:, t, c0:c0 + n], ps[:, :n])
            chunks = [(c, min(128, Wp - c)) for c in range(0, Wp, 128)]
            if Ws:
                chunks.append((Wp, 128))

            def phase2():
              WV = sb("WV", [128, 8, 512], BF16)
              if True:
                for eg in range(4):
                    k.dma('pool', WV[:, :, :], wv[:, :, eg * 512:(eg + 1) * 512])
                    for ci, (c0, L) in enumerate(chunks):
                        ps = PS[ci % 4]
                        for kt in range(8):
                            k.mm(ps[0:L, :], XN[:, kt, c0:c0 + L], WV[:, kt, :], start=(kt == 0), stop=(kt == 7))
                        k.act(AF.Identity, VTB[0:L, ci, eg * 512:(eg + 1) * 512], ps[0:L, :])
            def phase3():
              MK = sb("MK", [128, 2, 4, 128])
              GROW = sb("GROW", [128, 2, 4, 128])
              GCOL = sb("GCOL", [128, 3, 4])
              ST = [sb(f"ST{i}", [128, 128], BF16) for i in range(2)]
              QG = [sb(f"QG{i}", [128, 2, 128], BF16) for i in range(2)]
              YN = [sb(f"YN{i}", [128, 512], BF16) for i in range(2)]
              JUNK = sb("JUNK", [128, 512], BF16)
              SS = [sb(f"SS{i}", [128, 1]) for i in range(2)]
              KW = [sb(f"KW{i}", [128, 2, 128], BF16) for i in range(2)]
              k.dma('sp', MK[:, :, :, :], ret_mk[:, :, :, :])
              k.dma('sp', GROW[:, :, :, :], ret_grow[:, :, :, :])
              k.dma('sp', GCOL[:, :, :], ret_gcol[:, :, :])
              if Ws:
                SC = sb("SC", [128, NS, 4])
                CMASK = sb("CMASK", [128, NS, 128], BF16)
                KTS = sb("KTS", [128, 2, 128], BF16)
                QGZ = [sb(f"QGZ{i}", [128, 2, 128], BF16) for i in range(2)]
                KWZ = [sb(f"KWZ{i}", [128, 2, 128], BF16) for i in range(2)]
                S0F = [sb(f"S0F{i}", [128, 2, 512]) for i in range(2)]
                S0B = [sb(f"S0B{i}", [128, 2, 512], BF16) for i in range(2)]
                k.dma('sp', SC[:, :, :], ret_sc[:, :, :])
                k.dma('pool', CMASK[:, :, :], ret_cmask[:, :, :])
              if True:
                for ci, (c0, L) in enumerate(chunks):
                    samp = c0 >= Wp
                    mv = 1 if samp else 0
                    gv = 2 if samp else (0 if L == 128 else 1)

                    def head_gen(h):
                        hp = h % 2
                        gL = RET_G[h] ** (TS if samp else L)
                        for dt in range(2):
                            k.tt(QG[hp][:, dt, 0:L], QF[:, 2 * h + dt, c0:c0 + L], GROW[:, mv, h, 0:L], ALU.mult)
                        pss = PS[hp]
                        for dt in range(2):
                            k.mm(pss[0:L, 0:L], KF[:, 2 * h + dt, c0:c0 + L], QF[:, 2 * h + dt, c0:c0 + L],
                                 start=(dt == 0), stop=(dt == 1))
                        yield
                        k.tt(ST[hp][0:L, 0:L], pss[0:L, 0:L], MK[0:L, mv, h, 0:L], ALU.mult)
                        psy = PS[2 + hp]
                        vt = VTB[0:L, ci, h * 512:(h + 1) * 512]
                        k.mm(psy[0:L, :], ST[hp][0:L, 0:L], vt, start=True, stop=False)
                        if not samp:
                            for dt in range(2):
                                k.mm(psy[0:L, :], QG[hp][:, dt, 0:L], SBF[:, h, dt, :], start=False, stop=(dt == 1))
                        else:
                            for dt in range(2):
                                pst = PSB[4 + hp][0:L, 512 + dt * 128:512 + (dt + 1) * 128]
                                k.tr(pst, KF[:, 2 * h + dt, c0:c0 + L], IDB[:, :])
                                k.act(AF.Identity, KTS[0:L, dt, :], pst)
                            for i in range(NS):
                                ip = i % 2
                                k.dma('sp', S0F[ip][:, :, :], st_ret[i, h].rearrange("(dt p) e -> p dt e", p=128))
                                k.act(AF.Identity, S0B[ip][:, :, :], S0F[ip][:, :, :])
                                k.tt(QGZ[ip][:, :, :], QG[hp][:, :, :],
                                     CMASK[:, i, :].unsqueeze(1).to_broadcast([128, 2, 128]), ALU.mult)
                                for dt in range(2):
                                    k.mm(psy[0:L, :], QGZ[ip][:, dt, :], S0B[ip][:, dt, :], start=False,
                                         stop=(i == NS - 1 and dt == 1))
                                k.act(AF.Identity, KWZ[ip][:, :, :], KTS[:, :, :], scale=SC[:, i, h:h + 1])
                                for dt in range(2):
                                    k.mm(PS[6 + dt][:, :], KWZ[ip][:, dt, :], vt, start=True, stop=True)
                                    k.stt(S0F[ip][:, dt, :], S0F[ip][:, dt, :], gL, PS[6 + dt][:, :], ALU.mult, ALU.add)
                                k.dma('sp', o_s_ret[i, h].rearrange("(dt p) e -> p dt e", p=128), S0F[ip][:, :, :])
                        yield
                        k.act(AF.Square, JUNK[0:L, :], psy[0:L, :], accum_out=SS[hp][0:L, 0:1])
                        k.act(AF.Sqrt, SS[hp][0:L, 0:1], SS[hp][0:L, 0:1], scale=1.0 / 512.0, bias=CONST[0:L, 0:1])
                        yield
                        k.recip(SS[hp][0:L, 0:1], SS[hp][0:L, 0:1])
                        k.act(AF.Identity, YN[hp][0:L, :], psy[0:L, :], scale=SS[hp][0:L, 0:1])
                        yield
                        for et in range(4):
                            pst = PSB[4 + hp][:, et * 128:et * 128 + L]
                            k.tr(pst, YN[hp][0:L, et * 128:(et + 1) * 128], IDB[0:L, 0:L])
                        yield
                        for et in range(4):
                            pst = PSB[4 + hp][:, et * 128:et * 128 + L]
                            k.tt(Y[:, 4 * h + et, c0:c0 + L], pst, GF[:, 4 * h + et, c0:c0 + L], ALU.mult)
                        if not samp:
                            for dt in range(2):
                                pst = PSB[4 + hp][0:L, 512 + dt * 128:512 + (dt + 1) * 128]
                                k.tr(pst, KF[:, 2 * h + dt, c0:c0 + L], IDB[:, :])
                            yield
                            for dt in range(2):
                                pst = PSB[4 + hp][0:L, 512 + dt * 128:512 + (dt + 1) * 128]
                                k.act(AF.Identity, KW[hp][0:L, dt, :], pst, scale=GCOL[0:L, gv, h:h + 1])
                            yield
                            for dt in range(2):
                                k.mm(PS[6 + dt][:, :], KW[hp][0:L, dt, :], vt, start=True, stop=True)
                                k.stt(RETS[:, h, dt, :], RETS[:, h, dt, :], gL, PS[6 + dt][:, :], ALU.mult, ALU.add)
                                k.act(AF.Identity, SBF[:, h, dt, :], RETS[:, h, dt, :])
                        yield

                    if samp:
                        for h in range(4):
                            for _ in head_gen(h):
                                pass
                    else:
                        for hpair in range(2):
                            gens = [head_gen(2 * hpair), head_gen(2 * hpair + 1)]
                            alive = [True, True]
                            while any(alive):
                                for gi in range(2):
                                    if alive[gi] and next(gens[gi], 'end') == 'end':
                                        alive[gi] = False

            with layer_scope():
                phase1()
            with layer_scope():
                phase2()
            with layer_scope():
                phase3()
            out_proj(li, seg, Y, ret_w_o, ZTd)
            if si == len(SEGS) - 1:
                k.dma('sp', o_p_ret.rearrange("h (dt p) e -> p h dt e", p=128), RETS[:, :, :, :])
            else:
                k.dma('sp', rets_d[:, :, :, :], RETS[:, :, :, :])


        RWM = sb("RWM", [128, 16, 64])
        RWMB = sb("RWMB", [128, 16, 64], BF16)
        SHC = sb("SHC", [128, 8])
        RMU = sb("RMU", [128, 6, 8])
        RVEC = sb("RVEC", [128, 5, 16])
        DEC_C = 0.6065306597126334

        def rwkv_setup():
            k.memset(RWM[:, :, :], 0.0)
            k.memset(RWMB[:, :, :], 0.0)
            k.memset(SHC[:, :], 0.0)
            k.memset(CONST[:, 3:4], 64e-5)
            k.dma('sp', RMU[:, :, :], rw_mu[:, :, :])
            k.dma('sp', RVEC[:, :, :], rw_vecs[:, :, :])

        def rwkv_layer(li, seg, si):
            W, Wp, Ws = seg['W'], seg['Wp'], seg['Ws']
            cts = ctiles(W)
            chunks = [(c, min(128, Wp - c)) for c in range(0, Wp, 128)]
            if Ws:
                chunks.append((Wp, 128))
            nch = len(chunks)
            PSB = [PS[i][:, :].bitcast(BF16) for i in range(8)]
            RW0, RA0, RKK, RKA, RRK = (RVEC[:, i, :] for i in range(5))
            Y = BIGA
            GF = HB

            def phaseA():
                XX = sb("XX", [128, 8, W], BF16)
                XMb = sb("XM", [128, 8, W], BF16)
                SHN = sb("SHN", [128, 8])
                W1 = sb("W1", [128, 8, 64], BF16)
                A1W = sb("A1W", [128, 8, 64], BF16)
                TW = sb("TW", [64, W], BF16)
                TA = sb("TA", [64, W], BF16)
                BONES = sb("BONES", [128, 128], BF16)
                MSK = sb("MSK", [128, 2, 3, 128])
                SEL = sb("SEL", [128, 2], BF16)
                RT = sb("RT", [128, 4, W], BF16)
                AT = sb("AT", [128, 4, W], BF16)
                BT_ = sb("BT_", [128, 4, W], BF16)
                KT_ = sb("KT_", [128, 4, W], BF16)
                GL = sb("GL", [128, 4, 8])
                VTg = sb("VTg", [128, 6, 512], BF16)
                LNW = sb("LNW", [128, 512])
                LNB = sb("LNB", [128, 512])
                if Ws:
                    OSS = sb("OSS", [128, 8, NS])
                    STS = sb("STS", [128, 8, NS])
                    RESET = sb("RESET", [128, 128])
                    CMASK = sb("CMASK", [128, NS, 128], BF16)
                    OH = sb("OH", [128, NS], BF16)
                    GLS = sb("GLS", [128, 4, NS])
                    k.dma('sp', RESET[:, :], rw_reset[:, :])
                    k.dma('pool', CMASK[:, :, :], ret_cmask[:, :, :])
                    k.dma('pool', OH[:, :], rw_oh[:, :])
                    k.dma('sp', STS[:, :, :], st_rw_shift[:, :, :])
                k.dma('sp', MSK[:, :, :, :], rw_masks[:, :, :, :])
                k.dma('pool', SEL[:, :], rw_sel[:, :])
                k.dma('pool', W1[:, :, :], rw_w1.rearrange("(kt p) n -> p kt n", p=128))
                k.dma('pool', A1W[:, :, :], rw_a1.rearrange("(kt p) n -> p kt n", p=128))
                k.memset(BONES[:, :], 0.0)
                k.memset(BONES[0:64, 0:64], 1.0)
                k.memset(BONES[64:128, 64:128], 1.0)
                for kt in range(8):
                    k.stt(SHN[:, kt:kt + 1], H[:, kt, Wp - 1:Wp], NPRE[:, li, kt:kt + 1], RS[:, Wp - 1:Wp],
                          ALU.mult, ALU.mult)
                if Ws:
                    for kt in range(8):
                        hv = H[:, kt, Wp:W].rearrange("p (s t) -> p s t", t=TS)[:, :, TS - 1]
                        rv = RS[:, Wp:W].rearrange("p (s t) -> p s t", t=TS)[:, :, TS - 1]
                        k.stt(OSS[:, kt, :], hv, NPRE[:, li, kt:kt + 1], rv, ALU.mult, ALU.mult)
                    k.dma('sp', o_s_rw_shift[:, :, :], OSS[:, :, :])
                for kt in range(8):
                    k.tt(XX[:, kt, 1:W], XN[:, kt, 0:W - 1], XN[:, kt, 1:W], ALU.subtract)
                    k.tt(XX[:, kt, 0:1], SHC[:, kt:kt + 1], XN[:, kt, 0:1], ALU.subtract)
                    if Ws:
                        xs = XX[:, kt, Wp:W].rearrange("p (s t) -> p s t", t=TS)[:, :, 0]
                        x0 = XN[:, kt, Wp:W].rearrange("p (s t) -> p s t", t=TS)[:, :, 0]
                        k.tt(xs, STS[:, kt, :], x0, ALU.subtract)
                k.copy(SHC[:, :], SHN[:, :])
                if si == len(SEGS) - 1:
                    k.dma('sp', o_p_rw_shift[:, :], SHC[:, :])

                def mix(n_, dst):
                    for kt in range(8):
                        k.stt(dst[:, kt, 0:W], XX[:, kt, 0:W], RMU[:, n_, kt:kt + 1], XN[:, kt, 0:W], ALU.mult, ALU.add)

                mix(1, XMb)
                for ti, (c0, n) in enumerate(cts):
                    for kt in range(8):
                        k.mm(PS[ti][0:64, :n], W1[:, kt, :], XMb[:, kt, c0:c0 + n], start=(kt == 0), stop=(kt == 7))
                    k.act(AF.Tanh, TW[:, c0:c0 + n], PS[ti][0:64, :n])
                mix(4, XMb)
                for ti, (c0, n) in enumerate(cts):
                    for kt in range(8):
                        k.mm(PS[2 + ti][0:64, :n], A1W[:, kt, :], XMb[:, kt, c0:c0 + n], start=(kt == 0), stop=(kt == 7))
                    k.act(AF.Identity, TA[:, c0:c0 + n], PS[2 + ti][0:64, :n])
                mix(5, XMb)
                wg = rw_w_g.rearrange("(kt p) n -> p kt n", p=128)
                for t in range(16):
                    wi = WI[t % 2]
                    k.dma('pool', wi[:, 0, :, :], wg[:, :, t * 128:(t + 1) * 128])
                    for ti, (c0, n) in enumerate(cts):
                        ps = PS[4 + 2 * (t % 2) + ti]
                        for kt in range(8):
                            k.mm(ps[:, :n], wi[:, 0, kt, :], XMb[:, kt, c0:c0 + n], start=(kt == 0), stop=(kt == 7))
                        k.act(AF.Silu, GF[:, t, c0:c0 + n], ps[:, :n])
                wr = rw_w_r.rearrange("(kt p) n -> p kt n", p=128)
                wk = rw_w_k.rearrange("(kt p) n -> p kt n", p=128)
                wv = rw_w_v.rearrange("(kt p) n -> p kt n", p=128)
                def passes(hg):
                    W2T = [sb(f"W2T{i}", [64, 2, 128], BF16) for i in range(2)]
                    TB7s = [[sb(f"TB7_{p}_{i}", [128, W]) for i in range(7)] for p in range(2)]
                    SQBs = [sb(f"SQB{p}", [128, W], BF16) for p in range(2)]
                    k.dma('sp', LNW[:, :], rw_lnw[:, hg * 512:(hg + 1) * 512])
                    k.dma('sp', LNB[:, :], rw_lnb[:, hg * 512:(hg + 1) * 512])
                    mix(2, XMb)
                    def p1_tile(tl):
                        t = 4 * hg + tl
                        wi = WI[tl % 2]
                        w2t = W2T[tl % 2]
                        KFt, AS, SG_, KK, T5, CS, EG = TB7s[tl % 2]
                        SQB = SQBs[tl % 2]
                        k.dma('pool', wi[:, 1, :, :], wk[:, :, t * 128:(t + 1) * 128])
                        k.dma('pool', w2t[:, 0, :], rw_w2[:, t * 128:(t + 1) * 128])
                        k.dma('pool', w2t[:, 1, :], rw_a2[:, t * 128:(t + 1) * 128])
                        for ti, (c0, n) in enumerate(cts):
                            for kt in range(8):
                                k.mm(PS[2 + ti][:, :n], wi[:, 1, kt, :], XMb[:, kt, c0:c0 + n], start=(kt == 0), stop=(kt == 7))
                            k.mm(PS[4 + ti][:, :n], w2t[:, 1, :], TA[:, c0:c0 + n])
                            k.mm(PS[6 + ti][:, :n], w2t[:, 0, :], TW[:, c0:c0 + n])
                            k.act(AF.Identity, KFt[:, c0:c0 + n], PS[2 + ti][:, :n])
                            k.act(AF.Sigmoid, AS[:, c0:c0 + n], PS[4 + ti][:, :n], bias=RA0[:, t:t + 1])
                            k.act(AF.Sigmoid, SG_[:, c0:c0 + n], PS[6 + ti][:, :n], bias=RW0[:, t:t + 1])
                            yield
                        w_ = slice(0, W)
                        k.ts(KK[:, w_], KFt[:, w_], RKK[:, t:t + 1], ALU.mult)
                        k.act(AF.Square, SQB[:, w_], KK[:, w_])
                        for ti, (c0, n) in enumerate(cts):
                            k.mm(PS[ti][:, :n], BONES[:, :], SQB[:, c0:c0 + n])
                            k.ts(T5[:, c0:c0 + n], PS[ti][:, :n], 1e-24, ALU.max)
                        yield
                        k.act(AF.Sqrt, T5[:, w_], T5[:, w_])
                        k.recip(T5[:, w_], T5[:, w_])
                        k.tt(KK[:, w_], KK[:, w_], T5[:, w_], ALU.mult)
                        yield
                        k.ts(T5[:, w_], AS[:, w_], 1.0, ALU.subtract, RKA[:, t:t + 1], ALU.mult)
                        k.stt(T5[:, w_], T5[:, w_], 1.0, KFt[:, w_], ALU.add, ALU.mult)
                        k.tt(KFt[:, w_], KK[:, w_], AS[:, w_], ALU.mult)
                        for (c0, L) in chunks:
                            if c0 >= Wp:
                                k.scan(CS[:, c0:c0 + L], RESET[:, 0:L], SG_[:, c0:c0 + L], 0.0)
                            else:
                                k.scan(CS[:, c0:c0 + L], ONE_AP.to_broadcast([128, L]), SG_[:, c0:c0 + L], 0.0)
                        yield
                        k.act(AF.Exp, EG[:, w_], CS[:, w_], scale=-DEC_C)
                        k.act(AF.Identity, RT[:, tl, w_], EG[:, w_])
                        for ci, (c0, L) in enumerate(chunks):
                            if c0 >= Wp:
                                k.copy(GLS[:, tl, :], EG[:, c0:c0 + L].rearrange("p (s t) -> p s t", t=TS)[:, :, TS - 1])
                            else:
                                k.copy(GL[:, tl, ci:ci + 1], EG[:, c0 + L - 1:c0 + L])
                        yield
                        k.act(AF.Exp, AS[:, w_], CS[:, w_], scale=DEC_C)
                        k.tt(BT_[:, tl, w_], KFt[:, w_], AS[:, w_], ALU.mult)
                        k.tt(KT_[:, tl, w_], T5[:, w_], AS[:, w_], ALU.mult)
                        k.tt(CS[:, w_], CS[:, w_], SG_[:, w_], ALU.subtract)
                        k.act(AF.Exp, CS[:, w_], CS[:, w_], scale=-DEC_C)
                        k.stt(AT[:, tl, w_], KK[:, w_], -1.0, CS[:, w_], ALU.mult, ALU.mult)
                        yield

                    for tp in range(2):
                        gens = [p1_tile(2 * tp), p1_tile(2 * tp + 1)]
                        alive = [True, True]
                        while any(alive):
                            for gi in range(2):
                                if alive[gi] and next(gens[gi], 'end') == 'end':
                                    alive[gi] = False
                    mix(0, XMb)
                    for tl in range(4):
                        t = 4 * hg + tl
                        wi = WI[tl % 2]
                        k.dma('pool', wi[:, 0, :, :], wr[:, :, t * 128:(t + 1) * 128])
                        for ti, (c0, n) in enumerate(cts):
                            for kt in range(8):
                                k.mm(PS[ti][:, :n], wi[:, 0, kt, :], XMb[:, kt, c0:c0 + n], start=(kt == 0), stop=(kt == 7))
                            k.tt(RT[:, tl, c0:c0 + n], PS[ti][:, :n], RT[:, tl, c0:c0 + n], ALU.mult)

                def pass3(hg):
                    mix(3, XMb)
                    cnt_ = 0
                    for q in range(4):
                        wi = WI[q % 2]
                        k.dma('pool', wi[:, 0, :, :], wv[:, :, hg * 512 + q * 128:hg * 512 + (q + 1) * 128])
                        for ci, (c0, L) in enumerate(chunks):
                            ps = PS[5 + ci % 3]
                            for kt in range(8):
                                k.mm(ps[0:L, 0:128], XMb[:, kt, c0:c0 + L], wi[:, 0, kt, :], start=(kt == 0), stop=(kt == 7))
                            k.act(AF.Identity, VTg[0:L, ci, q * 128:(q + 1) * 128], ps[0:L, 0:128])
                            cnt_ += 1
                            if cnt_ % 4 == 0:
                                yield

                def chunkloop(hg):
                    MATS = [sb(f"MATS{i}", [128, 4, 8, 128], BF16) for i in range(2)]
                    LJ = [[sb(f"LJ{g}_{i}", [128, 4, 128], BF16) for i in range(2)] for g in range(2)]
                    NJ = [[sb(f"NJ{g}_{i}", [128, 4, 128], BF16) for i in range(2)] for g in range(2)]
                    PJ = [[sb(f"PJ{g}_{i}", [128, 4, 128], BF16) for i in range(2)] for g in range(2)]
                    BKTs = [sb(f"BKT{i}", [128, 1024], BF16) for i in range(2)]
                    RKc = sb("RKc", [128, 4, 128], BF16)
                    XB = sb("XB", [128, 512], BF16)
                    UB = sb("UB", [128, 512], BF16)
                    YS = sb("YS", [128, 512])
                    SQ = sb("SQ", [128, 512])
                    OUTB = sb("OUTB", [128, 512], BF16)
                    ST8 = sb("ST8", [128, 8])
                    ST8b = sb("ST8b", [128, 8])
                    if Ws:
                        S0B = sb("S0B", [128, NS, 64], BF16)
                        S0T = sb("S0T", [128, NS, 64])
                        AZt = sb("AZt", [128, NS, 128], BF16)
                        UZt = AZt
                        VZt = sb("VZt", [128, NS, 128], BF16)

                    def cinfo(ci):
                        c0, L = chunks[ci]
                        samp = c0 >= Wp
                        mv = 1 if samp else 0
                        blk_ = TS if samp else L
                        nlev = 1
                        while (1 << (nlev + 1)) < blk_:
                            nlev += 1
                        return c0, L, samp, mv, nlev

                    def precompute(ci, gen=None):
                        c0, L, samp, mv, nlev = cinfo(ci)

                        def step():
                            if gen is not None:
                                next(gen, None)
                        MX = MATS[ci % 2]
                        MSU = MSK[0:L, mv, 0, 0:L]
                        MSL = MSK[0:L, mv, 1, 0:L]
                        MIU = MSK[0:L, mv, 2, 0:L]
                        v4 = lambda ps: ps[0:L, :].rearrange("p (a b) -> p a b", b=128)[:, :, 0:L]
                        m4 = lambda M: M.unsqueeze(1).to_broadcast([L, 4, L])
                        v2 = lambda ps: ps[0:L, 0:256].rearrange("p (a b) -> p a b", b=128)[:, :, 0:L]
                        m2 = lambda M: M.unsqueeze(1).to_broadcast([L, 2, L])
                        for g4 in range(2):
                            lj, nj, pj = LJ[g4], NJ[g4], PJ[g4]

                            def st1(types):
                                for ti_, ty in enumerate(types):
                                    for i in range(4):
                                        hl = 4 * g4 + i
                                        tl, hh = hl // 2, hl % 2
                                        hr = slice(hh * 64, hh * 64 + 64)
                                        At, Bt = AT[hr, tl, c0:c0 + L], BT_[hr, tl, c0:c0 + L]
                                        Kt, Rt = KT_[hr, tl, c0:c0 + L], RT[hr, tl, c0:c0 + L]
                                        lhs, rhs = {'nab': (Bt, At), 'lab': (At, Bt), 'nak': (Kt, At),
                                                    'nrb': (Bt, Rt), 'nrk': (Kt, Rt)}[ty]
                                        pr = i // 2
                                        k.mm(PS[2 * ti_ + hh][0:L, pr * 128:pr * 128 + L], lhs, rhs)
                            st1(['nab', 'lab'])
                            for hh in range(2):
                                k.tt(nj[0][0:L, 2 * hh:2 * hh + 2, 0:L], v2(PS[hh]), m2(MSU), ALU.mult)
                                k.tt(lj[0][0:L, 2 * hh:2 * hh + 2, 0:L], v2(PS[2 + hh]), m2(MSL), ALU.mult)
                            k.tt(pj[0][0:L, :, 0:L], nj[0][0:L, :, 0:L], m4(IDB[0:L, 0:L]), ALU.add)
                            st1(['nak', 'nrb'])
                            for hh in range(2):
                                ms = slice(4 * g4 + 2 * hh, 4 * g4 + 2 * hh + 2)
                                k.tt(MX[0:L, 1, ms, 0:L], v2(PS[hh]), m2(MSU), ALU.mult)
                                k.tt(MX[0:L, 2, ms, 0:L], v2(PS[2 + hh]), m2(MIU), ALU.mult)
                            st1(['nrk'])
                            for hh in range(2):
                                ms = slice(4 * g4 + 2 * hh, 4 * g4 + 2 * hh + 2)
                                k.tt(MX[0:L, 3, ms, 0:L], v2(PS[hh]), m2(MIU), ALU.mult)
                        for j in range(1, nlev + 1):
                            last = j == nlev
                            step()
                            for g4 in range(2):
                                lj, nj, pj = LJ[g4], NJ[g4], PJ[g4]
                                bx, by = PS[2 * g4], PS[2 * g4 + 1]
                                lp, np_ = lj[(j - 1) % 2], nj[(j - 1) % 2]
                                ln, nn = lj[j % 2], nj[j % 2]
                                for i in range(4):
                                    k.mm(bx[0:L, i * 128:i * 128 + L], np_[0:L, i, 0:L], lp[0:L, i, 0:L])
                                if not last:
                                    for i in range(4):
                                        k.mm(by[0:L, i * 128:i * 128 + L], lp[0:L, i, 0:L], np_[0:L, i, 0:L])
                                k.act(AF.Identity, ln[0:L, :, 0:L], v4(bx))
                                if not last:
                                    k.copy(nn[0:L, :, 0:L], v4(by))
                            for g4 in range(2):
                                lj, pj = LJ[g4], PJ[g4]
                                bx = PS[2 * g4]
                                hs = slice(4 * g4, 4 * g4 + 4)
                                ln, pp, pn = lj[j % 2], pj[(j - 1) % 2], pj[j % 2]
                                for i in range(4):
                                    k.mm(bx[0:L, i * 128:i * 128 + L], IDB[0:L, 0:L], pp[0:L, i, 0:L], start=True, stop=False)
                                    k.mm(bx[0:L, i * 128:i * 128 + L], ln[0:L, i, 0:L], pp[0:L, i, 0:L], start=False, stop=True)
                                Pn = MX[0:L, 0, hs, 0:L] if last else pn[0:L, :, 0:L]
                                k.act(AF.Identity, Pn, v4(bx))
                        p5 = PSB[4]
                        bkt = BKTs[ci % 2]
                        for tl in range(4):
                            k.tr(p5[0:L, tl * 128:(tl + 1) * 128], BT_[:, tl, c0:c0 + L], IDB[:, :])
                            k.tr(p5[0:L, 512 + tl * 128:512 + (tl + 1) * 128], KT_[:, tl, c0:c0 + L], IDB[:, :])
                        k.act(AF.Identity, bkt[0:L, :], p5[0:L, :])

                    def hidx(hl):
                        return 4 * (hl // 4) + ((hl % 4) % 2) * 2 + (hl % 4) // 2

                    def sequential(ci):
                        c0, L, samp, mv, nlev = cinfo(ci)
                        MX = MATS[ci % 2]
                        BKT = BKTs[ci % 2]
                        p5 = PSB[5]
                        vt = lambda hl: VTg[0:L, ci, hl * 64:(hl + 1) * 64]
                        for hl in range(8):
                            tl, hh = hl // 2, hl % 2
                            hr = slice(hh * 64, hh * 64 + 64)
                            po = PS[5][0:L, hl * 64:(hl + 1) * 64]
                            if not samp:
                                k.mm(po, AT[hr, tl, c0:c0 + L], RWMB[hr, 4 * hg + tl, :], start=True, stop=False)
                            else:
                                if hh == 0:
                                    k.dma('pool', S0B[:, :, :], st_rw_wkv[4 * hg + tl, :, :, :])
                                    k.tt(AZt[:, :, :], AT[:, tl, c0:c0 + L].unsqueeze(1).to_broadcast([128, NS, 128]),
                                         CMASK[:, :, :], ALU.mult)
                                for i in range(NS):
                                    k.mm(po, AZt[hr, i, :], S0B[hr, i, :], start=(i == 0), stop=False)
                            k.mm(po, MX[0:L, 1, hidx(hl), 0:L], vt(hl), start=False, stop=True)
                        k.act(AF.Identity, XB[0:L, :], PS[5][0:L, :])
                        yield
                        for hl in range(8):
                            k.mm(PS[5][0:L, hl * 64:(hl + 1) * 64], MX[0:L, 0, hidx(hl), 0:L], XB[0:L, hl * 64:(hl + 1) * 64])
                        k.act(AF.Identity, UB[0:L, :], PS[5][0:L, :])
                        yield
                        if not samp:
                            for hl in range(8):
                                tl, hh = hl // 2, hl % 2
                                hr = slice(hh * 64, hh * 64 + 64)
                                po = PS[7][hr, tl * 64:(tl + 1) * 64]
                                k.mm(po, BKT[0:L, tl * 128 + hh * 64:tl * 128 + hh * 64 + 64], UB[0:L, hl * 64:(hl + 1) * 64],
                                     start=True, stop=False)
                                k.mm(po, BKT[0:L, 512 + tl * 128 + hh * 64:512 + tl * 128 + hh * 64 + 64], vt(hl),
                                     start=False, stop=True)
                        yield
                        for hl in range(8):
                            tl, hh = hl // 2, hl % 2
                            hr = slice(hh * 64, hh * 64 + 64)
                            po = PS[6][0:L, hl * 64:(hl + 1) * 64]
                            if not samp:
                                k.mm(po, RT[hr, tl, c0:c0 + L], RWMB[hr, 4 * hg + tl, :], start=True, stop=False)
                            else:
                                if hh == 0:
                                    k.dma('pool', S0B[:, :, :], st_rw_wkv[4 * hg + tl, :, :, :])
                                    k.tt(AZt[:, :, :], RT[:, tl, c0:c0 + L].unsqueeze(1).to_broadcast([128, NS, 128]),
                                         CMASK[:, :, :], ALU.mult)
                                for i in range(NS):
                                    k.mm(po, AZt[hr, i, :], S0B[hr, i, :], start=(i == 0), stop=False)
                            k.mm(po, MX[0:L, 2, hidx(hl), 0:L], UB[0:L, hl * 64:(hl + 1) * 64], start=False, stop=False)
                            k.mm(po, MX[0:L, 3, hidx(hl), 0:L], vt(hl), start=False, stop=True)
                        if not samp:
                            mg = RWM[:, 4 * hg:4 * hg + 4, :]
                            k.tt(mg, mg, PS[7][:, 0:256].rearrange("p (a b) -> p a b", b=64), ALU.add)
                            k.tt(mg, mg, GL[:, :, ci:ci + 1].to_broadcast([128, 4, 64]), ALU.mult)
                            k.act(AF.Identity, RWMB[:, 4 * hg:4 * hg + 4, :], mg)
                        k.act(AF.Identity, YS[0:L, :], PS[6][0:L, :])
                        yield
                        if samp:
                            for tl in range(4):
                                t = 4 * hg + tl
                                k.dma('sp', S0T[:, :, :], st_rw_wkv[t, :, :, :])
                                ohb = OH[:, :].unsqueeze(2).to_broadcast([128, NS, 128])
                                k.tt(UZt[:, :, :], UB[:, tl * 128:(tl + 1) * 128].unsqueeze(1).to_broadcast([128, NS, 128]),
                                     ohb, ALU.mult)
                                k.tt(VZt[:, :, :], VTg[:, ci, tl * 128:(tl + 1) * 128].unsqueeze(1).to_broadcast([128, NS, 128]),
                                     ohb, ALU.mult)
                                for hh in range(2):
                                    hr = slice(hh * 64, hh * 64 + 64)
                                    for i in range(NS):
                                        po = PS[6 + i // 8][hr, (i % 8) * 64:(i % 8 + 1) * 64]
                                        k.mm(po, BKT[:, tl * 128 + hh * 64:tl * 128 + hh * 64 + 64],
                                             UZt[:, i, hh * 64:(hh + 1) * 64], start=True, stop=False)
                                        k.mm(po, BKT[:, 512 + tl * 128 + hh * 64:512 + tl * 128 + hh * 64 + 64],
                                             VZt[:, i, hh * 64:(hh + 1) * 64], start=False, stop=True)
                                for half in range(2):
                                    sv = S0T[:, half * 8:(half + 1) * 8, :]
                                    k.tt(sv, sv, PS[6 + half][:, :].rearrange("p (a b) -> p a b", b=64), ALU.add)
                                k.tt(S0T[:, :, :], S0T[:, :, :], GLS[:, tl, :].unsqueeze(2).to_broadcast([128, NS, 64]), ALU.mult)
                                k.dma('sp', o_s_rw_wkv[t, :, :, :], S0T[:, :, :])
                        yield
                        YS3 = YS[0:L, :].rearrange("p (a b) -> p a b", b=64)
                        SQ3 = SQ[0:L, :].rearrange("p (a b) -> p a b", b=64)
                        k.op('dve', lambda e: e.reduce_sum(out=ST8[0:L, :], in_=YS3, axis=AX.X), r=[YS], w=[ST8])
                        k.ts(ST8[0:L, :], ST8[0:L, :], 1.0 / 64.0, ALU.mult)
                        k.tt(YS3, YS3, ST8[0:L, :].unsqueeze(2).to_broadcast([L, 8, 64]), ALU.subtract)
                        k.tt(SQ3, YS3, YS3, ALU.mult)
                        k.op('dve', lambda e: e.reduce_sum(out=ST8b[0:L, :], in_=SQ3, axis=AX.X), r=[SQ], w=[ST8b])
                        k.act(AF.Sqrt, ST8b[0:L, :], ST8b[0:L, :], scale=1.0 / 64.0, bias=CONST[0:L, 3:4])
                        k.recip(ST8b[0:L, :], ST8b[0:L, :])
                        k.tt(YS3, YS3, ST8b[0:L, :].unsqueeze(2).to_broadcast([L, 8, 64]), ALU.mult)
                        k.tt(YS[0:L, :], YS[0:L, :], LNW[0:L, :], ALU.mult)
                        k.tt(YS[0:L, :], YS[0:L, :], LNB[0:L, :], ALU.add)
                        yield
                        for tl in range(4):
                            k.stt(RKc[:, tl, 0:L], RT[:, tl, c0:c0 + L], RRK[:, 4 * hg + tl:4 * hg + tl + 1],
                                  KT_[:, tl, c0:c0 + L], ALU.mult, ALU.mult)
                            k.mm(PS[7][0:L, 256 + 2 * tl:256 + 2 * tl + 2], RKc[:, tl, 0:L], SEL[:, :])
                        k.tt(SQ3, VTg[0:L, ci, :].rearrange("p (a b) -> p a b", b=64),
                             PS[7][0:L, 256:264].unsqueeze(2).to_broadcast([L, 8, 64]), ALU.mult)
                        k.tt(OUTB[0:L, :], YS[0:L, :], SQ[0:L, :], ALU.add)
                        for tl in range(4):
                            k.tr(p5[:, tl * 128:tl * 128 + L], OUTB[0:L, tl * 128:(tl + 1) * 128], IDB[0:L, 0:L])
                            k.tt(Y[:, 4 * hg + tl, c0:c0 + L], p5[:, tl * 128:tl * 128 + L],
                                 GF[:, 4 * hg + tl, c0:c0 + L], ALU.mult)

                    g3 = pass3(hg)
                    precompute(0, g3)
                    for _ in g3:
                        pass
                    for ci in range(nch):
                        gen = sequential(ci)
                        if ci + 1 < nch:
                            precompute(ci + 1, gen)
                        for _ in gen:
                            pass

                for hg in range(4):
                    with layer_scope():
                        passes(hg)
                    with layer_scope():
                        chunkloop(hg)
                if si == len(SEGS) - 1:
                    k.dma('sp', o_p_rw_wkv[:, :, :], RWM[:, :, :])

            with layer_scope():
                phaseA()
            ZTd = [sb(f"ZTD{i}", [128, WMAX]) for i in range(8)]
            out_proj(li, seg, Y, rw_w_o, ZTd)

        LAYERS = {0: (lru_setup, lru_layer), 1: (s5_setup, s5_layer), 2: (rwkv_setup, rwkv_layer), 3: (ret_setup, ret_layer)}
        for li in order:
            LAYERS[li][0]()

        for si, seg in enumerate(SEGS):
            seg = dict(seg)
            seg['W'] = seg['Wp'] + seg['Ws']
            W = seg['W']
            g0 = seg['p0']
            k.dma('sp', H[:, :, 0:W], hin_v[:, :, g0:g0 + W])
            for li in order:
                pre_norm(li, seg)
                k.dma('sp', hspill[:, :, 0:W], H[:, :, 0:W])
                with layer_scope():
                    LAYERS[li][1](li, seg, si)
            lo = NMETA if si == 0 else 0
            k.dma('sp', yout_v[:, :, g0 + lo - NMETA:g0 + W - NMETA], H[:, :, lo:W])
        k.finish()
    P.ninstr = k.ninstr
    return P


def _ct(v, ntile):
    v = np.asarray(v)
    lead = v.shape[:-1]
    v = v.reshape(lead + (ntile, 128))
    nd = v.ndim
    perm = (nd - 1, nd - 2) + tuple(range(nd - 2))
    return np.ascontiguousarray(v.transpose(perm))


def prep_core(inp, c):
    f = lambda a: np.ascontiguousarray(a, dtype=np.float32)
    m = {}
    xp = inp['x_prompt'][c]
    xs = inp['x_sample'][c * NS:(c + 1) * NS].reshape(WS, D)
    m['hin'] = f(np.concatenate([inp['meta_tokens'], xp, xs], axis=0).T)
    m['normpre'] = f(_ct(inp['norm_pre'], 8))
    m['normpre'] = f(m['normpre'].transpose(0, 2, 1))
    m['normpost'] = f(_ct(inp['norm_post'], 8).transpose(0, 2, 1))
    m['ident'] = np.eye(128, dtype=np.float32)
    m['lru_w_in'] = f(inp['lru_w_in'][0])
    m['lru_cw'] = f(_ct(inp['lru_conv_w'][0], 16))
    m['lru_cb'] = f(_ct(inp['lru_conv_b'][0], 16))
    m['lru_wa'] = f(inp['lru_wa'][0])
    m['lru_wx'] = f(inp['lru_wx'][0])
    m['lru_ba'] = f(inp['lru_ba'][0].T)
    m['lru_bx'] = f(inp['lru_bx'][0].T)
    m['lru_lam'] = f(_ct(inp['lru_lam'][0], 16))
    m['lru_w_out'] = f(inp['lru_w_out'][0])
    sl = slice(c * NS, (c + 1) * NS)
    m['st_lru_conv'] = f(_ct(inp['state_lru_conv'][0, sl], 16))
    m['st_lru_h'] = f(_ct(inp['state_lru_h'][0, sl], 16))
    m['s5_w_in'] = f(inp['s5_w_in'][0])
    chan = lambda a: f(np.asarray(a).reshape(64, 128).T)
    m['s5_are'] = chan(inp['s5_a_re'][0])
    m['s5_aim'] = chan(inp['s5_a_im'][0])
    m['s5_ldt'] = chan(np.repeat(inp['s5_log_dt'][0][:, None], 64, axis=1))
    bre = inp['s5_b_re'][0].reshape(16, 8, 64, 16)
    bim = inp['s5_b_im'][0].reshape(16, 8, 64, 16)
    cre = inp['s5_c_re'][0].reshape(16, 8, 16, 64)
    cim = inp['s5_c_im'][0].reshape(16, 8, 16, 64)
    BR = np.zeros((16, 128, 4, 128), np.float32); BI = np.zeros_like(BR)
    CR = np.zeros((16, 128, 4, 128), np.float32); CI = np.zeros_like(CR)
    for gl in range(8):
        s_, g2 = gl // 2, gl % 2
        BR[:, 16 * gl:16 * gl + 16, s_, g2 * 64:(g2 + 1) * 64] = bre[:, gl].transpose(0, 2, 1)
        BI[:, 16 * gl:16 * gl + 16, s_, g2 * 64:(g2 + 1) * 64] = bim[:, gl].transpose(0, 2, 1)
        CR[:, g2 * 64:(g2 + 1) * 64, s_, 16 * gl:16 * gl + 16] = cre[:, gl].transpose(0, 2, 1)
        CI[:, g2 * 64:(g2 + 1) * 64, s_, 16 * gl:16 * gl + 16] = cim[:, gl].transpose(0, 2, 1)
    m['s5_bre'], m['s5_bim'], m['s5_cre'], m['s5_cim'] = BR, BI, CR, CI
    m['s5_d'] = f(_ct(inp['s5_d'][0], 16))
    m['s5_glu_w'] = f(inp['s5_glu_w'][0])
    m['s5_glu_b'] = f(_ct(inp['s5_glu_b'][0], 16))
    m['s5_w_out'] = f(inp['s5_w_out'][0])
    m['s5_iota'] = f(np.tile(np.arange(1, 257, dtype=np.float32)[None, :], (128, 1)))
    m['st_s5_re'] = f(inp['state_s5_re'][0, sl].reshape(NS, 64, 128).transpose(2, 1, 0))
    m['st_s5_im'] = f(inp['state_s5_im'][0, sl].reshape(NS, 64, 128).transpose(2, 1, 0))
    m['rw_mu'] = f(_ct(inp['rwkv_mu'][0], 8))
    m['rw_mu'] = f(m['rw_mu'].transpose(0, 2, 1))
    for n_ in ('w_r', 'w_k', 'w_v', 'w_g', 'w_o'):
        m['rw_' + n_] = f(inp['rwkv_' + n_][0])
    m['rw_w1'] = f(inp['rwkv_w1'][0]); m['rw_w2'] = f(inp['rwkv_w2'][0])
    m['rw_a1'] = f(inp['rwkv_a1'][0]); m['rw_a2'] = f(inp['rwkv_a2'][0])
    vecs = np.stack([inp['rwkv_w0'][0], inp['rwkv_a0'][0], inp['rwkv_k_k'][0], inp['rwkv_k_a'][0], inp['rwkv_r_k'][0]])
    m['rw_vecs'] = f(_ct(vecs, 16).transpose(0, 2, 1))
    m['rw_lnw'] = f(np.tile(inp['rwkv_ln_w'][0][None, :], (128, 1)))
    m['rw_lnb'] = f(np.tile(inp['rwkv_ln_b'][0][None, :], (128, 1)))
    ii = np.arange(128)[:, None]; jj = np.arange(128)[None, :]
    same_ = (ii // TS) == (jj // TS)
    msk = np.zeros((128, 2, 3, 128), np.float32)
    msk[:, 0, 0] = ii < jj; msk[:, 0, 1] = jj < ii; msk[:, 0, 2] = ii <= jj
    msk[:, 1, 0] = (ii < jj) & same_; msk[:, 1, 1] = (jj < ii) & same_; msk[:, 1, 2] = (ii <= jj) & same_
    m['rw_masks'] = msk
    rs_ = np.ones((128, 128), np.float32); rs_[:, ::TS] = 0.0
    m['rw_reset'] = rs_
    sel = np.zeros((128, 2), np.float32); sel[:64, 0] = 1.0; sel[64:, 1] = 1.0
    m['rw_sel'] = sel
    m['rw_oh'] = f(np.arange(128)[:, None] // TS == np.arange(NS)[None, :])
    m['st_rw_shift'] = f(_ct(inp['state_rwkv_shift'][0, sl], 8))
    wkv = inp['state_rwkv_wkv'][0, sl].reshape(NS, 16, 2, 64, 64)
    m['st_rw_wkv'] = f(wkv.transpose(1, 2, 4, 0, 3).reshape(16, 128, NS, 64))
    for n_ in ('w_q', 'w_k', 'w_v', 'w_g', 'w_o'):
        m['ret_' + n_] = f(inp['ret_' + n_][0])
    half = 128
    inv = (np.float32(10000.0) ** (-(np.arange(half, dtype=np.float32) / np.float32(half)))).astype(np.float32)
    pos = np.concatenate([np.arange(PT), np.tile(16384 + np.arange(TS), NS)]).astype(np.float32)
    ang = (pos[None, :] * inv[:, None]).astype(np.float32).astype(np.float64)
    m['ret_cos'] = f(np.cos(ang))
    m['ret_sin'] = f(np.sin(ang))
    logg = np.log1p(-np.exp2(-5.0 - np.arange(4, dtype=np.float64)))
    mm_ = np.arange(128)[:, None]; ll = np.arange(128)[None, :]
    mk = np.zeros((128, 2, 4, 128)); grow = np.zeros((128, 2, 4, 128)); gcol = np.zeros((128, 3, 4))
    for h in range(4):
        dlt = ll - mm_
        mk[:, 0, h, :] = np.where(dlt >= 0, np.exp(dlt * logg[h]), 0.0)
        same = (mm_ // TS) == (ll // TS)
        mk[:, 1, h, :] = np.where((dlt >= 0) & same, np.exp(dlt * logg[h]), 0.0)
        grow[:, 0, h, :] = np.exp((np.arange(128) + 1.0) * logg[h])[None, :]
        grow[:, 1, h, :] = np.exp(((np.arange(128) % TS) + 1.0) * logg[h])[None, :]
        gcol[:, 0, h] = np.exp((127.0 - np.arange(128)) * logg[h])
        gcol[:, 1, h] = np.exp((15.0 - np.arange(128)) * logg[h])
        gcol[:, 2, h] = np.exp((TS - 1.0 - (np.arange(128) % TS)) * logg[h])
    m['ret_mk'], m['ret_grow'], m['ret_gcol'] = f(mk), f(grow), f(gcol)
    onehot = (np.arange(128)[:, None] // TS == np.arange(NS)[None, :]).astype(np.float64)
    m['ret_sc'] = f(onehot[:, :, None] * gcol[:, None, 2, :])
    m['ret_cmask'] = f(np.tile(onehot.T[None, :, :], (128, 1, 1)))
    m['st_ret'] = f(inp['state_ret'][0, sl])
    return m


def _unct(a):
    a = np.asarray(a)
    nd = a.ndim
    perm = tuple(range(2, nd)) + (1, 0)
    a = a.transpose(perm)
    return a.reshape(a.shape[:-2] + (a.shape[-2] * 128,))


def gather(results):
    n = len(results)
    y_prompt = np.zeros((n, SEQ, D), np.float32)
    y_sample = np.zeros((n * NS, TS, D), np.float32)
    p_lru_conv = np.zeros((1, n, 3, E), np.float32)
    p_lru_h = np.zeros((1, n, E), np.float32)
    s_lru_conv = np.zeros((1, n * NS, 3, E), np.float32)
    s_lru_h = np.zeros((1, n * NS, E), np.float32)
    p_s5_re = np.zeros((1, n, 128, 64), np.float32)
    p_s5_im = np.zeros((1, n, 128, 64), np.float32)
    s_s5_re = np.zeros((1, n * NS, 128, 64), np.float32)
    s_s5_im = np.zeros((1, n * NS, 128, 64), np.float32)
    p_rwkv_shift = np.zeros((1, n, D), np.float32)
    s_rwkv_shift = np.zeros((1, n * NS, D), np.float32)
    p_rwkv_wkv = np.zeros((1, n, 32, 64, 64), np.float32)
    s_rwkv_wkv = np.zeros((1, n * NS, 32, 64, 64), np.float32)
    p_ret = np.zeros((1, n, 4, 256, 512), np.float32)
    s_ret = np.zeros((1, n * NS, 4, 256, 512), np.float32)
    for c, r in enumerate(results):
        yo = r['yout']
        y_prompt[c] = yo[:, :SEQ].T
        y_sample[c * NS:(c + 1) * NS] = yo[:, SEQ:].T.reshape(NS, TS, D)
        sl = slice(c * NS, (c + 1) * NS)
        p_lru_conv[0, c] = _unct(r['o_p_lru_conv'])
        p_lru_h[0, c] = _unct(r['o_p_lru_h'])
        s_lru_conv[0, sl] = _unct(r['o_s_lru_conv'])
        s_lru_h[0, sl] = _unct(r['o_s_lru_h'])
        if 'o_p_rw_shift' in r:
            p_rwkv_shift[0, c] = _unct(r['o_p_rw_shift'])
            s_rwkv_shift[0, sl] = _unct(r['o_s_rw_shift'])
            p_rwkv_wkv[0, c] = r['o_p_rw_wkv'].reshape(2, 64, 16, 64).transpose(2, 0, 3, 1).reshape(32, 64, 64)
            s_rwkv_wkv[0, sl] = r['o_s_rw_wkv'].reshape(16, 2, 64, NS, 64).transpose(3, 0, 1, 4, 2).reshape(NS, 32, 64, 64)
        if 'o_p_ret' in r:
            p_ret[0, c] = r['o_p_ret']
            s_ret[0, sl] = r['o_s_ret']
        if 'o_p_s5_re' in r:
            p_s5_re[0, c] = r['o_p_s5_re'].T.reshape(128, 64)
            p_s5_im[0, c] = r['o_p_s5_im'].T.reshape(128, 64)
            s_s5_re[0, sl] = r['o_s_s5_re'].transpose(2, 1, 0).reshape(NS, 128, 64)
            s_s5_im[0, sl] = r['o_s_s5_im'].transpose(2, 1, 0).reshape(NS, 128, 64)
    return dict(y_prompt=y_prompt, y_sample=y_sample, p_lru_conv=p_lru_conv, p_lru_h=p_lru_h,
                s_lru_conv=s_lru_conv, s_lru_h=s_lru_h,
                p_s5_re=p_s5_re, p_s5_im=p_s5_im, s_s5_re=s_s5_re, s_s5_im=s_s5_im,
                p_rwkv_shift=p_rwkv_shift, p_rwkv_wkv=p_rwkv_wkv,
                s_rwkv_shift=s_rwkv_shift, s_rwkv_wkv=s_rwkv_wkv, p_ret=p_ret, s_ret=s_ret)


OUT_ORDER = ("y_prompt", "y_sample", "p_lru_conv", "p_lru_h", "p_s5_re", "p_s5_im", "p_rwkv_shift",
             "p_rwkv_wkv", "p_ret", "s_lru_conv", "s_lru_h", "s_s5_re", "s_s5_im", "s_rwkv_shift",
             "s_rwkv_wkv", "s_ret")


def kernel(**inputs):
    inputs = {k_: np.asarray(v) for k_, v in inputs.items()}
    P = build((0, 1, 2, 3))
    in_maps = []
    shared = None
    for c in range(NCORES):
        m = prep_core(inputs, c)
        if shared is None:
            shared = m
        else:
            for k_ in list(m.keys()):
                if not (k_.startswith("st_") or k_ == "hin"):
                    m[k_] = shared[k_]
        in_maps.append(m)
    res = run_bass_kernel_spmd(P.nc, in_maps, core_ids=list(range(NCORES)))
    g = gather(res.results)
    return tuple(g[n] for n in OUT_ORDER)
```

```python
import contextlib
import numpy as np
import concourse.bass as bass
import concourse.mybir as mybir
from concourse.bass_utils import run_bass_kernel_spmd

F32 = mybir.dt.float32
BF16 = mybir.dt.bfloat16
ALU = mybir.AluOpType
AF = mybir.ActivationFunctionType
AX = mybir.AxisListType

NCORES = 8
D = 1024
E = 2048
NMETA = 16
SEQ = 2048
PT = NMETA + SEQ
NS = 16
TS = 8
WS = NS * TS
WALL = PT + WS
WMAX = 768
SEGS = [dict(p0=0, Wp=768, Ws=0), dict(p0=768, Wp=768, Ws=0), dict(p0=1536, Wp=528, Ws=WS)]
PI = float(np.pi)


def ctiles(W):
    res = []
    c = 0
    while c < W:
        n = min(512, W - c)
        res.append((c, n))
        c += n
    return res


class KB:
    EPOCH = 30000
    NEAR = 4

    def __init__(self, nc, es):
        self.nc, self.es = nc, es
        self.engs = dict(pe=nc.tensor, act=nc.scalar, dve=nc.vector, pool=nc.gpsimd, sp=nc.sync)
        self.semobj = {}
        self.cnt = {e: 0 for e in self.engs}
        self.epoch = {e: 0 for e in self.engs}
        self.nsem = 0
        for e in self.engs:
            self.semobj[(e, 0)] = self._newsem()
        self.seen = {e: {} for e in self.engs}
        self.ndma = 24
        self.dma_tgt = [0] * self.ndma
        self.dma_ep = [0] * self.ndma
        for i in range(self.ndma):
            self.semobj[('dma', i, 0)] = self._newsem()
        self.dma_slots = {'pool': (0, 4), 'sp': (4, self.ndma)}
        self.dma_rrq = {'pool': 0, 'sp': 4}
        self.lastw = {}
        self.readers = {}
        self.ninstr = 0

    def _newsem(self):
        self.nsem += 1
        return self.es.enter_context(self.nc.semaphore(f"sem{self.nsem}"))

    def _wait(self, e, sk, c, raw=False):
        if sk[0] == e:
            if not (raw and e != 'pe' and sk[1] == self.epoch[e] and self.cnt[e] - c < self.NEAR):
                return
        if self.seen[e].get(sk, 0) >= c:
            return
        self.engs[e].wait_ge(self.semobj[sk], c)
        self.seen[e][sk] = c

    @staticmethod
    def _key(r):
        sub = None
        if isinstance(r, tuple):
            r, sub = r
        if not isinstance(r, str):
            n = r.name
            r = n() if callable(n) else n
        return r, sub

    def _collect(self, reads, writes):
        toks = []
        for r in reads:
            n, s = self._key(r)
            for s2, t in self.lastw.get(n, {}).items():
                if s is None or s2 is None or s == s2:
                    toks.append((t, True))
        for w in writes:
            n, s = self._key(w)
            for s2, t in self.lastw.get(n, {}).items():
                if s is None or s2 is None or s == s2:
                    toks.append((t, False))
            for s2, d in self.readers.get(n, {}).items():
                if s is None or s2 is None or s == s2:
                    toks.extend((t, False) for t in d.values())
        return toks

    def _record(self, reads, writes, tok, who):
        for r in reads:
            n, s = self._key(r)
            self.readers.setdefault(n, {}).setdefault(s, {})[who] = tok
        for w in writes:
            n, s = self._key(w)
            lw = self.lastw.setdefault(n, {})
            rd = self.readers.setdefault(n, {})
            if s is None:
                lw.clear()
                rd.clear()
            else:
                rd.pop(s, None)
            lw[s] = tok

    def op(self, e, fn, r=(), w=()):
        for ((sk, c), raw) in self._collect(r, w):
            self._wait(e, sk, c, raw)
        ins = fn(self.engs[e])
        if self.cnt[e] >= self.EPOCH:
            self.epoch[e] += 1
            self.cnt[e] = 0
            self.semobj[(e, self.epoch[e])] = self._newsem()
        sk = (e, self.epoch[e])
        self.cnt[e] += 1
        ins.then_inc(self.semobj[sk], 1)
        tok = (sk, self.cnt[e])
        self._record(r, w, tok, e)
        self.ninstr += 1
        return tok

    def dma(self, q, out, in_, r=None, w=None, **kw):
        r = [in_] if r is None else r
        w = [out] if w is None else w
        for ((sk, c), raw) in self._collect(r, w):
            self._wait(q, sk, c)
        lo, hi = self.dma_slots[q]
        i = self.dma_rrq[q]
        self.dma_rrq[q] = lo + (i + 1 - lo) % (hi - lo)
        sk = ('dma', i, self.dma_ep[i])
        if self.dma_tgt[i] > 0:
            self._wait(q, sk, self.dma_tgt[i])
        if self.dma_tgt[i] >= self.EPOCH:
            self.dma_ep[i] += 1
            self.dma_tgt[i] = 0
            sk = ('dma', i, self.dma_ep[i])
            self.semobj[sk] = self._newsem()
        ins = self.engs[q].dma_start(out=out, in_=in_, **kw)
        self.dma_tgt[i] += 16
        ins.then_inc(self.semobj[sk], 16)
        tok = (sk, self.dma_tgt[i])
        self._record(r, w, tok, ('dma', i))
        self.ninstr += 1
        return tok

    def barrier(self):
        for e in self.engs:
            for e2 in self.engs:
                if e2 != e and self.cnt[e2] > 0:
                    self._wait(e, (e2, self.epoch[e2]), self.cnt[e2])
            for i in range(self.ndma):
                if self.dma_tgt[i] > 0:
                    self._wait(e, ('dma', i, self.dma_ep[i]), self.dma_tgt[i])

    def finish(self):
        e = 'sp'
        for i in range(self.ndma):
            if self.dma_tgt[i] > 0:
                self._wait(e, ('dma', i, self.dma_ep[i]), self.dma_tgt[i])
        for e2 in self.engs:
            if e2 != e and self.cnt[e2] > 0:
                self._wait(e, (e2, self.epoch[e2]), self.cnt[e2])

    def mm(self, ps, lhsT, rhs, start=True, stop=True, r=None, w=None):
        return self.op('pe', lambda e: e.matmul(ps, lhsT, rhs, start=start, stop=stop),
                       r=r if r is not None else [lhsT, rhs], w=w if w is not None else [ps])

    def tr(self, ps, in_, ident, r=None, w=None):
        return self.op('pe', lambda e: e.transpose(ps, in_, ident),
                       r=r if r is not None else [in_, ident], w=w if w is not None else [ps])

    def act(self, func, out, in_, scale=1.0, bias=None, accum_out=None, r=None, w=None, eng='act'):
        rr = [in_]
        kw = {}
        if bias is not None:
            kw['bias'] = bias
            if not isinstance(bias, (int, float)):
                rr.append(bias)
        if not isinstance(scale, (int, float)):
            rr.append(scale)
        ww = [out]
        if accum_out is not None:
            kw['accum_out'] = accum_out
            ww.append(accum_out)
        return self.op(eng, lambda e: e.activation(out=out, in_=in_, func=func, scale=scale, **kw),
                       r=r if r is not None else rr, w=w if w is not None else ww)

    def tt(self, out, in0, in1, op, r=None, w=None, eng='dve'):
        return self.op(eng, lambda e: e.tensor_tensor(out=out, in0=in0, in1=in1, op=op),
                       r=r if r is not None else [in0, in1], w=w if w is not None else [out])

    def ts(self, out, in0, s1, op0, s2=None, op1=None, r=None, w=None, eng='dve'):
        rr = [in0] + [s for s in (s1, s2) if s is not None and not isinstance(s, (int, float))]
        if s2 is None:
            f = lambda e: e.tensor_scalar(out=out, in0=in0, scalar1=s1, scalar2=None, op0=op0)
        else:
            f = lambda e: e.tensor_scalar(out=out, in0=in0, scalar1=s1, scalar2=s2, op0=op0, op1=op1)
        return self.op(eng, f, r=r if r is not None else rr, w=w if w is not None else [out])

    def stt(self, out, in0, scalar, in1, op0, op1, r=None, w=None, eng='dve'):
        rr = [in0, in1] + ([scalar] if not isinstance(scalar, (int, float)) else [])
        return self.op(eng, lambda e: e.scalar_tensor_tensor(out=out, in0=in0, scalar=scalar, in1=in1,
                                                             op0=op0, op1=op1),
                       r=r if r is not None else rr, w=w if w is not None else [out])

    def scan(self, out, d0, d1, init, op0=None, op1=None, r=None, w=None):
        op0 = op0 or ALU.mult
        op1 = op1 or ALU.add
        rr = [d0, d1] + ([init] if not isinstance(init, (int, float)) else [])
        return self.op('dve', lambda e: e.tensor_tensor_scan(out=out, data0=d0, data1=d1, initial=init,
                                                             op0=op0, op1=op1),
                       r=r if r is not None else rr, w=w if w is not None else [out])

    def copy(self, out, in_, eng='dve', r=None, w=None):
        if eng == 'act':
            return self.act(AF.Identity, out, in_, r=r, w=w)
        return self.op(eng, lambda e: e.tensor_copy(out=out, in_=in_),
                       r=r if r is not None else [in_], w=w if w is not None else [out])

    def memset(self, out, val, eng='dve', w=None):
        return self.op(eng, lambda e: e.memset(out, val), r=[], w=w if w is not None else [out])

    def recip(self, out, in_, r=None, w=None):
        return self.op('dve', lambda e: e.reciprocal(out=out, in_=in_),
                       r=r if r is not None else [in_], w=w if w is not None else [out])


class Prog:
    pass


def build(order=(0, 1, 2, 3)):
    nc = bass.Bass("TRN2", target_bir_lowering=False)
    es = contextlib.ExitStack()
    P = Prog()
    P.nc = nc
    with es:
        k = KB(nc, es)
        P.k = k

        def din(name, shape):
            return nc.dram_tensor(name, list(shape), F32, kind="ExternalInput").ap()

        def dout(name, shape):
            return nc.dram_tensor(name, list(shape), F32, kind="ExternalOutput").ap()

        P.uid = 0
        P.scope = es

        def sb(name, shape, dt=F32):
            P.uid += 1
            return P.scope.enter_context(nc.sbuf_tensor(f"{name}_u{P.uid}", list(shape), dt))

        @contextlib.contextmanager
        def layer_scope():
            old = P.scope
            with contextlib.ExitStack() as ls:
                P.scope = ls
                yield
                k.barrier()
            P.scope = old

        hin = din("hin", [D, WALL])
        yout = dout("yout", [D, SEQ + WS])
        normpre = din("normpre", [128, 4, 8])
        normpost = din("normpost", [128, 4, 8])
        ident_d = din("ident", [128, 128])
        lru_w_in = din("lru_w_in", [D, 2 * E])
        lru_cw = din("lru_cw", [128, 16, 4])
        lru_cb = din("lru_cb", [128, 16])
        lru_wa = din("lru_wa", [16, 128, 128])
        lru_wx = din("lru_wx", [16, 128, 128])
        lru_ba = din("lru_ba", [128, 16])
        lru_bx = din("lru_bx", [128, 16])
        lru_lam = din("lru_lam", [128, 16])
        lru_w_out = din("lru_w_out", [E, D])
        st_lru_conv = din("st_lru_conv", [128, 16, NS, 3])
        st_lru_h = din("st_lru_h", [128, 16, NS])
        o_p_lru_conv = dout("o_p_lru_conv", [128, 16, 3])
        o_p_lru_h = dout("o_p_lru_h", [128, 16])
        o_s_lru_conv = dout("o_s_lru_conv", [128, 16, NS, 3])
        o_s_lru_h = dout("o_s_lru_h", [128, 16, NS])

        s5_w_in = din("s5_w_in", [D, 2 * E])
        s5_are = din("s5_are", [128, 64])
        s5_aim = din("s5_aim", [128, 64])
        s5_ldt = din("s5_ldt", [128, 64])
        s5_bre = din("s5_bre", [16, 128, 4, 128])
        s5_bim = din("s5_bim", [16, 128, 4, 128])
        s5_cre = din("s5_cre", [16, 128, 4, 128])
        s5_cim = din("s5_cim", [16, 128, 4, 128])
        s5_d = din("s5_d", [128, 16])
        s5_glu_w = din("s5_glu_w", [E, E])
        s5_glu_b = din("s5_glu_b", [128, 16])
        s5_w_out = din("s5_w_out", [E, D])
        s5_iota = din("s5_iota", [128, 256])
        st_s5_re = din("st_s5_re", [128, 64, NS])
        st_s5_im = din("st_s5_im", [128, 64, NS])
        o_p_s5_re = dout("o_p_s5_re", [128, 64])
        o_p_s5_im = dout("o_p_s5_im", [128, 64])
        o_s_s5_re = dout("o_s_s5_re", [128, 64, NS])
        o_s_s5_im = dout("o_s_s5_im", [128, 64, NS])

        rw_mu = din("rw_mu", [128, 6, 8])
        rw_w_r = din("rw_w_r", [D, E])
        rw_w_k = din("rw_w_k", [D, E])
        rw_w_v = din("rw_w_v", [D, E])
        rw_w_g = din("rw_w_g", [D, E])
        rw_w_o = din("rw_w_o", [E, D])
        rw_w1 = din("rw_w1", [D, 64])
        rw_w2 = din("rw_w2", [64, E])
        rw_a1 = din("rw_a1", [D, 64])
        rw_a2 = din("rw_a2", [64, E])
        rw_vecs = din("rw_vecs", [128, 5, 16])
        rw_lnw = din("rw_lnw", [128, E])
        rw_lnb = din("rw_lnb", [128, E])
        rw_masks = din("rw_masks", [128, 2, 3, 128])
        rw_reset = din("rw_reset", [128, 128])
        rw_sel = din("rw_sel", [128, 2])
        rw_oh = din("rw_oh", [128, NS])
        st_rw_shift = din("st_rw_shift", [128, 8, NS])
        st_rw_wkv = din("st_rw_wkv", [16, 128, NS, 64])
        o_p_rw_shift = dout("o_p_rw_shift", [128, 8])
        o_s_rw_shift = dout("o_s_rw_shift", [128, 8, NS])
        o_p_rw_wkv = dout("o_p_rw_wkv", [128, 16, 64])
        o_s_rw_wkv = dout("o_s_rw_wkv", [16, 128, NS, 64])
        ret_w_q = din("ret_w_q", [D, D])
        ret_w_k = din("ret_w_k", [D, D])
        ret_w_v = din("ret_w_v", [D, E])
        ret_w_g = din("ret_w_g", [D, E])
        ret_w_o = din("ret_w_o", [E, D])
        ret_cos = din("ret_cos", [128, WALL])
        ret_sin = din("ret_sin", [128, WALL])
        ret_mk = din("ret_mk", [128, 2, 4, 128])
        ret_grow = din("ret_grow", [128, 2, 4, 128])
        ret_gcol = din("ret_gcol", [128, 3, 4])
        ret_sc = din("ret_sc", [128, NS, 4])
        ret_cmask = din("ret_cmask", [128, NS, 128])
        st_ret = din("st_ret", [NS, 4, 256, 512])
        o_p_ret = dout("o_p_ret", [4, 256, 512])
        o_s_ret = dout("o_s_ret", [NS, 4, 256, 512])

        H = sb("H", [128, 8, WMAX])
        XN = sb("XN", [128, 8, WMAX], BF16)
        BIGA = sb("BIGA", [128, 16, WMAX], BF16)
        HB = H[:, :, :].bitcast(BF16).rearrange("p a (b c) -> p (a b) c", c=WMAX)
        hspill = nc.dram_tensor("hspill", [128, 8, WMAX], F32, kind="Internal").ap()
        RS = sb("RS", [128, WMAX])
        ONES = sb("ONES", [128, 128], BF16)
        CONST = sb("CONST", [128, 8])
        NPRE = sb("NPRE", [128, 4, 8])
        NPOST = sb("NPOST", [128, 4, 8])
        WI = [sb(f"WI{i}", [128, 2, 8, 128], BF16) for i in range(2)]

        def mk_tf(n=9):
            return [[sb(f"TF{p}_{i}", [128, WMAX + 16]) for i in range(n)] for p in range(2)]

        def mk_tb():
            return [sb(f"TB{p}", [128, WMAX], BF16) for p in range(2)]
        PS = [es.enter_context(nc.psum_tensor(f"PS{i}", [128, 512], F32)) for i in range(8)]

        EPS_AP = CONST[:, 0:1]
        ONE_AP = CONST[:, 1:2]
        k.memset(CONST[:, 0:1], 1e-6)
        k.memset(CONST[:, 1:2], 1.0)
        k.memset(CONST[:, 2:3], -PI)
        NPI_AP = CONST[:, 2:3]
        k.memset(ONES[:, :], 1.0)
        k.dma('sp', NPRE[:, :, :], normpre[:, :, :])
        k.dma('sp', NPOST[:, :, :], normpost[:, :, :])

        hin_v = hin.rearrange("(kt p) c -> p kt c", p=128)
        yout_v = yout.rearrange("(kt p) c -> p kt c", p=128)

        def pre_norm(li, seg):
            W = seg['W']
            cts_ = ctiles(W)
            for ti, (c0, n) in enumerate(cts_):
                for kt in range(8):
                    k.act(AF.Square, XN[:, kt, c0:c0 + n], H[:, kt, c0:c0 + n])
            for ti, (c0, n) in enumerate(cts_):
                ps = PS[6 + ti % 2]
                for kt in range(8):
                    k.mm(ps[:, :n], ONES[:, :], XN[:, kt, c0:c0 + n], start=(kt == 0), stop=(kt == 7))
            for ti, (c0, n) in enumerate(cts_):
                ps = PS[6 + ti % 2]
                k.act(AF.Sqrt, RS[:, c0:c0 + n], ps[:, :n], scale=1.0 / D, bias=EPS_AP)
                k.recip(RS[:, c0:c0 + n], RS[:, c0:c0 + n])
            for ti, (c0, n) in enumerate(cts_):
                for kt in range(8):
                    k.stt(XN[:, kt, c0:c0 + n], H[:, kt, c0:c0 + n], NPRE[:, li, kt:kt + 1],
                          RS[:, c0:c0 + n], ALU.mult, ALU.mult)

        def out_proj(li, seg, Y, w_out, ZTd):
            W = seg['W']
            wv = w_out.rearrange("(kt p) n -> p kt n", p=128)
            cts = ctiles(W)
            ZSQ = [sb(f"ZSQ{i}", [128, 512], BF16) for i in range(2)]
            WO = [sb(f"WO{i}", [128, 16, 128], BF16) for i in range(2)]
            pending = None
            for d in range(8):
                wo = WO[d % 2]
                k.dma('pool', wo[:, :, :], wv[:, :, d * 128:(d + 1) * 128])
                for ti, (c0, n) in enumerate(cts):
                    ps = PS[(2 * d + ti) % 6]
                    for kt in range(16):
                        k.mm(ps[:, :n], wo[:, kt, :], Y[:, kt, c0:c0 + n], start=(kt == 0), stop=(kt == 15))
                    if pending is not None:
                        pending()
                    zs = ZSQ[(2 * d + ti) % 2]
                    k.act(AF.Identity, ZTd[d][:, c0:c0 + n], ps[:, :n])
                    k.act(AF.Square, zs[:, :n], ps[:, :n])
                    pending = (lambda ti=ti, n=n, zs=zs, d=d:
                               k.mm(PS[6 + ti][:, :n], ONES[:, :], zs[:, :n], start=(d == 0), stop=(d == 7)))
            pending()
            k.dma('sp', H[:, :, 0:W], hspill[:, :, 0:W])
            for ti, (c0, n) in enumerate(cts):
                k.act(AF.Sqrt, RS[:, c0:c0 + n], PS[6 + ti][:, :n], scale=1.0 / D, bias=EPS_AP)
                k.recip(RS[:, c0:c0 + n], RS[:, c0:c0 + n])
                for d in range(8):
                    k.stt(ZTd[d][:, c0:c0 + n], ZTd[d][:, c0:c0 + n], NPOST[:, li, d:d + 1],
                          RS[:, c0:c0 + n], ALU.mult, ALU.mult)
                    k.tt(H[:, d, c0:c0 + n], H[:, d, c0:c0 + n], ZTd[d][:, c0:c0 + n], ALU.add)

        LRUCONV = sb("LRUCONV", [128, 16, 3])
        LRUH = sb("LRUH", [128, 16])
        LCW = sb("LCW", [128, 16, 4])
        LCB = sb("LCB", [128, 16])
        LBA = sb("LBA", [128, 16])
        LBX = sb("LBX", [128, 16])
        LSP = sb("LSP", [128, 16])
        LSP8 = sb("LSP8", [128, 16])
        LSP16 = sb("LSP16", [128, 16])
        TMP16 = sb("TMP16", [128, NS])
        TMP16B = sb("TMP16B", [128, NS])

        def lru_setup():
            k.dma('sp', LCW[:, :, :], lru_cw[:, :, :])
            k.dma('sp', LCB[:, :], lru_cb[:, :])
            k.dma('sp', LBA[:, :], lru_ba[:, :])
            k.dma('sp', LBX[:, :], lru_bx[:, :])
            k.dma('sp', LSP[:, :], lru_lam[:, :])
            k.act(AF.Exp, LSP[:, :], LSP[:, :], scale=-1.0)
            k.act(AF.Ln, LSP[:, :], LSP[:, :], bias=ONE_AP)
            k.ts(LSP8[:, :], LSP[:, :], -8.0, ALU.mult)
            k.ts(LSP16[:, :], LSP[:, :], -16.0, ALU.mult)

        def lru_layer(li, seg, si):
            W, Wp, Ws = seg['W'], seg['Wp'], seg['Ws']
            cts = ctiles(W)
            wv = lru_w_in.rearrange("(kt p) n -> p kt n", p=128)
            Y = BIGA
            TF = mk_tf(9)
            TBh = mk_tb()
            UES = [sb(f"UES{i}", [128, NS, 3 + TS]) for i in range(2)]
            WAB = [sb(f"WAB{i}", [128, 2, 128], BF16) for i in range(2)]
            if Ws:
                STC = sb("STC", [128, 16, NS, 3])
                STH = sb("STH", [128, 16, NS])
                OSC = sb("OSC", [128, 16, NS, 3])
                OSH = sb("OSH", [128, 16, NS])
                k.dma('sp', STC[:, :, :, :], st_lru_conv[:, :, :, :])
                k.dma('sp', STH[:, :, :], st_lru_h[:, :, :])
            def lru_tile(j):
                par = j % 2
                UE, XC, GR, GI, A, S, BX, HS, SG = TF[par]
                XCB = TBh[par]
                ues = UES[par]
                wi = WI[par]
                wab = WAB[par]
                k.dma('pool', wi[:, 0, :, :], wv[:, :, j * 128:(j + 1) * 128])
                k.dma('pool', wi[:, 1, :, :], wv[:, :, E + j * 128:E + (j + 1) * 128])
                k.dma('pool', wab[:, 0, :], lru_wa[j, :, :])
                k.dma('pool', wab[:, 1, :], lru_wx[j, :, :])
                if si == 0:
                    k.memset(UE[:, 0:3], 0.0)
                else:
                    k.copy(UE[:, 0:3], LRUCONV[:, j, :])
                if Ws:
                    k.copy(ues[:, :, 0:3], STC[:, j, :, :])
                for ti, (c0, n) in enumerate(cts):
                    for kt in range(8):
                        k.mm(PS[ti][:, :n], wi[:, 0, kt, :], XN[:, kt, c0:c0 + n], start=(kt == 0), stop=(kt == 7))
                    for kt in range(8):
                        k.mm(PS[2 + ti][:, :n], wi[:, 1, kt, :], XN[:, kt, c0:c0 + n], start=(kt == 0), stop=(kt == 7))
                    npr = min(c0 + n, Wp) - c0
                    if npr > 0:
                        k.act(AF.Identity, UE[:, 3 + c0:3 + c0 + npr], PS[ti][:, :npr])
                    if c0 + n > Wp:
                        s0 = max(c0, Wp)
                        ns_ = c0 + n - s0
                        q0 = (s0 - Wp) // TS
                        k.act(AF.Identity, ues[:, q0:q0 + ns_ // TS, 3:3 + TS],
                              PS[ti][:, s0 - c0:s0 - c0 + ns_].rearrange("p (s t) -> p s t", t=TS))
                    k.act(AF.Silu, SG[:, c0:c0 + n], PS[2 + ti][:, :n])
                    yield
                k.ts(XC[:, 0:Wp], UE[:, 3:3 + Wp], LCW[:, j, 3:4], ALU.mult, LCB[:, j:j + 1], ALU.add)
                for c in (2, 1, 0):
                    k.stt(XC[:, 0:Wp], UE[:, c:c + Wp], LCW[:, j, c:c + 1], XC[:, 0:Wp], ALU.mult, ALU.add)
                if Ws:
                    XC3 = XC[:, Wp:W].rearrange("p (s t) -> p s t", t=TS)
                    k.ts(XC3, ues[:, :, 3:3 + TS], LCW[:, j, 3:4], ALU.mult, LCB[:, j:j + 1], ALU.add)
                    for c in (2, 1, 0):
                        k.stt(XC3, ues[:, :, c:c + TS], LCW[:, j, c:c + 1], XC3, ALU.mult, ALU.add)
                yield
                k.act(AF.Identity, XCB[:, 0:W], XC[:, 0:W])
                yield
                for ti, (c0, n) in enumerate(cts):
                    k.mm(PS[4 + ti][:, :n], wab[:, 0, :], XCB[:, c0:c0 + n])
                    k.mm(PS[6 + ti][:, :n], wab[:, 1, :], XCB[:, c0:c0 + n])
                    k.act(AF.Sigmoid, GR[:, c0:c0 + n], PS[4 + ti][:, :n], bias=LBA[:, j:j + 1])
                    k.act(AF.Sigmoid, GI[:, c0:c0 + n], PS[6 + ti][:, :n], bias=LBX[:, j:j + 1])
                yield
                k.act(AF.Exp, A[:, 0:W], GR[:, 0:W], scale=LSP8[:, j:j + 1])
                k.act(AF.Exp, S[:, 0:W], GR[:, 0:W], scale=LSP16[:, j:j + 1])
                k.act(AF.Sqrt, S[:, 0:W], S[:, 0:W], scale=-1.0, bias=ONE_AP)
                yield
                k.tt(BX[:, 0:W], S[:, 0:W], GI[:, 0:W], ALU.mult)
                k.tt(BX[:, 0:W], BX[:, 0:W], XC[:, 0:W], ALU.mult)
                k.scan(HS[:, 0:Wp], A[:, 0:Wp], BX[:, 0:Wp], 0.0 if si == 0 else LRUH[:, j:j + 1])
                if Ws:
                    A3 = A[:, Wp:W].rearrange("p (s t) -> p s t", t=TS)
                    BX3 = BX[:, Wp:W].rearrange("p (s t) -> p s t", t=TS)
                    HS3 = HS[:, Wp:W].rearrange("p (s t) -> p s t", t=TS)
                    for q in range(NS):
                        c1 = Wp + q * TS
                        k.scan(HS[:, c1:c1 + TS], A[:, c1:c1 + TS], BX[:, c1:c1 + TS], STH[:, j, q:q + 1])
                yield
                k.tt(Y[:, j, 0:W], HS[:, 0:W], SG[:, 0:W], ALU.mult)
                k.copy(LRUCONV[:, j, :], UE[:, Wp:Wp + 3])
                k.copy(LRUH[:, j:j + 1], HS[:, Wp - 1:Wp])
                if Ws:
                    k.copy(OSC[:, j, :, :], ues[:, :, TS:TS + 3])
                    k.copy(OSH[:, j, :], HS3[:, :, TS - 1])
                yield

            for jp in range(8):
                gens = [lru_tile(2 * jp), lru_tile(2 * jp + 1)]
                alive = [True, True]
                while any(alive):
                    for gi in range(2):
                        if alive[gi] and next(gens[gi], 'end') == 'end':
                            alive[gi] = False
            out_proj(li, seg, Y, lru_w_out, TF[0][0:8])
            if si == len(SEGS) - 1:
                k.dma('sp', o_p_lru_conv[:, :, :], LRUCONV[:, :, :])
                k.dma('sp', o_p_lru_h[:, :], LRUH[:, :])
                k.dma('sp', o_s_lru_conv[:, :, :, :], OSC[:, :, :, :])
                k.dma('sp', o_s_lru_h[:, :, :], OSH[:, :, :])


        def sm(name, shape=(128, 64), dt=F32):
            return sb(name, list(shape), dt)
        S5ARE, S5AIM, S5DT, S5MAG, S5TH = (sm(n) for n in ("S5ARE", "S5AIM", "S5DT", "S5MAG", "S5TH"))
        S5FRE, S5FIM, S5NFRE, S5NFIM, S5FIR, S5FII = (sm(n) for n in ("S5FRE", "S5FIM", "S5NFRE", "S5NFIM", "S5FIR", "S5FII"))
        S5T = [sm(f"S5T{i}") for i in range(6)]
        S5R, S5I, OP5R, OP5I = (sm(n) for n in ("S5R", "S5I", "OP5R", "OP5I"))
        S5TI = sm("S5TI", (128, 64), mybir.dt.int32)
        IOTA = sm("IOTA", (128, 256))
        S5D = sm("S5D", (128, 16))
        S5GB = sm("S5GB", (128, 16))
        TWO_PI = 2.0 * PI

        I32 = mybir.dt.int32

        def sincos(outs, outc, ang, tmp, tmpi):
            k.ts(tmp, ang, 1.0 / TWO_PI, ALU.mult)
            k.copy(tmpi, tmp)
            k.copy(tmp, tmpi)
            k.stt(tmp, tmp, -TWO_PI, ang, ALU.mult, ALU.add)
            k.ts(outc, tmp, 0.5 * PI, ALU.add)
            k.ts(outs, tmp, PI, ALU.is_gt, -TWO_PI, ALU.mult)
            k.tt(tmp, tmp, outs, ALU.add)
            k.act(AF.Sin, outs, tmp)
            k.ts(tmp, outc, PI, ALU.is_gt, -TWO_PI, ALU.mult)
            k.tt(tmp, tmp, outc, ALU.add)
            k.act(AF.Sin, outc, tmp)

        RUN = 256
        s5tab = nc.dram_tensor("s5tab", [64, 128, 2, RUN], F32, kind="Internal").ap()
        s5cp = nc.dram_tensor("s5cp", [16, 3, 128, 4, 128], BF16, kind="Internal").ap()
        CLAST = sm("CLAST", (128, 64, 3, 3))

        def s5_setup():
            k.dma('sp', S5ARE[:, :], s5_are[:, :])
            k.dma('sp', S5AIM[:, :], s5_aim[:, :])
            k.dma('sp', S5DT[:, :], s5_ldt[:, :])
            k.dma('sp', IOTA[:, :], s5_iota[:, :])
            k.dma('sp', S5D[:, :], s5_d[:, :])
            k.dma('sp', S5GB[:, :], s5_glu_b[:, :])
            t0, t1, t2, t3, t4, t5 = S5T
            k.act(AF.Exp, S5DT[:, :], S5DT[:, :])
            k.tt(t0[:, :], S5DT[:, :], S5ARE[:, :], ALU.mult)
            k.act(AF.Exp, S5MAG[:, :], t0[:, :])
            k.tt(S5TH[:, :], S5DT[:, :], S5AIM[:, :], ALU.mult)
            sincos(t1[:, :], t2[:, :], S5TH[:, :], t0[:, :], S5TI[:, :])
            k.tt(t3[:, :], S5MAG[:, :], t2[:, :], ALU.mult)
            k.tt(t4[:, :], S5MAG[:, :], t1[:, :], ALU.mult)
            k.ts(t3[:, :], t3[:, :], -1.0, ALU.add)
            k.tt(t0[:, :], S5ARE[:, :], S5ARE[:, :], ALU.mult)
            k.tt(t1[:, :], S5AIM[:, :], S5AIM[:, :], ALU.mult)
            k.tt(t0[:, :], t0[:, :], t1[:, :], ALU.add)
            k.recip(t0[:, :], t0[:, :])
            k.tt(t1[:, :], t3[:, :], S5ARE[:, :], ALU.mult)
            k.tt(t2[:, :], t4[:, :], S5AIM[:, :], ALU.mult)
            k.tt(t1[:, :], t1[:, :], t2[:, :], ALU.add)
            k.tt(S5FRE[:, :], t1[:, :], t0[:, :], ALU.mult)
            k.tt(t1[:, :], t4[:, :], S5ARE[:, :], ALU.mult)
            k.tt(t2[:, :], t3[:, :], S5AIM[:, :], ALU.mult)
            k.tt(t1[:, :], t1[:, :], t2[:, :], ALU.subtract)
            k.tt(S5FIM[:, :], t1[:, :], t0[:, :], ALU.mult)
            k.ts(S5NFRE[:, :], S5FRE[:, :], -1.0, ALU.mult)
            k.ts(S5NFIM[:, :], S5FIM[:, :], -1.0, ALU.mult)
            k.tt(t0[:, :], S5FRE[:, :], S5FRE[:, :], ALU.mult)
            k.tt(t1[:, :], S5FIM[:, :], S5FIM[:, :], ALU.mult)
            k.tt(t0[:, :], t0[:, :], t1[:, :], ALU.add)
            k.recip(t0[:, :], t0[:, :])
            k.tt(S5FIR[:, :], S5FRE[:, :], t0[:, :], ALU.mult)
            k.tt(S5FII[:, :], S5NFIM[:, :], t0[:, :], ALU.mult)
            with layer_scope():
                TT_ = [sm(f"TTB{i}", (128, 2, RUN)) for i in range(2)]
                TG_ = [sm(f"TTG{i}", (128, RUN)) for i in range(2)]
                TM_ = [sm(f"TTM{i}", (128, RUN)) for i in range(2)]
                TI_ = [sm(f"TTI{i}", (128, RUN), mybir.dt.int32) for i in range(2)]
                for sg in range(64):
                    p_ = sg % 2
                    k.ts(TG_[p_][:, :], IOTA[:, :], S5TH[:, sg:sg + 1], ALU.mult)
                    sincos(TT_[p_][:, 1, :], TT_[p_][:, 0, :], TG_[p_][:, :], TM_[p_][:, :], TI_[p_][:, :])
                    k.dma('sp', s5tab[sg, :, :, :], TT_[p_][:, :, :])
                    for vi, col in enumerate((RUN - 1, 15, TS - 1)):
                        k.copy(CLAST[:, sg, vi, 0:2], TT_[p_][:, :, col])
                        k.ts(CLAST[:, sg, vi, 2:3], TT_[p_][:, 1, col:col + 1], -1.0, ALU.mult)

        def s5_layer(li, seg, si):
            W, Wp, Ws = seg['W'], seg['Wp'], seg['Ws']
            cts = ctiles(W)
            wv = s5_w_in.rearrange("(kt p) n -> p kt n", p=128)
            BIGB = HB
            FB = [sb(f"FB{i}", [128, WMAX + 16]) for i in range(8)]
            WRs = [[FB[0], FB[1]], [FB[2], FB[3]]]
            U32s = [FB[4], FB[5]]
            G1, Q1 = FB[6], FB[7]
            UBs = [sm(f"UB{i}", (128, WMAX), BF16) for i in range(2)]
            PB = [sm(f"PB{i}", (128, 2, WMAX), BF16) for i in range(2)]
            VB = [sm(f"VB{i}", (128, 2, WMAX), BF16) for i in range(2)]
            WB = [sm(f"WB{i}", (128, 2, WMAX), BF16) for i in range(2)]
            TP = [[sm(f"TP{i}_{q}", (128, WMAX), BF16) for q in range(4)] for i in range(2)]
            NCPR = [sm(f"NCPR{i}", (128, 4, 128), BF16) for i in range(2)]
            TT1 = [sm(f"TT1{i}", (128, WMAX), BF16) for i in range(2)]
            TT2 = [sm(f"TT2{i}", (128, WMAX), BF16) for i in range(2)]
            TCB = [sm(f"TCB{i}", (128, 2, RUN), BF16) for i in range(2)]
            BBR = [sm(f"BBR{i}", (128, 4, 128), BF16) for i in range(2)]
            BBI = [sm(f"BBI{i}", (128, 4, 128), BF16) for i in range(2)]
            CCR = [sm(f"CCR{i}", (128, 4, 128)) for i in range(2)]
            CCI = [sm(f"CCI{i}", (128, 4, 128)) for i in range(2)]
            CPR = [sm(f"CPR{i}", (128, 4, 128), BF16) for i in range(2)]
            CPI = [sm(f"CPI{i}", (128, 4, 128), BF16) for i in range(2)]
            CT1 = sm("CT1", (128, 128))
            XLS = [sm(f"XLS{i}", (128, 8, 4)) for i in range(2)]
            if Ws:
                X0R = sm("X0R", (128, 64, NS))
                X0I = sm("X0I", (128, 64, NS))
                OS5R = sm("OS5R", (128, 64, NS))
                OS5I = sm("OS5I", (128, 64, NS))
                X7 = [sm(f"X7{i}", (128, 2, NS)) for i in range(2)]
                RSTM = sm("RSTM", (128, WS))
                MAGR1 = sm("MAGR", (128, WS))
                MAGR = [MAGR1, MAGR1]
                k.dma('sp', RSTM[:, :], rw_reset[:, :])
                k.dma('sp', X0R[:, :, :], st_s5_re[:, :, :])
                k.dma('sp', X0I[:, :, :], st_s5_im[:, :, :])
                fir = S5FIR[:, :].unsqueeze(2).to_broadcast([128, 64, NS])
                fii = S5FII[:, :].unsqueeze(2).to_broadcast([128, 64, NS])
                k.tt(OS5I[:, :, :], X0R[:, :, :], fii, ALU.mult)
                k.tt(OS5R[:, :, :], X0I[:, :, :], fii, ALU.mult)
                k.tt(X0R[:, :, :], X0R[:, :, :], fir, ALU.mult)
                k.tt(X0R[:, :, :], X0R[:, :, :], OS5R[:, :, :], ALU.subtract)
                k.tt(X0I[:, :, :], X0I[:, :, :], fir, ALU.mult)
                k.tt(X0I[:, :, :], X0I[:, :, :], OS5I[:, :, :], ALU.add)

            groups = []
            nfull = Wp // RUN
            if nfull:
                groups.append((0, nfull, RUN))
            if Wp - nfull * RUN > 0:
                groups.append((nfull * RUN, 1, Wp - nfull * RUN))
            if Ws:
                groups.append((Wp, Ws // TS, TS))
            runs = []
            c = 0
            while c < Wp:
                L = min(RUN, Wp - c)
                runs.append((c, L))
                c += L

            def tile_of(c):
                for ti, (c0, n) in enumerate(cts):
                    if c0 <= c < c0 + n:
                        return ti, c0
                raise ValueError

            def rot(dst, src, tcb, sign):
                for (g0, nr, rl) in groups:
                    sh = [128, nr, rl]
                    cb = tcb[:, 0, 0:rl].unsqueeze(1).to_broadcast(sh)
                    sb_ = tcb[:, 1, 0:rl].unsqueeze(1).to_broadcast(sh)
                    v3 = lambda t: t[:, g0:g0 + nr * rl].rearrange("p (r l) -> p r l", l=rl)
                    sr, si_ = v3(src[:, 0, :]), v3(src[:, 1, :])
                    t1, t2 = v3(rot.t1), v3(rot.t2)
                    k.tt(t1, sr, cb, ALU.mult)
                    k.tt(t2, si_, sb_, ALU.mult)
                    k.tt(v3(dst[:, 0, :]), t1, t2, ALU.add if sign < 0 else ALU.subtract)
                    k.tt(t1, si_, cb, ALU.mult)
                    k.tt(t2, sr, sb_, ALU.mult)
                    k.tt(v3(dst[:, 1, :]), t1, t2, ALU.subtract if sign < 0 else ALU.add)

            for j in range(16):
                par = j % 2
                U32 = U32s[par]
                UB = UBs[par]
                wi = WI[par]
                k.dma('pool', wi[:, 0, :, :], wv[:, :, j * 128:(j + 1) * 128])
                k.dma('pool', BBR[par][:, :, :], s5_bre[j, :, :, :])
                k.dma('pool', BBI[par][:, :, :], s5_bim[j, :, :, :])
                if si == 0:
                    k.dma('sp', CCR[par][:, :, :], s5_cre[j, :, :, :])
                    k.dma('sp', CCI[par][:, :, :], s5_cim[j, :, :, :])
                    for s_ in range(4):
                        sg = 4 * j + s_
                        k.ts(CT1[:, :], CCR[par][:, s_, :], S5FRE[:, sg:sg + 1], ALU.mult)
                        k.stt(CPR[par][:, s_, :], CCI[par][:, s_, :], S5NFIM[:, sg:sg + 1], CT1[:, :], ALU.mult, ALU.add)
                        k.ts(NCPR[par][:, s_, :], CPR[par][:, s_, :], -1.0, ALU.mult)
                        k.ts(CT1[:, :], CCR[par][:, s_, :], S5NFIM[:, sg:sg + 1], ALU.mult)
                        k.stt(CPI[par][:, s_, :], CCI[par][:, s_, :], S5NFRE[:, sg:sg + 1], CT1[:, :], ALU.mult, ALU.add)
                    k.dma('sp', s5cp[j, 0, :, :, :], CPR[par][:, :, :])
                    k.dma('sp', s5cp[j, 1, :, :, :], NCPR[par][:, :, :])
                    k.dma('sp', s5cp[j, 2, :, :, :], CPI[par][:, :, :])
                else:
                    k.dma('sp', CPR[par][:, :, :], s5cp[j, 0, :, :, :])
                    k.dma('sp', NCPR[par][:, :, :], s5cp[j, 1, :, :, :])
                    k.dma('sp', CPI[par][:, :, :], s5cp[j, 2, :, :, :])
                for ti, (c0, n) in enumerate(cts):
                    for kt in range(8):
                        k.mm(PS[ti][:, :n], wi[:, 0, kt, :], XN[:, kt, c0:c0 + n], start=(kt == 0), stop=(kt == 7))
                    k.act(AF.Identity, U32[:, c0:c0 + n], PS[ti][:, :n])
                    k.act(AF.Identity, UB[:, c0:c0 + n], PS[ti][:, :n])
                for pair in range(2):
                    tiles = [(2 * pair + q, q) for q in range(2)]
                    for s_, sl_ in tiles:
                        sg = 4 * j + s_
                        k.dma('pool', TCB[sl_][:, :, :], s5tab[sg, :, :, :])
                        for ti, (c0, n) in enumerate(cts):
                            k.mm(PS[2 + ti][:, :n], BBR[par][:, s_, :], UB[:, c0:c0 + n])
                            k.mm(PS[4 + ti][:, :n], BBI[par][:, s_, :], UB[:, c0:c0 + n])
                            k.act(AF.Identity, PB[sl_][:, 0, c0:c0 + n], PS[2 + ti][:, :n])
                            k.act(AF.Identity, PB[sl_][:, 1, c0:c0 + n], PS[4 + ti][:, :n])
                    for s_, sl_ in tiles:
                        rot.t1, rot.t2 = TT1[sl_], TT2[sl_]
                        rot(VB[sl_], PB[sl_], TCB[sl_], -1)
                    for ri, (c, L) in enumerate(runs):
                        for s_, sl_ in tiles:
                            sg = 4 * j + s_
                            WR, WIm = WRs[sl_]
                            xls = XLS[sl_]
                            magb = S5MAG[:, sg:sg + 1]
                            if ri == 0:
                                ir = 0.0 if si == 0 else S5R[:, sg:sg + 1]
                                ii = 0.0 if si == 0 else S5I[:, sg:sg + 1]
                            else:
                                ir, ii = xls[:, ri - 1, 0:1], xls[:, ri - 1, 1:2]
                            k.scan(WR[:, c:c + L], magb.to_broadcast([128, L]), VB[sl_][:, 0, c:c + L], ir)
                            k.scan(WIm[:, c:c + L], magb.to_broadcast([128, L]), VB[sl_][:, 1, c:c + L], ii)
                        for s_, sl_ in tiles:
                            sg = 4 * j + s_
                            WR, WIm = WRs[sl_]
                            xls = XLS[sl_]
                            last = ri == len(runs) - 1
                            orr = S5R[:, sg:sg + 1] if last else xls[:, ri, 0:1]
                            oii = S5I[:, sg:sg + 1] if last else xls[:, ri, 1:2]
                            wl, wil = WR[:, c + L - 1:c + L], WIm[:, c + L - 1:c + L]
                            vi_ = 0 if L == RUN else 1
                            assert L in (RUN, 16)
                            cl, sl2, nsl = CLAST[:, sg, vi_, 0:1], CLAST[:, sg, vi_, 1:2], CLAST[:, sg, vi_, 2:3]
                            k.act(AF.Identity, xls[:, ri, 2:3], wil, scale=nsl)
                            k.act(AF.Identity, xls[:, ri, 3:4], wl, scale=sl2)
                            k.act(AF.Identity, orr, wl, scale=cl, bias=xls[:, ri, 2:3])
                            k.act(AF.Identity, oii, wil, scale=cl, bias=xls[:, ri, 3:4])
                    if Ws:
                        for s_, sl_ in tiles:
                            sg = 4 * j + s_
                            WR, WIm = WRs[sl_]
                            magb = S5MAG[:, sg:sg + 1]
                            k.act(AF.Identity, MAGR[sl_][:, :], RSTM[:, :], scale=magb)
                            v0r = VB[sl_][:, 0, Wp:W].rearrange("p (s t) -> p s t", t=TS)[:, :, 0]
                            v0i = VB[sl_][:, 1, Wp:W].rearrange("p (s t) -> p s t", t=TS)[:, :, 0]
                            k.stt(v0r, X0R[:, sg, :], magb, v0r, ALU.mult, ALU.add)
                            k.stt(v0i, X0I[:, sg, :], magb, v0i, ALU.mult, ALU.add)
                            k.scan(WR[:, Wp:W], MAGR[sl_][:, :], VB[sl_][:, 0, Wp:W], 0.0)
                            k.scan(WIm[:, Wp:W], MAGR[sl_][:, :], VB[sl_][:, 1, Wp:W], 0.0)
                            wr7 = WR[:, Wp:W].rearrange("p (s t) -> p s t", t=TS)[:, :, TS - 1]
                            wi7 = WIm[:, Wp:W].rearrange("p (s t) -> p s t", t=TS)[:, :, TS - 1]
                            c7, s7 = CLAST[:, sg, 2, 0:1], CLAST[:, sg, 2, 1:2]
                            x7 = X7[sl_]
                            k.ts(TMP16[:, :], wi7, s7, ALU.mult)
                            k.stt(x7[:, 0, :], wr7, c7, TMP16[:, :], ALU.mult, ALU.subtract)
                            k.ts(TMP16B[:, :], wr7, s7, ALU.mult)
                            k.stt(x7[:, 1, :], wi7, c7, TMP16B[:, :], ALU.mult, ALU.add)
                            k.ts(TMP16[:, :], x7[:, 0, :], S5FRE[:, sg:sg + 1], ALU.mult)
                            k.stt(OS5R[:, sg, :], x7[:, 1, :], S5NFIM[:, sg:sg + 1], TMP16[:, :], ALU.mult, ALU.add)
                            k.ts(TMP16B[:, :], x7[:, 1, :], S5FRE[:, sg:sg + 1], ALU.mult)
                            k.stt(OS5I[:, sg, :], x7[:, 0, :], S5FIM[:, sg:sg + 1], TMP16B[:, :], ALU.mult, ALU.add)
                    for s_, sl_ in tiles:
                        WR, WIm = WRs[sl_]
                        k.act(AF.Identity, WB[sl_][:, 0, 0:W], WR[:, 0:W])
                        k.act(AF.Identity, WB[sl_][:, 1, 0:W], WIm[:, 0:W])
                    for s_, sl_ in tiles:
                        t1, t2, t3, t4 = TP[sl_]
                        for (g0, nr, rl) in groups:
                            sh = [128, nr, rl]
                            cb = TCB[sl_][:, 0, 0:rl].unsqueeze(1).to_broadcast(sh)
                            sb_ = TCB[sl_][:, 1, 0:rl].unsqueeze(1).to_broadcast(sh)
                            v3 = lambda t: t[:, g0:g0 + nr * rl].rearrange("p (r l) -> p r l", l=rl)
                            wr_, wi_ = v3(WB[sl_][:, 0, :]), v3(WB[sl_][:, 1, :])
                            k.tt(v3(t1), wr_, cb, ALU.mult)
                            k.tt(v3(t2), wi_, sb_, ALU.mult)
                            k.tt(v3(t3), wi_, cb, ALU.mult)
                            k.tt(v3(t4), wr_, sb_, ALU.mult)
                    for s_, sl_ in tiles:
                        t1, t2, t3, t4 = TP[sl_]
                        for ti, (c0, n) in enumerate(cts):
                            k.mm(PS[6 + ti][:, :n], CPR[par][:, s_, :], t1[:, c0:c0 + n], start=(s_ == 0), stop=False)
                            k.mm(PS[6 + ti][:, :n], NCPR[par][:, s_, :], t2[:, c0:c0 + n], start=False, stop=False)
                            k.mm(PS[6 + ti][:, :n], CPI[par][:, s_, :], t3[:, c0:c0 + n], start=False, stop=False)
                            k.mm(PS[6 + ti][:, :n], CPI[par][:, s_, :], t4[:, c0:c0 + n], start=False, stop=(s_ == 3))
                for ti, (c0, n) in enumerate(cts):
                    k.stt(G1[:, c0:c0 + n], U32[:, c0:c0 + n], S5D[:, j:j + 1], PS[6 + ti][:, :n], ALU.mult, ALU.add)
                k.act(AF.Square, Q1[:, 0:W], G1[:, 0:W], scale=0.044715 ** 0.5)
                k.stt(Q1[:, 0:W], Q1[:, 0:W], 1.0, G1[:, 0:W], ALU.add, ALU.mult)
                k.act(AF.Sigmoid, Q1[:, 0:W], Q1[:, 0:W], scale=1.5957691216057308)
                k.tt(BIGA[:, j, 0:W], G1[:, 0:W], Q1[:, 0:W], ALU.mult)
            gw = s5_glu_w.rearrange("(kt p) n -> p kt n", p=128)
            WO = [sb(f"WG{i}", [128, 16, 128], BF16) for i in range(2)]
            for i in range(16):
                par = i % 2
                wo, wi = WO[par], WI[par]
                SGM, SG = FB[2 * par], FB[2 * par + 1]
                k.dma('pool', wo[:, :, :], gw[:, :, i * 128:(i + 1) * 128])
                k.dma('pool', wi[:, 1, :, :], wv[:, :, E + i * 128:E + (i + 1) * 128])
                for ti, (c0, n) in enumerate(cts):
                    pa, pb = PS[4 * par + ti], PS[4 * par + 2 + ti]
                    for kt in range(16):
                        k.mm(pa[:, :n], wo[:, kt, :], BIGA[:, kt, c0:c0 + n], start=(kt == 0), stop=(kt == 15))
                    for kt in range(8):
                        k.mm(pb[:, :n], wi[:, 1, kt, :], XN[:, kt, c0:c0 + n], start=(kt == 0), stop=(kt == 7))
                    k.act(AF.Sigmoid, SGM[:, c0:c0 + n], pa[:, :n], bias=S5GB[:, i:i + 1])
                    k.act(AF.Silu, SG[:, c0:c0 + n], pb[:, :n])
                k.tt(SGM[:, 0:W], SGM[:, 0:W], SG[:, 0:W], ALU.mult)
                k.tt(BIGB[:, i, 0:W], BIGA[:, i, 0:W], SGM[:, 0:W], ALU.mult)
            out_proj(li, seg, BIGB, s5_w_out, FB)
            if si == len(SEGS) - 1:
                t0, t1 = S5T[0], S5T[1]
                k.tt(t0[:, :], S5FRE[:, :], S5R[:, :], ALU.mult)
                k.tt(t1[:, :], S5FIM[:, :], S5I[:, :], ALU.mult)
                k.tt(OP5R[:, :], t0[:, :], t1[:, :], ALU.subtract)
                k.tt(t0[:, :], S5FRE[:, :], S5I[:, :], ALU.mult)
                k.tt(t1[:, :], S5FIM[:, :], S5R[:, :], ALU.mult)
                k.tt(OP5I[:, :], t0[:, :], t1[:, :], ALU.add)
                k.dma('sp', o_p_s5_re[:, :], OP5R[:, :])
                k.dma('sp', o_p_s5_im[:, :], OP5I[:, :])
                k.dma('sp', o_s_s5_re[:, :, :], OS5R[:, :, :])
                k.dma('sp', o_s_s5_im[:, :, :], OS5I[:, :, :])


        rets_d = nc.dram_tensor("rets_d", [128, 4, 2, 512], F32, kind="Internal").ap()
        IDB = sb("IDB", [128, 128], BF16)
        RET_G = [1.0 - 2.0 ** (-5.0 - h) for h in range(4)]

        k.dma('pool', IDB[:, :], ident_d[:, :])

        def ret_setup():
            pass

        def ret_layer(li, seg, si):
            W, Wp, Ws = seg['W'], seg['Wp'], seg['Ws']
            cts = ctiles(W)
            g0 = seg['p0']
            QF = sb("QF", [128, 8, WMAX], BF16)
            KF = sb("KF", [128, 8, WMAX], BF16)
            SBF = sb("SBF", [128, 4, 2, 512], BF16)
            RETS = sb("RETS", [128, 4, 2, 512])
            if si == 0:
                k.memset(RETS[:, :, :, :], 0.0)
            else:
                k.dma('sp', RETS[:, :, :, :], rets_d[:, :, :, :])
            for h_ in range(4):
                k.act(AF.Identity, SBF[:, h_, :, :], RETS[:, h_, :, :])
            VTB = sb("VTB", [128, 6, E], BF16)
            VTF = VTB[:, :, :].bitcast(F32).rearrange("p a b -> p (a b)")
            ZTd = [VTF[:, d * WMAX:(d + 1) * WMAX] for d in range(8)]
            GF = HB
            Y = BIGA
            PSB = [PS[i][:, :].bitcast(BF16) for i in range(8)]
            wq = ret_w_q.rearrange("(kt p) n -> p kt n", p=128)
            wk = ret_w_k.rearrange("(kt p) n -> p kt n", p=128)
            wv = ret_w_v.rearrange("(kt p) n -> p kt n", p=128)
            wg = ret_w_g.rearrange("(kt p) n -> p kt n", p=128)
            def phase1():
              RC = sb("RC", [128, WMAX])
              RSN = sb("RSN", [128, WMAX])
              T4 = [sb(f"RT{i}", [128, WMAX]) for i in range(4)]
              k.dma('sp', RC[:, 0:W], ret_cos[:, g0:g0 + W])
              k.dma('sp', RSN[:, 0:W], ret_sin[:, g0:g0 + W])
              nload = 0
              if True:
                for (wsrc, dst, scale) in ((wq, QF, 1.0), (wk, KF, 1.0 / 16.0)):
                    for h in range(4):
                        for dt in range(2):
                            t = 2 * h + dt
                            wi = WI[nload % 2]
                            nload += 1
                            k.dma('pool', wi[:, 0, :, :], wsrc[:, :, t * 128:(t + 1) * 128])
                            for ti, (c0, n) in enumerate(cts):
                                ps = PS[2 * dt + ti]
                                for kt in range(8):
                                    k.mm(ps[:, :n], wi[:, 0, kt, :], XN[:, kt, c0:c0 + n], start=(kt == 0), stop=(kt == 7))
                                k.act(AF.Identity, T4[dt][:, c0:c0 + n], ps[:, :n], scale=scale)
                        k.tt(T4[2][:, 0:W], T4[0][:, 0:W], RC[:, 0:W], ALU.mult)
                        k.tt(T4[3][:, 0:W], T4[1][:, 0:W], RSN[:, 0:W], ALU.mult)
                        k.tt(dst[:, 2 * h, 0:W], T4[2][:, 0:W], T4[3][:, 0:W], ALU.subtract)
                        k.tt(T4[2][:, 0:W], T4[1][:, 0:W], RC[:, 0:W], ALU.mult)
                        k.tt(T4[3][:, 0:W], T4[0][:, 0:W], RSN[:, 0:W], ALU.mult)
                        k.tt(dst[:, 2 * h + 1, 0:W], T4[2][:, 0:W], T4[3][:, 0:W], ALU.add)
                for t in range(16):
                    wi = WI[nload % 2]
                    nload += 1
                    k.dma('pool', wi[:, 0, :, :], wg[:, :, t * 128:(t + 1) * 128])
                    for ti, (c0, n) in enumerate(cts):
                        ps = PS[4 + 2 * (t % 2) + ti]
                        for kt in range(8):
                            k.mm(ps[:, :n], wi[:, 0, kt, :], XN[:, kt, c0:c0 + n], start=(kt == 0), stop=(kt == 7))
                        k.act(AF.Silu, GF[:, t, c0:c0 + n], ps[:, :n])
            chunks = [(c, min(128, Wp - c)) for c in range(0, Wp, 128)]
            if Ws:
                chunks.append((Wp, 128))

            def phase2():
              WV = sb("WV", [128, 8, 512], BF16)
              if True:
                for eg in range(4):
                    k.dma('pool', WV[:, :, :], wv[:, :, eg * 512:(eg + 1) * 512])
                    for ci, (c0, L) in enumerate(chunks):
                        ps = PS[ci % 4]
                        for kt in range(8):
                            k.mm(ps[0:L, :], XN[:, kt, c0:c0 + L], WV[:, kt, :], start=(kt == 0), stop=(kt == 7))
                        k.act(AF.Identity, VTB[0:L, ci, eg * 512:(eg + 1) * 512], ps[0:L, :])
            def phase3():
              MK = sb("MK", [128, 2, 4, 128])
              GROW = sb("GROW", [128, 2, 4, 128])
              GCOL = sb("GCOL", [128, 3, 4])
              ST = [sb(f"ST{i}", [128, 128], BF16) for i in range(2)]
              QG = [sb(f"QG{i}", [128, 2, 128], BF16) for i in range(2)]
              YN = [sb(f"YN{i}", [128, 512], BF16) for i in range(2)]
              JUNK = sb("JUNK", [128, 512], BF16)
              SS = [sb(f"SS{i}", [128, 1]) for i in range(2)]
              KW = [sb(f"KW{i}", [128, 2, 128], BF16) for i in range(2)]
              k.dma('sp', MK[:, :, :, :], ret_mk[:, :, :, :])
              k.dma('sp', GROW[:, :, :, :], ret_grow[:, :, :, :])
              k.dma('sp', GCOL[:, :, :], ret_gcol[:, :, :])
              if Ws:
                SC = sb("SC", [128, NS, 4])
                CMASK = sb("CMASK", [128, NS, 128], BF16)
                KTS = sb("KTS", [128, 2, 128], BF16)
                QGZ = [sb(f"QGZ{i}", [128, 2, 128], BF16) for i in range(2)]
                KWZ = [sb(f"KWZ{i}", [128, 2, 128], BF16) for i in range(2)]
                S0F = [sb(f"S0F{i}", [128, 2, 512]) for i in range(2)]
                S0B = [sb(f"S0B{i}", [128, 2, 512], BF16) for i in range(2)]
                k.dma('sp', SC[:, :, :], ret_sc[:, :, :])
                k.dma('pool', CMASK[:, :, :], ret_cmask[:, :, :])
              if True:
                for ci, (c0, L) in enumerate(chunks):
                    samp = c0 >= Wp
                    mv = 1 if samp else 0
                    gv = 2 if samp else (0 if L == 128 else 1)

                    def head_gen(h):
                        hp = h % 2
                        gL = RET_G[h] ** (TS if samp else L)
                        for dt in range(2):
                            k.tt(QG[hp][:, dt, 0:L], QF[:, 2 * h + dt, c0:c0 + L], GROW[:, mv, h, 0:L], ALU.mult)
                        pss = PS[hp]
                        for dt in range(2):
                            k.mm(pss[0:L, 0:L], KF[:, 2 * h + dt, c0:c0 + L], QF[:, 2 * h + dt, c0:c0 + L],
                                 start=(dt == 0), stop=(dt == 1))
                        yield
                        k.tt(ST[hp][0:L, 0:L], pss[0:L, 0:L], MK[0:L, mv, h, 0:L], ALU.mult)
                        psy = PS[2 + hp]
                        vt = VTB[0:L, ci, h * 512:(h + 1) * 512]
                        k.mm(psy[0:L, :], ST[hp][0:L, 0:L], vt, start=True, stop=False)
                        if not samp:
                            for dt in range(2):
                                k.mm(psy[0:L, :], QG[hp][:, dt, 0:L], SBF[:, h, dt, :], start=False, stop=(dt == 1))
                        else:
                            for dt in range(2):
                                pst = PSB[4 + hp][0:L, 512 + dt * 128:512 + (dt + 1) * 128]
                                k.tr(pst, KF[:, 2 * h + dt, c0:c0 + L], IDB[:, :])
                                k.act(AF.Identity, KTS[0:L, dt, :], pst)
                            for i in range(NS):
                                ip = i % 2
                                k.dma('sp', S0F[ip][:, :, :], st_ret[i, h].rearrange("(dt p) e -> p dt e", p=128))
                                k.act(AF.Identity, S0B[ip][:, :, :], S0F[ip][:, :, :])
                                k.tt(QGZ[ip][:, :, :], QG[hp][:, :, :],
                                     CMASK[:, i, :].unsqueeze(1).to_broadcast([128, 2, 128]), ALU.mult)
                                for dt in range(2):
                                    k.mm(psy[0:L, :], QGZ[ip][:, dt, :], S0B[ip][:, dt, :], start=False,
                                         stop=(i == NS - 1 and dt == 1))
                                k.act(AF.Identity, KWZ[ip][:, :, :], KTS[:, :, :], scale=SC[:, i, h:h + 1])
                                for dt in range(2):
                                    k.mm(PS[6 + dt][:, :], KWZ[ip][:, dt, :], vt, start=True, stop=True)
                                    k.stt(S0F[ip][:, dt, :], S0F[ip][:, dt, :], gL, PS[6 + dt][:, :], ALU.mult, ALU.add)
                                k.dma('sp', o_s_ret[i, h].rearrange("(dt p) e -> p dt e", p=128), S0F[ip][:, :, :])
                        yield
                        k.act(AF.Square, JUNK[0:L, :], psy[0:L, :], accum_out=SS[hp][0:L, 0:1])
                        k.act(AF.Sqrt, SS[hp][0:L, 0:1], SS[hp][0:L, 0:1], scale=1.0 / 512.0, bias=CONST[0:L, 0:1])
                        yield
                        k.recip(SS[hp][0:L, 0:1], SS[hp][0:L, 0:1])
                        k.act(AF.Identity, YN[hp][0:L, :], psy[0:L, :], scale=SS[hp][0:L, 0:1])
                        yield
                        for et in range(4):
                            pst = PSB[4 + hp][:, et * 128:et * 128 + L]
                            k.tr(pst, YN[hp][0:L, et * 128:(et + 1) * 128], IDB[0:L, 0:L])
                        yield
                        for et in range(4):
                            pst = PSB[4 + hp][:, et * 128:et * 128 + L]
                            k.tt(Y[:, 4 * h + et, c0:c0 + L], pst, GF[:, 4 * h + et, c0:c0 + L], ALU.mult)
                        if not samp:
                            for dt in range(2):
                                pst = PSB[4 + hp][0:L, 512 + dt * 128:512 + (dt + 1) * 128]
                                k.tr(pst, KF[:, 2 * h + dt, c0:c0 + L], IDB[:, :])
                            yield
                            for dt in range(2):
                                pst = PSB[4 + hp][0:L, 512 + dt * 128:512 + (dt + 1) * 128]
                                k.act(AF.Identity, KW[hp][0:L, dt, :], pst, scale=GCOL[0:L, gv, h:h + 1])
                            yield
                            for dt in range(2):
                                k.mm(PS[6 + dt][:, :], KW[hp][0:L, dt, :], vt, start=True, stop=True)
                                k.stt(RETS[:, h, dt, :], RETS[:, h, dt, :], gL, PS[6 + dt][:, :], ALU.mult, ALU.add)
                                k.act(AF.Identity, SBF[:, h, dt, :], RETS[:, h, dt, :])
                        yield

                    if samp:
                        for h in range(4):
                            for _ in head_gen(h):
                                pass
                    else:
                        for hpair in range(2):
                            gens = [head_gen(2 * hpair), head_gen(2 * hpair + 1)]
                            alive = [True, True]
                            while any(alive):
                                for gi in range(2):
                                    if alive[gi] and next(gens[gi], 'end') == 'end':
                                        alive[gi] = False

            with layer_scope():
                phase1()
            with layer_scope():
                phase2()
            with layer_scope():
                phase3()
            out_proj(li, seg, Y, ret_w_o, ZTd)
            if si == len(SEGS) - 1:
                k.dma('sp', o_p_ret.rearrange("h (dt p) e -> p h dt e", p=128), RETS[:, :, :, :])
            else:
                k.dma('sp', rets_d[:, :, :, :], RETS[:, :, :, :])


        RWM = sb("RWM", [128, 16, 64])
        RWMB = sb("RWMB", [128, 16, 64], BF16)
        SHC = sb("SHC", [128, 8])
        RMU = sb("RMU", [128, 6, 8])
        RVEC = sb("RVEC", [128, 5, 16])
        DEC_C = 0.6065306597126334

        def rwkv_setup():
            k.memset(RWM[:, :, :], 0.0)
            k.memset(RWMB[:, :, :], 0.0)
            k.memset(SHC[:, :], 0.0)
            k.memset(CONST[:, 3:4], 64e-5)
            k.dma('sp', RMU[:, :, :], rw_mu[:, :, :])
            k.dma('sp', RVEC[:, :, :], rw_vecs[:, :, :])

        def rwkv_layer(li, seg, si):
            W, Wp, Ws = seg['W'], seg['Wp'], seg['Ws']
            cts = ctiles(W)
            chunks = [(c, min(128, Wp - c)) for c in range(0, Wp, 128)]
            if Ws:
                chunks.append((Wp, 128))
            nch = len(chunks)
            PSB = [PS[i][:, :].bitcast(BF16) for i in range(8)]
            RW0, RA0, RKK, RKA, RRK = (RVEC[:, i, :] for i in range(5))
            Y = BIGA
            GF = HB

            def phaseA():
                XX = sb("XX", [128, 8, W], BF16)
                XMb = sb("XM", [128, 8, W], BF16)
                SHN = sb("SHN", [128, 8])
                W1 = sb("W1", [128, 8, 64], BF16)
                A1W = sb("A1W", [128, 8, 64], BF16)
                TW = sb("TW", [64, W], BF16)
                TA = sb("TA", [64, W], BF16)
                BONES = sb("BONES", [128, 128], BF16)
                MSK = sb("MSK", [128, 2, 3, 128])
                SEL = sb("SEL", [128, 2], BF16)
                RT = sb("RT", [128, 4, W], BF16)
                AT = sb("AT", [128, 4, W], BF16)
                BT_ = sb("BT_", [128, 4, W], BF16)
                KT_ = sb("KT_", [128, 4, W], BF16)
                GL = sb("GL", [128, 4, 8])
                VTg = sb("VTg", [128, 6, 512], BF16)
                LNW = sb("LNW", [128, 512])
                LNB = sb("LNB", [128, 512])
                if Ws:
                    OSS = sb("OSS", [128, 8, NS])
                    STS = sb("STS", [128, 8, NS])
                    RESET = sb("RESET", [128, 128])
                    CMASK = sb("CMASK", [128, NS, 128], BF16)
                    OH = sb("OH", [128, NS], BF16)
                    GLS = sb("GLS", [128, 4, NS])
                    k.dma('sp', RESET[:, :], rw_reset[:, :])
                    k.dma('pool', CMASK[:, :, :], ret_cmask[:, :, :])
                    k.dma('pool', OH[:, :], rw_oh[:, :])
                    k.dma('sp', STS[:, :, :], st_rw_shift[:, :, :])
                k.dma('sp', MSK[:, :, :, :], rw_masks[:, :, :, :])
                k.dma('pool', SEL[:, :], rw_sel[:, :])
                k.dma('pool', W1[:, :, :], rw_w1.rearrange("(kt p) n -> p kt n", p=128))
                k.dma('pool', A1W[:, :, :], rw_a1.rearrange("(kt p) n -> p kt n", p=128))
                k.memset(BONES[:, :], 0.0)
                k.memset(BONES[0:64, 0:64], 1.0)
                k.memset(BONES[64:128, 64:128], 1.0)
                for kt in range(8):
                    k.stt(SHN[:, kt:kt + 1], H[:, kt, Wp - 1:Wp], NPRE[:, li, kt:kt + 1], RS[:, Wp - 1:Wp],
                          ALU.mult, ALU.mult)
                if Ws:
                    for kt in range(8):
                        hv = H[:, kt, Wp:W].rearrange("p (s t) -> p s t", t=TS)[:, :, TS - 1]
                        rv = RS[:, Wp:W].rearrange("p (s t) -> p s t", t=TS)[:, :, TS - 1]
                        k.stt(OSS[:, kt, :], hv, NPRE[:, li, kt:kt + 1], rv, ALU.mult, ALU.mult)
                    k.dma('sp', o_s_rw_shift[:, :, :], OSS[:, :, :])
                for kt in range(8):
                    k.tt(XX[:, kt, 1:W], XN[:, kt, 0:W - 1], XN[:, kt, 1:W], ALU.subtract)
                    k.tt(XX[:, kt, 0:1], SHC[:, kt:kt + 1], XN[:, kt, 0:1], ALU.subtract)
                    if Ws:
                        xs = XX[:, kt, Wp:W].rearrange("p (s t) -> p s t", t=TS)[:, :, 0]
                        x0 = XN[:, kt, Wp:W].rearrange("p (s t) -> p s t", t=TS)[:, :, 0]
                        k.tt(xs, STS[:, kt, :], x0, ALU.subtract)
                k.copy(SHC[:, :], SHN[:, :])
                if si == len(SEGS) - 1:
                    k.dma('sp', o_p_rw_shift[:, :], SHC[:, :])

                def mix(n_, dst):
                    for kt in range(8):
                        k.stt(dst[:, kt, 0:W], XX[:, kt, 0:W], RMU[:, n_, kt:kt + 1], XN[:, kt, 0:W], ALU.mult, ALU.add)

                mix(1, XMb)
                for ti, (c0, n) in enumerate(cts):
                    for kt in range(8):
                        k.mm(PS[ti][0:64, :n], W1[:, kt, :], XMb[:, kt, c0:c0 + n], start=(kt == 0), stop=(kt == 7))
                    k.act(AF.Tanh, TW[:, c0:c0 + n], PS[ti][0:64, :n])
                mix(4, XMb)
                for ti, (c0, n) in enumerate(cts):
                    for kt in range(8):
                        k.mm(PS[2 + ti][0:64, :n], A1W[:, kt, :], XMb[:, kt, c0:c0 + n], start=(kt == 0), stop=(kt == 7))
                    k.act(AF.Identity, TA[:, c0:c0 + n], PS[2 + ti][0:64, :n])
                mix(5, XMb)
                wg = rw_w_g.rearrange("(kt p) n -> p kt n", p=128)
                for t in range(16):
                    wi = WI[t % 2]
                    k.dma('pool', wi[:, 0, :, :], wg[:, :, t * 128:(t + 1) * 128])
                    for ti, (c0, n) in enumerate(cts):
                        ps = PS[4 + 2 * (t % 2) + ti]
                        for kt in range(8):
                            k.mm(ps[:, :n], wi[:, 0, kt, :], XMb[:, kt, c0:c0 + n], start=(kt == 0), stop=(kt == 7))
                        k.act(AF.Silu, GF[:, t, c0:c0 + n], ps[:, :n])
                wr = rw_w_r.rearrange("(kt p) n -> p kt n", p=128)
                wk = rw_w_k.rearrange("(kt p) n -> p kt n", p=128)
                wv = rw_w_v.rearrange("(kt p) n -> p kt n", p=128)
                def passes(hg):
                    W2T = [sb(f"W2T{i}", [64, 2, 128], BF16) for i in range(2)]
                    TB7s = [[sb(f"TB7_{p}_{i}", [128, W]) for i in range(7)] for p in range(2)]
                    SQBs = [sb(f"SQB{p}", [128, W], BF16) for p in range(2)]
                    k.dma('sp', LNW[:, :], rw_lnw[:, hg * 512:(hg + 1) * 512])
                    k.dma('sp', LNB[:, :], rw_lnb[:, hg * 512:(hg + 1) * 512])
                    mix(2, XMb)
                    def p1_tile(tl):
                        t = 4 * hg + tl
                        wi = WI[tl % 2]
                        w2t = W2T[tl % 2]
                        KFt, AS, SG_, KK, T5, CS, EG = TB7s[tl % 2]
                        SQB = SQBs[tl % 2]
                        k.dma('pool', wi[:, 1, :, :], wk[:, :, t * 128:(t + 1) * 128])
                        k.dma('pool', w2t[:, 0, :], rw_w2[:, t * 128:(t + 1) * 128])
                        k.dma('pool', w2t[:, 1, :], rw_a2[:, t * 128:(t + 1) * 128])
                        for ti, (c0, n) in enumerate(cts):
                            for kt in range(8):
                                k.mm(PS[2 + ti][:, :n], wi[:, 1, kt, :], XMb[:, kt, c0:c0 + n], start=(kt == 0), stop=(kt == 7))
                            k.mm(PS[4 + ti][:, :n], w2t[:, 1, :], TA[:, c0:c0 + n])
                            k.mm(PS[6 + ti][:, :n], w2t[:, 0, :], TW[:, c0:c0 + n])
                            k.act(AF.Identity, KFt[:, c0:c0 + n], PS[2 + ti][:, :n])
                            k.act(AF.Sigmoid, AS[:, c0:c0 + n], PS[4 + ti][:, :n], bias=RA0[:, t:t + 1])
                            k.act(AF.Sigmoid, SG_[:, c0:c0 + n], PS[6 + ti][:, :n], bias=RW0[:, t:t + 1])
                            yield
                        w_ = slice(0, W)
                        k.ts(KK[:, w_], KFt[:, w_], RKK[:, t:t + 1], ALU.mult)
                        k.act(AF.Square, SQB[:, w_], KK[:, w_])
                        for ti, (c0, n) in enumerate(cts):
                            k.mm(PS[ti][:, :n], BONES[:, :], SQB[:, c0:c0 + n])
                            k.ts(T5[:, c0:c0 + n], PS[ti][:, :n], 1e-24, ALU.max)
                        yield
                        k.act(AF.Sqrt, T5[:, w_], T5[:, w_])
                        k.recip(T5[:, w_], T5[:, w_])
                        k.tt(KK[:, w_], KK[:, w_], T5[:, w_], ALU.mult)
                        yield
                        k.ts(T5[:, w_], AS[:, w_], 1.0, ALU.subtract, RKA[:, t:t + 1], ALU.mult)
                        k.stt(T5[:, w_], T5[:, w_], 1.0, KFt[:, w_], ALU.add, ALU.mult)
                        k.tt(KFt[:, w_], KK[:, w_], AS[:, w_], ALU.mult)
                        for (c0, L) in chunks:
                            if c0 >= Wp:
                                k.scan(CS[:, c0:c0 + L], RESET[:, 0:L], SG_[:, c0:c0 + L], 0.0)
                            else:
                                k.scan(CS[:, c0:c0 + L], ONE_AP.to_broadcast([128, L]), SG_[:, c0:c0 + L], 0.0)
                        yield
                        k.act(AF.Exp, EG[:, w_], CS[:, w_], scale=-DEC_C)
                        k.act(AF.Identity, RT[:, tl, w_], EG[:, w_])
                        for ci, (c0, L) in enumerate(chunks):
                            if c0 >= Wp:
                                k.copy(GLS[:, tl, :], EG[:, c0:c0 + L].rearrange("p (s t) -> p s t", t=TS)[:, :, TS - 1])
                            else:
                                k.copy(GL[:, tl, ci:ci + 1], EG[:, c0 + L - 1:c0 + L])
                        yield
                        k.act(AF.Exp, AS[:, w_], CS[:, w_], scale=DEC_C)
                        k.tt(BT_[:, tl, w_], KFt[:, w_], AS[:, w_], ALU.mult)
                        k.tt(KT_[:, tl, w_], T5[:, w_], AS[:, w_], ALU.mult)
                        k.tt(CS[:, w_], CS[:, w_], SG_[:, w_], ALU.subtract)
                        k.act(AF.Exp, CS[:, w_], CS[:, w_], scale=-DEC_C)
                        k.stt(AT[:, tl, w_], KK[:, w_], -1.0, CS[:, w_], ALU.mult, ALU.mult)
                        yield

                    for tp in range(2):
                        gens = [p1_tile(2 * tp), p1_tile(2 * tp + 1)]
                        alive = [True, True]
                        while any(alive):
                            for gi in range(2):
                                if alive[gi] and next(gens[gi], 'end') == 'end':
                                    alive[gi] = False
                    mix(0, XMb)
                    for tl in range(4):
                        t = 4 * hg + tl
                        wi = WI[tl % 2]
                        k.dma('pool', wi[:, 0, :, :], wr[:, :, t * 128:(t + 1) * 128])
                        for ti, (c0, n) in enumerate(cts):
                            for kt in range(8):
                                k.mm(PS[ti][:, :n], wi[:, 0, kt, :], XMb[:, kt, c0:c0 + n], start=(kt == 0), stop=(kt == 7))
                            k.tt(RT[:, tl, c0:c0 + n], PS[ti][:, :n], RT[:, tl, c0:c0 + n], ALU.mult)

                def pass3(hg):
                    mix(3, XMb)
                    cnt_ = 0
                    for q in range(4):
                        wi = WI[q % 2]
                        k.dma('pool', wi[:, 0, :, :], wv[:, :, hg * 512 + q * 128:hg * 512 + (q + 1) * 128])
                        for ci, (c0, L) in enumerate(chunks):
                            ps = PS[5 + ci % 3]
                            for kt in range(8):
                                k.mm(ps[0:L, 0:128], XMb[:, kt, c0:c0 + L], wi[:, 0, kt, :], start=(kt == 0), stop=(kt == 7))
                            k.act(AF.Identity, VTg[0:L, ci, q * 128:(q + 1) * 128], ps[0:L, 0:128])
                            cnt_ += 1
                            if cnt_ % 4 == 0:
                                yield

                def chunkloop(hg):
                    MATS = [sb(f"MATS{i}", [128, 4, 8, 128], BF16) for i in range(2)]
                    LJ = [[sb(f"LJ{g}_{i}", [128, 4, 128], BF16) for i in range(2)] for g in range(2)]
                    NJ = [[sb(f"NJ{g}_{i}", [128, 4, 128], BF16) for i in range(2)] for g in range(2)]
                    PJ = [[sb(f"PJ{g}_{i}", [128, 4, 128], BF16) for i in range(2)] for g in range(2)]
                    BKTs = [sb(f"BKT{i}", [128, 1024], BF16) for i in range(2)]
                    RKc = sb("RKc", [128, 4, 128], BF16)
                    XB = sb("XB", [128, 512], BF16)
                    UB = sb("UB", [128, 512], BF16)
                    YS = sb("YS", [128, 512])
                    SQ = sb("SQ", [128, 512])
                    OUTB = sb("OUTB", [128, 512], BF16)
                    ST8 = sb("ST8", [128, 8])
                    ST8b = sb("ST8b", [128, 8])
                    if Ws:
                        S0B = sb("S0B", [128, NS, 64], BF16)
                        S0T = sb("S0T", [128, NS, 64])
                        AZt = sb("AZt", [128, NS, 128], BF16)
                        UZt = AZt
                        VZt = sb("VZt", [128, NS, 128], BF16)

                    def cinfo(ci):
                        c0, L = chunks[ci]
                        samp = c0 >= Wp
                        mv = 1 if samp else 0
                        blk_ = TS if samp else L
                        nlev = 1
                        while (1 << (nlev + 1)) < blk_:
                            nlev += 1
                        return c0, L, samp, mv, nlev

                    def precompute(ci, gen=None):
                        c0, L, samp, mv, nlev = cinfo(ci)

                        def step():
                            if gen is not None:
                                next(gen, None)
                        MX = MATS[ci % 2]
                        MSU = MSK[0:L, mv, 0, 0:L]
                        MSL = MSK[0:L, mv, 1, 0:L]
                        MIU = MSK[0:L, mv, 2, 0:L]
                        v4 = lambda ps: ps[0:L, :].rearrange("p (a b) -> p a b", b=128)[:, :, 0:L]
                        m4 = lambda M: M.unsqueeze(1).to_broadcast([L, 4, L])
                        v2 = lambda ps: ps[0:L, 0:256].rearrange("p (a b) -> p a b", b=128)[:, :, 0:L]
                        m2 = lambda M: M.unsqueeze(1).to_broadcast([L, 2, L])
                        for g4 in range(2):
                            lj, nj, pj = LJ[g4], NJ[g4], PJ[g4]

                            def st1(types):
                                for ti_, ty in enumerate(types):
                                    for i in range(4):
                                        hl = 4 * g4 + i
                                        tl, hh = hl // 2, hl % 2
                                        hr = slice(hh * 64, hh * 64 + 64)
                                        At, Bt = AT[hr, tl, c0:c0 + L], BT_[hr, tl, c0:c0 + L]
                                        Kt, Rt = KT_[hr, tl, c0:c0 + L], RT[hr, tl, c0:c0 + L]
                                        lhs, rhs = {'nab': (Bt, At), 'lab': (At, Bt), 'nak': (Kt, At),
                                                    'nrb': (Bt, Rt), 'nrk': (Kt, Rt)}[ty]
                                        pr = i // 2
                                        k.mm(PS[2 * ti_ + hh][0:L, pr * 128:pr * 128 + L], lhs, rhs)
                            st1(['nab', 'lab'])
                            for hh in range(2):
                                k.tt(nj[0][0:L, 2 * hh:2 * hh + 2, 0:L], v2(PS[hh]), m2(MSU), ALU.mult)
                                k.tt(lj[0][0:L, 2 * hh:2 * hh + 2, 0:L], v2(PS[2 + hh]), m2(MSL), ALU.mult)
                            k.tt(pj[0][0:L, :, 0:L], nj[0][0:L, :, 0:L], m4(IDB[0:L, 0:L]), ALU.add)
                            st1(['nak', 'nrb'])
                            for hh in range(2):
                                ms = slice(4 * g4 + 2 * hh, 4 * g4 + 2 * hh + 2)
                                k.tt(MX[0:L, 1, ms, 0:L], v2(PS[hh]), m2(MSU), ALU.mult)
                                k.tt(MX[0:L, 2, ms, 0:L], v2(PS[2 + hh]), m2(MIU), ALU.mult)
                            st1(['nrk'])
                            for hh in range(2):
                                ms = slice(4 * g4 + 2 * hh, 4 * g4 + 2 * hh + 2)
                                k.tt(MX[0:L, 3, ms, 0:L], v2(PS[hh]), m2(MIU), ALU.mult)
                        for j in range(1, nlev + 1):
                            last = j == nlev
                            step()
                            for g4 in range(2):
                                lj, nj, pj = LJ[g4], NJ[g4], PJ[g4]
                                bx, by = PS[2 * g4], PS[2 * g4 + 1]
                                lp, np_ = lj[(j - 1) % 2], nj[(j - 1) % 2]
                                ln, nn = lj[j % 2], nj[j % 2]
                                for i in range(4):
                                    k.mm(bx[0:L, i * 128:i * 128 + L], np_[0:L, i, 0:L], lp[0:L, i, 0:L])
                                if not last:
                                    for i in range(4):
                                        k.mm(by[0:L, i * 128:i * 128 + L], lp[0:L, i, 0:L], np_[0:L, i, 0:L])
                                k.act(AF.Identity, ln[0:L, :, 0:L], v4(bx))
                                if not last:
                                    k.copy(nn[0:L, :, 0:L], v4(by))
                            for g4 in range(2):
                                lj, pj = LJ[g4], PJ[g4]
                                bx = PS[2 * g4]
                                hs = slice(4 * g4, 4 * g4 + 4)
                                ln, pp, pn = lj[j % 2], pj[(j - 1) % 2], pj[j % 2]
                                for i in range(4):
                                    k.mm(bx[0:L, i * 128:i * 128 + L], IDB[0:L, 0:L], pp[0:L, i, 0:L], start=True, stop=False)
                                    k.mm(bx[0:L, i * 128:i * 128 + L], ln[0:L, i, 0:L], pp[0:L, i, 0:L], start=False, stop=True)
                                Pn = MX[0:L, 0, hs, 0:L] if last else pn[0:L, :, 0:L]
                                k.act(AF.Identity, Pn, v4(bx))
                        p5 = PSB[4]
                        bkt = BKTs[ci % 2]
                        for tl in range(4):
                            k.tr(p5[0:L, tl * 128:(tl + 1) * 128], BT_[:, tl, c0:c0 + L], IDB[:, :])
                            k.tr(p5[0:L, 512 + tl * 128:512 + (tl + 1) * 128], KT_[:, tl, c0:c0 + L], IDB[:, :])
                        k.act(AF.Identity, bkt[0:L, :], p5[0:L, :])

                    def hidx(hl):
                        return 4 * (hl // 4) + ((hl % 4) % 2) * 2 + (hl % 4) // 2

                    def sequential(ci):
                        c0, L, samp, mv, nlev = cinfo(ci)
                        MX = MATS[ci % 2]
                        BKT = BKTs[ci % 2]
                        p5 = PSB[5]
                        vt = lambda hl: VTg[0:L, ci, hl * 64:(hl + 1) * 64]
                        for hl in range(8):
                            tl, hh = hl // 2, hl % 2
                            hr = slice(hh * 64, hh * 64 + 64)
                            po = PS[5][0:L, hl * 64:(hl + 1) * 64]
                            if not samp:
                                k.mm(po, AT[hr, tl, c0:c0 + L], RWMB[hr, 4 * hg + tl, :], start=True, stop=False)
                            else:
                                if hh == 0:
                                    k.dma('pool', S0B[:, :, :], st_rw_wkv[4 * hg + tl, :, :, :])
                                    k.tt(AZt[:, :, :], AT[:, tl, c0:c0 + L].unsqueeze(1).to_broadcast([128, NS, 128]),
                                         CMASK[:, :, :], ALU.mult)
                                for i in range(NS):
                                    k.mm(po, AZt[hr, i, :], S0B[hr, i, :], start=(i == 0), stop=False)
                            k.mm(po, MX[0:L, 1, hidx(hl), 0:L], vt(hl), start=False, stop=True)
                        k.act(AF.Identity, XB[0:L, :], PS[5][0:L, :])
                        yield
                        for hl in range(8):
                            k.mm(PS[5][0:L, hl * 64:(hl + 1) * 64], MX[0:L, 0, hidx(hl), 0:L], XB[0:L, hl * 64:(hl + 1) * 64])
                        k.act(AF.Identity, UB[0:L, :], PS[5][0:L, :])
                        yield
                        if not samp:
                            for hl in range(8):
                                tl, hh = hl // 2, hl % 2
                                hr = slice(hh * 64, hh * 64 + 64)
                                po = PS[7][hr, tl * 64:(tl + 1) * 64]
                                k.mm(po, BKT[0:L, tl * 128 + hh * 64:tl * 128 + hh * 64 + 64], UB[0:L, hl * 64:(hl + 1) * 64],
                                     start=True, stop=False)
                                k.mm(po, BKT[0:L, 512 + tl * 128 + hh * 64:512 + tl * 128 + hh * 64 + 64], vt(hl),
                                     start=False, stop=True)
                        yield
                        for hl in range(8):
                            tl, hh = hl // 2, hl % 2
                            hr = slice(hh * 64, hh * 64 + 64)
                            po = PS[6][0:L, hl * 64:(hl + 1) * 64]
                            if not samp:
                                k.mm(po, RT[hr, tl, c0:c0 + L], RWMB[hr, 4 * hg + tl, :], start=True, stop=False)
                            else:
                                if hh == 0:
                                    k.dma('pool', S0B[:, :, :], st_rw_wkv[4 * hg + tl, :, :, :])
                                    k.tt(AZt[:, :, :], RT[:, tl, c0:c0 + L].unsqueeze(1).to_broadcast([128, NS, 128]),
                                         CMASK[:, :, :], ALU.mult)
                                for i in range(NS):
                                    k.mm(po, AZt[hr, i, :], S0B[hr, i, :], start=(i == 0), stop=False)
                            k.mm(po, MX[0:L, 2, hidx(hl), 0:L], UB[0:L, hl * 64:(hl + 1) * 64], start=False, stop=False)
                            k.mm(po, MX[0:L, 3, hidx(hl), 0:L], vt(hl), start=False, stop=True)
                        if not samp:
                            mg = RWM[:, 4 * hg:4 * hg + 4, :]
                            k.tt(mg, mg, PS[7][:, 0:256].rearrange("p (a b) -> p a b", b=64), ALU.add)
                            k.tt(mg, mg, GL[:, :, ci:ci + 1].to_broadcast([128, 4, 64]), ALU.mult)
                            k.act(AF.Identity, RWMB[:, 4 * hg:4 * hg + 4, :], mg)
                        k.act(AF.Identity, YS[0:L, :], PS[6][0:L, :])
                        yield
                        if samp:
                            for tl in range(4):
                                t = 4 * hg + tl
                                k.dma('sp', S0T[:, :, :], st_rw_wkv[t, :, :, :])
                                ohb = OH[:, :].unsqueeze(2).to_broadcast([128, NS, 128])
                                k.tt(UZt[:, :, :], UB[:, tl * 128:(tl + 1) * 128].unsqueeze(1).to_broadcast([128, NS, 128]),
                                     ohb, ALU.mult)
                                k.tt(VZt[:, :, :], VTg[:, ci, tl * 128:(tl + 1) * 128].unsqueeze(1).to_broadcast([128, NS, 128]),
                                     ohb, ALU.mult)
                                for hh in range(2):
                                    hr = slice(hh * 64, hh * 64 + 64)
                                    for i in range(NS):
                                        po = PS[6 + i // 8][hr, (i % 8) * 64:(i % 8 + 1) * 64]
                                        k.mm(po, BKT[:, tl * 128 + hh * 64:tl * 128 + hh * 64 + 64],
                                             UZt[:, i, hh * 64:(hh + 1) * 64], start=True, stop=False)
                                        k.mm(po, BKT[:, 512 + tl * 128 + hh * 64:512 + tl * 128 + hh * 64 + 64],
                                             VZt[:, i, hh * 64:(hh + 1) * 64], start=False, stop=True)
                                for half in range(2):
                                    sv = S0T[:, half * 8:(half + 1) * 8, :]
                                    k.tt(sv, sv, PS[6 + half][:, :].rearrange("p (a b) -> p a b", b=64), ALU.add)
                                k.tt(S0T[:, :, :], S0T[:, :, :], GLS[:, tl, :].unsqueeze(2).to_broadcast([128, NS, 64]), ALU.mult)
                                k.dma('sp', o_s_rw_wkv[t, :, :, :], S0T[:, :, :])
                        yield
                        YS3 = YS[0:L, :].rearrange("p (a b) -> p a b", b=64)
                        SQ3 = SQ[0:L, :].rearrange("p (a b) -> p a b", b=64)
                        k.op('dve', lambda e: e.reduce_sum(out=ST8[0:L, :], in_=YS3, axis=AX.X), r=[YS], w=[ST8])
                        k.ts(ST8[0:L, :], ST8[0:L, :], 1.0 / 64.0, ALU.mult)
                        k.tt(YS3, YS3, ST8[0:L, :].unsqueeze(2).to_broadcast([L, 8, 64]), ALU.subtract)
                        k.tt(SQ3, YS3, YS3, ALU.mult)
                        k.op('dve', lambda e: e.reduce_sum(out=ST8b[0:L, :], in_=SQ3, axis=AX.X), r=[SQ], w=[ST8b])
                        k.act(AF.Sqrt, ST8b[0:L, :], ST8b[0:L, :], scale=1.0 / 64.0, bias=CONST[0:L, 3:4])
                        k.recip(ST8b[0:L, :], ST8b[0:L, :])
                        k.tt(YS3, YS3, ST8b[0:L, :].unsqueeze(2).to_broadcast([L, 8, 64]), ALU.mult)
                        k.tt(YS[0:L, :], YS[0:L, :], LNW[0:L, :], ALU.mult)
                        k.tt(YS[0:L, :], YS[0:L, :], LNB[0:L, :], ALU.add)
                        yield
                        for tl in range(4):
                            k.stt(RKc[:, tl, 0:L], RT[:, tl, c0:c0 + L], RRK[:, 4 * hg + tl:4 * hg + tl + 1],
                                  KT_[:, tl, c0:c0 + L], ALU.mult, ALU.mult)
                            k.mm(PS[7][0:L, 256 + 2 * tl:256 + 2 * tl + 2], RKc[:, tl, 0:L], SEL[:, :])
                        k.tt(SQ3, VTg[0:L, ci, :].rearrange("p (a b) -> p a b", b=64),
                             PS[7][0:L, 256:264].unsqueeze(2).to_broadcast([L, 8, 64]), ALU.mult)
                        k.tt(OUTB[0:L, :], YS[0:L, :], SQ[0:L, :], ALU.add)
                        for tl in range(4):
                            k.tr(p5[:, tl * 128:tl * 128 + L], OUTB[0:L, tl * 128:(tl + 1) * 128], IDB[0:L, 0:L])
                            k.tt(Y[:, 4 * hg + tl, c0:c0 + L], p5[:, tl * 128:tl * 128 + L],
                                 GF[:, 4 * hg + tl, c0:c0 + L], ALU.mult)

                    g3 = pass3(hg)
                    precompute(0, g3)
                    for _ in g3:
                        pass
                    for ci in range(nch):
                        gen = sequential(ci)
                        if ci + 1 < nch:
                            precompute(ci + 1, gen)
                        for _ in gen:
                            pass

                for hg in range(4):
                    with layer_scope():
                        passes(hg)
                    with layer_scope():
                        chunkloop(hg)
                if si == len(SEGS) - 1:
                    k.dma('sp', o_p_rw_wkv[:, :, :], RWM[:, :, :])

            with layer_scope():
                phaseA()
            ZTd = [sb(f"ZTD{i}", [128, WMAX]) for i in range(8)]
            out_proj(li, seg, Y, rw_w_o, ZTd)

        LAYERS = {0: (lru_setup, lru_layer), 1: (s5_setup, s5_layer), 2: (rwkv_setup, rwkv_layer), 3: (ret_setup, ret_layer)}
        for li in order:
            LAYERS[li][0]()

        for si, seg in enumerate(SEGS):
            seg = dict(seg)
            seg['W'] = seg['Wp'] + seg['Ws']
            W = seg['W']
            g0 = seg['p0']
            k.dma('sp', H[:, :, 0:W], hin_v[:, :, g0:g0 + W])
            for li in order:
                pre_norm(li, seg)
                k.dma('sp', hspill[:, :, 0:W], H[:, :, 0:W])
                with layer_scope():
                    LAYERS[li][1](li, seg, si)
            lo = NMETA if si == 0 else 0
            k.dma('sp', yout_v[:, :, g0 + lo - NMETA:g0 + W - NMETA], H[:, :, lo:W])
        k.finish()
    P.ninstr = k.ninstr
    return P


def _ct(v, ntile):
    v = np.asarray(v)
    lead = v.shape[:-1]
    v = v.reshape(lead + (ntile, 128))
    nd = v.ndim
    perm = (nd - 1, nd - 2) + tuple(range(nd - 2))
    return np.ascontiguousarray(v.transpose(perm))


def prep_core(inp, c):
    f = lambda a: np.ascontiguousarray(a, dtype=np.float32)
    m = {}
    xp = inp['x_prompt'][c]
    xs = inp['x_sample'][c * NS:(c + 1) * NS].reshape(WS, D)
    m['hin'] = f(np.concatenate([inp['meta_tokens'], xp, xs], axis=0).T)
    m['normpre'] = f(_ct(inp['norm_pre'], 8))
    m['normpre'] = f(m['normpre'].transpose(0, 2, 1))
    m['normpost'] = f(_ct(inp['norm_post'], 8).transpose(0, 2, 1))
    m['ident'] = np.eye(128, dtype=np.float32)
    m['lru_w_in'] = f(inp['lru_w_in'][0])
    m['lru_cw'] = f(_ct(inp['lru_conv_w'][0], 16))
    m['lru_cb'] = f(_ct(inp['lru_conv_b'][0], 16))
    m['lru_wa'] = f(inp['lru_wa'][0])
    m['lru_wx'] = f(inp['lru_wx'][0])
    m['lru_ba'] = f(inp['lru_ba'][0].T)
    m['lru_bx'] = f(inp['lru_bx'][0].T)
    m['lru_lam'] = f(_ct(inp['lru_lam'][0], 16))
    m['lru_w_out'] = f(inp['lru_w_out'][0])
    sl = slice(c * NS, (c + 1) * NS)
    m['st_lru_conv'] = f(_ct(inp['state_lru_conv'][0, sl], 16))
    m['st_lru_h'] = f(_ct(inp['state_lru_h'][0, sl], 16))
    m['s5_w_in'] = f(inp['s5_w_in'][0])
    chan = lambda a: f(np.asarray(a).reshape(64, 128).T)
    m['s5_are'] = chan(inp['s5_a_re'][0])
    m['s5_aim'] = chan(inp['s5_a_im'][0])
    m['s5_ldt'] = chan(np.repeat(inp['s5_log_dt'][0][:, None], 64, axis=1))
    bre = inp['s5_b_re'][0].reshape(16, 8, 64, 16)
    bim = inp['s5_b_im'][0].reshape(16, 8, 64, 16)
    cre = inp['s5_c_re'][0].reshape(16, 8, 16, 64)
    cim = inp['s5_c_im'][0].reshape(16, 8, 16, 64)
    BR = np.zeros((16, 128, 4, 128), np.float32); BI = np.zeros_like(BR)
    CR = np.zeros((16, 128, 4, 128), np.float32); CI = np.zeros_like(CR)
    for gl in range(8):
        s_, g2 = gl // 2, gl % 2
        BR[:, 16 * gl:16 * gl + 16, s_, g2 * 64:(g2 + 1) * 64] = bre[:, gl].transpose(0, 2, 1)
        BI[:, 16 * gl:16 * gl + 16, s_, g2 * 64:(g2 + 1) * 64] = bim[:, gl].transpose(0, 2, 1)
        CR[:, g2 * 64:(g2 + 1) * 64, s_, 16 * gl:16 * gl + 16] = cre[:, gl].transpose(0, 2, 1)
        CI[:, g2 * 64:(g2 + 1) * 64, s_, 16 * gl:16 * gl + 16] = cim[:, gl].transpose(0, 2, 1)
    m['s5_bre'], m['s5_bim'], m['s5_cre'], m['s5_cim'] = BR, BI, CR, CI
    m['s5_d'] = f(_ct(inp['s5_d'][0], 16))
    m['s5_glu_w'] = f(inp['s5_glu_w'][0])
    m['s5_glu_b'] = f(_ct(inp['s5_glu_b'][0], 16))
    m['s5_w_out'] = f(inp['s5_w_out'][0])
    m['s5_iota'] = f(np.tile(np.arange(1, 257, dtype=np.float32)[None, :], (128, 1)))
    m['st_s5_re'] = f(inp['state_s5_re'][0, sl].reshape(NS, 64, 128).transpose(2, 1, 0))
    m['st_s5_im'] = f(inp['state_s5_im'][0, sl].reshape(NS, 64, 128).transpose(2, 1, 0))
    m['rw_mu'] = f(_ct(inp['rwkv_mu'][0], 8))
    m['rw_mu'] = f(m['rw_mu'].transpose(0, 2, 1))
    for n_ in ('w_r', 'w_k', 'w_v', 'w_g', 'w_o'):
        m['rw_' + n_] = f(inp['rwkv_' + n_][0])
    m['rw_w1'] = f(inp['rwkv_w1'][0]); m['rw_w2'] = f(inp['rwkv_w2'][0])
    m['rw_a1'] = f(inp['rwkv_a1'][0]); m['rw_a2'] = f(inp['rwkv_a2'][0])
    vecs = np.stack([inp['rwkv_w0'][0], inp['rwkv_a0'][0], inp['rwkv_k_k'][0], inp['rwkv_k_a'][0], inp['rwkv_r_k'][0]])
    m['rw_vecs'] = f(_ct(vecs, 16).transpose(0, 2, 1))
    m['rw_lnw'] = f(np.tile(inp['rwkv_ln_w'][0][None, :], (128, 1)))
    m['rw_lnb'] = f(np.tile(inp['rwkv_ln_b'][0][None, :], (128, 1)))
    ii = np.arange(128)[:, None]; jj = np.arange(128)[None, :]
    same_ = (ii // TS) == (jj // TS)
    msk = np.zeros((128, 2, 3, 128), np.float32)
    msk[:, 0, 0] = ii < jj; msk[:, 0, 1] = jj < ii; msk[:, 0, 2] = ii <= jj
    msk[:, 1, 0] = (ii < jj) & same_; msk[:, 1, 1] = (jj < ii) & same_; msk[:, 1, 2] = (ii <= jj) & same_
    m['rw_masks'] = msk
    rs_ = np.ones((128, 128), np.float32); rs_[:, ::TS] = 0.0
    m['rw_reset'] = rs_
    sel = np.zeros((128, 2), np.float32); sel[:64, 0] = 1.0; sel[64:, 1] = 1.0
    m['rw_sel'] = sel
    m['rw_oh'] = f(np.arange(128)[:, None] // TS == np.arange(NS)[None, :])
    m['st_rw_shift'] = f(_ct(inp['state_rwkv_shift'][0, sl], 8))
    wkv = inp['state_rwkv_wkv'][0, sl].reshape(NS, 16, 2, 64, 64)
    m['st_rw_wkv'] = f(wkv.transpose(1, 2, 4, 0, 3).reshape(16, 128, NS, 64))
    for n_ in ('w_q', 'w_k', 'w_v', 'w_g', 'w_o'):
        m['ret_' + n_] = f(inp['ret_' + n_][0])
    half = 128
    inv = (np.float32(10000.0) ** (-(np.arange(half, dtype=np.float32) / np.float32(half)))).astype(np.float32)
    pos = np.concatenate([np.arange(PT), np.tile(16384 + np.arange(TS), NS)]).astype(np.float32)
    ang = (pos[None, :] * inv[:, None]).astype(np.float32).astype(np.float64)
    m['ret_cos'] = f(np.cos(ang))
    m['ret_sin'] = f(np.sin(ang))
    logg = np.log1p(-np.exp2(-5.0 - np.arange(4, dtype=np.float64)))
    mm_ = np.arange(128)[:, None]; ll = np.arange(128)[None, :]
    mk = np.zeros((128, 2, 4, 128)); grow = np.zeros((128, 2, 4, 128)); gcol = np.zeros((128, 3, 4))
    for h in range(4):
        dlt = ll - mm_
        mk[:, 0, h, :] = np.where(dlt >= 0, np.exp(dlt * logg[h]), 0.0)
        same = (mm_ // TS) == (ll // TS)
        mk[:, 1, h, :] = np.where((dlt >= 0) & same, np.exp(dlt * logg[h]), 0.0)
        grow[:, 0, h, :] = np.exp((np.arange(128) + 1.0) * logg[h])[None, :]
        grow[:, 1, h, :] = np.exp(((np.arange(128) % TS) + 1.0) * logg[h])[None, :]
        gcol[:, 0, h] = np.exp((127.0 - np.arange(128)) * logg[h])
        gcol[:, 1, h] = np.exp((15.0 - np.arange(128)) * logg[h])
        gcol[:, 2, h] = np.exp((TS - 1.0 - (np.arange(128) % TS)) * logg[h])
    m['ret_mk'], m['ret_grow'], m['ret_gcol'] = f(mk), f(grow), f(gcol)
    onehot = (np.arange(128)[:, None] // TS == np.arange(NS)[None, :]).astype(np.float64)
    m['ret_sc'] = f(onehot[:, :, None] * gcol[:, None, 2, :])
    m['ret_cmask'] = f(np.tile(onehot.T[None, :, :], (128, 1, 1)))
    m['st_ret'] = f(inp['state_ret'][0, sl])
    return m


def _unct(a):
    a = np.asarray(a)
    nd = a.ndim
    perm = tuple(range(2, nd)) + (1, 0)
    a = a.transpose(perm)
    return a.reshape(a.shape[:-2] + (a.shape[-2] * 128,))


def gather(results):
    n = len(results)
    y_prompt = np.zeros((n, SEQ, D), np.float32)
    y_sample = np.zeros((n * NS, TS, D), np.float32)
    p_lru_conv = np.zeros((1, n, 3, E), np.float32)
    p_lru_h = np.zeros((1, n, E), np.float32)
    s_lru_conv = np.zeros((1, n * NS, 3, E), np.float32)
    s_lru_h = np.zeros((1, n * NS, E), np.float32)
    p_s5_re = np.zeros((1, n, 128, 64), np.float32)
    p_s5_im = np.zeros((1, n, 128, 64), np.float32)
    s_s5_re = np.zeros((1, n * NS, 128, 64), np.float32)
    s_s5_im = np.zeros((1, n * NS, 128, 64), np.float32)
    p_rwkv_shift = np.zeros((1, n, D), np.float32)
    s_rwkv_shift = np.zeros((1, n * NS, D), np.float32)
    p_rwkv_wkv = np.zeros((1, n, 32, 64, 64), np.float32)
    s_rwkv_wkv = np.zeros((1, n * NS, 32, 64, 64), np.float32)
    p_ret = np.zeros((1, n, 4, 256, 512), np.float32)
    s_ret = np.zeros((1, n * NS, 4, 256, 512), np.float32)
    for c, r in enumerate(results):
        yo = r['yout']
        y_prompt[c] = yo[:, :SEQ].T
        y_sample[c * NS:(c + 1) * NS] = yo[:, SEQ:].T.reshape(NS, TS, D)
        sl = slice(c * NS, (c + 1) * NS)
        p_lru_conv[0, c] = _unct(r['o_p_lru_conv'])
        p_lru_h[0, c] = _unct(r['o_p_lru_h'])
        s_lru_conv[0, sl] = _unct(r['o_s_lru_conv'])
        s_lru_h[0, sl] = _unct(r['o_s_lru_h'])
        if 'o_p_rw_shift' in r:
            p_rwkv_shift[0, c] = _unct(r['o_p_rw_shift'])
            s_rwkv_shift[0, sl] = _unct(r['o_s_rw_shift'])
            p_rwkv_wkv[0, c] = r['o_p_rw_wkv'].reshape(2, 64, 16, 64).transpose(2, 0, 3, 1).reshape(32, 64, 64)
            s_rwkv_wkv[0, sl] = r['o_s_rw_wkv'].reshape(16, 2, 64, NS, 64).transpose(3, 0, 1, 4, 2).reshape(NS, 32, 64, 64)
        if 'o_p_ret' in r:
            p_ret[0, c] = r['o_p_ret']
            s_ret[0, sl] = r['o_s_ret']
        if 'o_p_s5_re' in r:
            p_s5_re[0, c] = r['o_p_s5_re'].T.reshape(128, 64)
            p_s5_im[0, c] = r['o_p_s5_im'].T.reshape(128, 64)
            s_s5_re[0, sl] = r['o_s_s5_re'].transpose(2, 1, 0).reshape(NS, 128, 64)
            s_s5_im[0, sl] = r['o_s_s5_im'].transpose(2, 1, 0).reshape(NS, 128, 64)
    return dict(y_prompt=y_prompt, y_sample=y_sample, p_lru_conv=p_lru_conv, p_lru_h=p_lru_h,
                s_lru_conv=s_lru_conv, s_lru_h=s_lru_h,
                p_s5_re=p_s5_re, p_s5_im=p_s5_im, s_s5_re=s_s5_re, s_s5_im=s_s5_im,
                p_rwkv_shift=p_rwkv_shift, p_rwkv_wkv=p_rwkv_wkv,
                s_rwkv_shift=s_rwkv_shift, s_rwkv_wkv=s_rwkv_wkv, p_ret=p_ret, s_ret=s_ret)


OUT_ORDER = ("y_prompt", "y_sample", "p_lru_conv", "p_lru_h", "p_s5_re", "p_s5_im", "p_rwkv_shift",
             "p_rwkv_wkv", "p_ret", "s_lru_conv", "s_lru_h", "s_s5_re", "s_s5_im", "s_rwkv_shift",
             "s_rwkv_wkv", "s_ret")


def kernel(**inputs):
    inputs = {k_: np.asarray(v) for k_, v in inputs.items()}
    P = build((0, 1, 2, 3))
    in_maps = []
    shared = None
    for c in range(NCORES):
        m = prep_core(inputs, c)
        if shared is None:
            shared = m
        else:
            for k_ in list(m.keys()):
                if not (k_.startswith("st_") or k_ == "hin"):
                    m[k_] = shared[k_]
        in_maps.append(m)
    res = run_bass_kernel_spmd(P.nc, in_maps, core_ids=list(range(NCORES)))
    g = gather(res.results)
    return tuple(g[n] for n in OUT_ORDER)
```
